# Optimizing a Trainium2 kernel written in Bass

```python
import jax, jax.numpy as jnp
from jax import lax
import numpy as np

D_MODEL = 1024
BATCH = 32
SEQ = 2048
DEPTH = 1

SB_HEADS = 8
SB_HEAD_DIM = 64
SB_WIDTH = SB_HEADS * SB_HEAD_DIM
SB_BLOCK = 128

GLA_HEADS = 4
GLA_DK = 64
GLA_DV = 128
GLA_KW = GLA_HEADS * GLA_DK
GLA_VW = GLA_HEADS * GLA_DV
GLA_RANK = 16
GLA_TAU = 16.0
GLA_CHUNK = 64

D_FF = 2816
CONV_WIDTH = 3

EPS = 1e-6

IN_SIZES = (SB_WIDTH, SB_WIDTH, SB_WIDTH,
            GLA_KW, GLA_KW, GLA_VW, GLA_VW,
            GLA_RANK,
            D_MODEL, D_MODEL)
IN_TOTAL = sum(IN_SIZES)

kernel_name = "hybrid_stickbreak_gla_convffn"


def rms_norm(x, g):
    xf = x.astype(jnp.float32)
    xf = xf * lax.rsqrt(jnp.mean(xf * xf, axis=-1, keepdims=True) + EPS)
    return xf.astype(x.dtype) * g


def to_heads(t, n_heads):
    b, s, _ = t.shape
    return t.reshape(b, s, n_heads, -1).transpose(0, 2, 1, 3)


def from_heads(t):
    b, h, s, d = t.shape
    return t.transpose(0, 2, 1, 3).reshape(b, s, h * d)


def stick_breaking_attention(q, k, v):
    s_len, dh = q.shape[2], q.shape[3]
    scale = dh ** -0.5
    outs = []
    for i0 in range(0, s_len, SB_BLOCK):
        end = i0 + SB_BLOCK
        qb = q[:, :, i0:end]
        kb = k[:, :, :end]
        vb = v[:, :, :end]
        z = jnp.einsum('bhqd,bhkd->bhqk', qb, kb).astype(jnp.float32) * scale
        t_idx = i0 + jnp.arange(SB_BLOCK)[:, None]
        s_idx = jnp.arange(end)[None, :]
        strict = s_idx < t_idx
        log_keep = jnp.where(strict, jax.nn.log_sigmoid(-z), 0.0)
        tail = lax.cumsum(log_keep, axis=3, reverse=True) - log_keep
        log_w = jax.nn.log_sigmoid(z) + tail
        w = jnp.where(strict, jnp.exp(log_w), 0.0)
        outs.append(jnp.einsum('bhqk,bhkd->bhqd', w.astype(v.dtype), vb))
    return jnp.concatenate(outs, axis=2)


def gla_chunked(q, k, v, log_a):
    b, h, s_len, dk = q.shape
    dv = v.shape[-1]
    c = GLA_CHUNK
    n = s_len // c
    qf = q.astype(jnp.float32).reshape(b, h, n, c, dk)
    kf = k.astype(jnp.float32).reshape(b, h, n, c, dk)
    vf = v.astype(jnp.float32).reshape(b, h, n, c, dv)
    cum = jnp.cumsum(log_a.reshape(b, h, n, c, dk), axis=3)
    cum_last = cum[:, :, :, -1:]
    q_dec = qf * jnp.exp(cum)
    k_inv = kf * jnp.exp(-cum)
    k_to_end = kf * jnp.exp(cum_last - cum)
    att = jnp.einsum('bhnck,bhnsk->bhncs', q_dec, k_inv)
    causal = jnp.tril(jnp.ones((c, c), dtype=bool))
    att = jnp.where(causal, att, 0.0)
    o_intra = jnp.einsum('bhncs,bhnsv->bhncv', att, vf)
    d_state = jnp.einsum('bhnck,bhncv->bhnkv', k_to_end, vf)
    chunk_decay = jnp.exp(cum_last[:, :, :, 0])

    def step(state, inp):
        dec, ds = inp
        return dec[..., None] * state + ds, state

    init = jnp.zeros((b, h, dk, dv), jnp.float32)
    _, states_before = lax.scan(step, init,
                                (jnp.moveaxis(chunk_decay, 2, 0), jnp.moveaxis(d_state, 2, 0)))
    states_before = jnp.moveaxis(states_before, 0, 2)
    o_inter = jnp.einsum('bhnck,bhnkv->bhncv', q_dec, states_before)
    return (o_intra + o_inter).reshape(b, h, s_len, dv)


def mixing_sublayer(xn, w_in, b_gate, w_alpha_up, b_alpha, gla_norm_g,
                    w_branch_sb, w_branch_gla, w_out):
    proj = xn @ w_in
    offsets = np.cumsum(IN_SIZES)[:-1].tolist()
    (sb_q, sb_k, sb_v, g_q, g_k, g_v, g_r, g_a,
     gate_sb, gate_gla) = jnp.split(proj, offsets, axis=-1)

    o_sb = stick_breaking_attention(to_heads(sb_q, SB_HEADS),
                                    to_heads(sb_k, SB_HEADS),
                                    to_heads(sb_v, SB_HEADS))
    o_sb = from_heads(o_sb)

    a_pre = (g_a @ w_alpha_up + b_alpha).astype(jnp.float32)
    log_a = jax.nn.log_sigmoid(a_pre) / GLA_TAU
    o_gla = gla_chunked(to_heads(g_q, GLA_HEADS) * (GLA_DK ** -0.5),
                        to_heads(g_k, GLA_HEADS),
                        to_heads(g_v, GLA_HEADS),
                        to_heads(log_a, GLA_HEADS))
    o_gla = o_gla * lax.rsqrt(jnp.mean(o_gla * o_gla, axis=-1, keepdims=True) + EPS)
    o_gla = from_heads(o_gla).astype(xn.dtype) * gla_norm_g * jax.nn.silu(g_r)

    y = (jax.nn.sigmoid(gate_sb + b_gate[0]) * (o_sb @ w_branch_sb)
         + jax.nn.sigmoid(gate_gla + b_gate[1]) * (o_gla @ w_branch_gla))
    return y @ w_out


def causal_depthwise_conv(u, w, bias):
    k_w = w.shape[0]
    s_len = u.shape[1]
    up = jnp.pad(u, ((0, 0), (k_w - 1, 0), (0, 0)))
    out = bias
    for i in range(k_w):
        out = out + up[:, i:i + s_len] * w[i]
    return out


def conv_ffn(hn, w_ffn_in, conv_w, conv_b, w_ffn_out):
    u = hn @ w_ffn_in
    a, g = jnp.split(u, 2, axis=-1)
    a = causal_depthwise_conv(a, conv_w, conv_b)
    return (jax.nn.gelu(a) * g) @ w_ffn_out


def setup_inputs(seed: int = 0) -> dict:
    key = jax.random.key(seed)
    ks = jax.random.split(key, 16)
    f32 = jnp.float32

    def nrm(k, shape, scale):
        return jax.random.normal(k, shape, f32) * scale

    return {
        "x": jax.random.normal(ks[0], (BATCH, SEQ, D_MODEL), f32),
        "norm_mix_g": 1.0 + nrm(ks[1], (DEPTH, D_MODEL), 0.02),
        "w_in": nrm(ks[2], (DEPTH, D_MODEL, IN_TOTAL), D_MODEL ** -0.5),
        "b_gate": nrm(ks[3], (DEPTH, 2, D_MODEL), 0.02),
        "w_alpha_up": nrm(ks[4], (DEPTH, GLA_RANK, GLA_KW), GLA_RANK ** -0.5),
        "b_alpha": nrm(ks[5], (DEPTH, GLA_KW), 0.1),
        "gla_norm_g": 1.0 + nrm(ks[6], (DEPTH, GLA_VW), 0.02),
        "w_branch_sb": nrm(ks[7], (DEPTH, SB_WIDTH, D_MODEL), SB_WIDTH ** -0.5),
        "w_branch_gla": nrm(ks[8], (DEPTH, GLA_VW, D_MODEL), GLA_VW ** -0.5),
        "w_out": nrm(ks[9], (DEPTH, D_MODEL, D_MODEL), D_MODEL ** -0.5),
        "norm_ffn_g": 1.0 + nrm(ks[10], (DEPTH, D_MODEL), 0.02),
        "w_ffn_in": nrm(ks[11], (DEPTH, D_MODEL, 2 * D_FF), D_MODEL ** -0.5),
        "conv_w": nrm(ks[12], (DEPTH, CONV_WIDTH, D_FF), CONV_WIDTH ** -0.5),
        "conv_b": nrm(ks[13], (DEPTH, D_FF), 0.01),
        "w_ffn_out": nrm(ks[14], (DEPTH, D_FF, D_MODEL), D_FF ** -0.5),
        "norm_final_g": 1.0 + nrm(ks[15], (D_MODEL,), 0.02),
    }


def reference(x, norm_mix_g, w_in, b_gate, w_alpha_up, b_alpha, gla_norm_g,
              w_branch_sb, w_branch_gla, w_out, norm_ffn_g, w_ffn_in, conv_w,
              conv_b, w_ffn_out, norm_final_g):
    h = x
    for i in range(DEPTH):
        xn = rms_norm(h, norm_mix_g[i])
        h = h + mixing_sublayer(xn, w_in[i], b_gate[i], w_alpha_up[i], b_alpha[i],
                                gla_norm_g[i], w_branch_sb[i], w_branch_gla[i], w_out[i])
        hn = rms_norm(h, norm_ffn_g[i])
        h = h + conv_ffn(hn, w_ffn_in[i], conv_w[i], conv_b[i], w_ffn_out[i])
    return rms_norm(h, norm_final_g)
```

```python
import numpy as np
from contextlib import ExitStack
import concourse.bass as bass
import concourse.mybir as mybir
from concourse.bass_utils import run_bass_kernel_spmd

F32 = mybir.dt.float32
BF16 = mybir.dt.bfloat16
AF = mybir.ActivationFunctionType
ALU = mybir.AluOpType

D = 1024
KC = 8
IN_TOTAL = 5136
OFF_SBQ, OFF_SBK, OFF_SBV = 0, 512, 1024
OFF_G = 1536
OFF_GSB, OFF_GGLA = 3088, 4112
DFF = 2816
NF = 22
EPS = 1e-6
N_CORES = 8


class Res:
    __slots__ = ("name", "last_write", "reads", "dma_sem", "dma_cnt", "parent", "children")

    def __init__(self, name="", parent=None):
        self.name = name
        self.last_write = None
        self.reads = {}
        self.dma_sem = None
        self.dma_cnt = 0
        self.parent = parent
        self.children = []
        if parent is not None:
            parent.children.append(self)

    def related(self):
        out = [self]
        if self.parent is not None:
            out.append(self.parent)
        out.extend(self.children)
        return out


class Sched:
    def __init__(self, nc, ctx):
        self.nc = nc
        self.ctx = ctx
        self.engs = {"pe": nc.tensor, "act": nc.scalar, "dve": nc.vector,
                     "pool": nc.gpsimd, "sp": nc.sync}
        self.sems = {}
        self.cnt = {}
        self.semobj = {}
        for k in ("pe", "act", "dve", "pool"):
            self.sems[k] = ctx.enter_context(nc.semaphore("s_" + k))
            self.cnt[k] = 0
            self.semobj[k] = self.sems[k]
        self.waited = {}
        self.pending = {k: False for k in self.cnt}
        self.dma_total = {}
        self.n_dma_sems = 0
        self.n_wait = 0
        self.n_ins = 0

    def _dma_sem(self, r):
        if r.dma_sem is None:
            key = "d%d" % self.n_dma_sems
            self.n_dma_sems += 1
            self.semobj[key] = self.ctx.enter_context(self.nc.semaphore(key))
            self.dma_total[key] = 0
            r.dma_sem = key
        return r.dma_sem

    def share_dma_sem(self, r_from, r_to):
        r_to.dma_sem = self._dma_sem(r_from)

    def _wait(self, eng, deps):
        e = self.engs[eng]
        for (sk, val) in deps:
            if self.waited.get((eng, sk), 0) >= val:
                continue
            e.wait_ge(self.semobj[sk], val)
            self.waited[(eng, sk)] = val
            self.n_wait += 1

    def _deps(self, eng, reads, writes):
        deps = {}

        def add(ev, kind):
            if ev is None:
                return
            sk, val = ev
            if sk == eng and (eng == "pe" or kind == "war"):
                return
            if deps.get(sk, 0) < val:
                deps[sk] = val
        for r0 in reads:
            for r in r0.related():
                add(r.last_write, "raw")
        for w0 in writes:
            for w in w0.related():
                add(w.last_write, "waw")
                for sk, val in w.reads.items():
                    add((sk, val), "war")
        return list(deps.items())

    def begin_region(self):
        self.region = []

    def end_region(self):
        ops, self.region = self.region, None
        n = len(ops)
        last_w, readers = {}, {}
        preds = [set() for _ in range(n)]
        for i, (eng, fn, reads, writes, signal, cost) in enumerate(ops):
            for r0 in reads:
                for r in r0.related():
                    if id(r) in last_w:
                        preds[i].add(last_w[id(r)])
            for w0 in writes:
                for w in w0.related():
                    if id(w) in last_w:
                        preds[i].add(last_w[id(w)])
                    preds[i].update(readers.get(id(w), ()))
            for r0 in reads:
                readers.setdefault(id(r0), []).append(i)
            for w0 in writes:
                last_w[id(w0)] = i
                readers[id(w0)] = []
            preds[i].discard(i)
        succs = [[] for _ in range(n)]
        npred = [len(p) for p in preds]
        for i, p in enumerate(preds):
            for j in p:
                succs[j].append(i)
        eng_free = {}
        fin = [0.0] * n
        ready = [i for i in range(n) if npred[i] == 0]
        LAT = 0.15
        while ready:
            best, best_t = None, None
            for i in ready:
                eng = ops[i][0]
                t = eng_free.get(eng, 0.0)
                for j in preds[i]:
                    tj = fin[j] + (LAT if ops[j][0] != eng else 0.0)
                    if tj > t:
                        t = tj
                if best is None or t < best_t - 1e-9 or (abs(t - best_t) <= 1e-9 and i < best):
                    best, best_t = i, t
            ready.remove(best)
            eng, fn, reads, writes, signal, cost = ops[best]
            fin[best] = best_t + cost
            if isinstance(eng, tuple):
                fn(reads, writes)
            else:
                eng_free[eng] = fin[best]
                self.op(eng, fn, reads=reads, writes=writes, signal=signal)
            for k in succs[best]:
                npred[k] -= 1
                if npred[k] == 0:
                    ready.append(k)

    def op(self, eng, fn, reads=(), writes=(), signal=True, cost=None):
        if getattr(self, "region", None) is not None:
            if cost is None:
                cost = {"pe": 0.25, "act": 0.6, "dve": 0.5, "pool": 1.0}[eng]
            self.region.append((eng, fn, tuple(reads), tuple(writes), signal, cost))
            return None
        self._wait(eng, self._deps(eng, reads, writes))
        ins = fn()
        self.n_ins += 1
        if signal:
            self.cnt[eng] += 1
            ins.then_inc(self.sems[eng], 1)
            self.pending[eng] = False
            val = self.cnt[eng]
        else:
            self.pending[eng] = True
            val = self.cnt[eng] + 1
        for r in reads:
            if r.reads.get(eng, 0) < val:
                r.reads[eng] = val
        for w in writes:
            w.last_write = (eng, val)
            w.reads = {}
        return ins

    def dma(self, q, out, in_, reads=(), writes=(), **kw):
        if getattr(self, "region", None) is not None:
            def emit(reads_, writes_, q=q, out=out, in_=in_, kw=kw):
                reg, self.region = self.region, None
                self.dma(q, out, in_, reads=reads_, writes=writes_, **kw)
                self.region = reg
            self.region.append((("dma", q, len(self.region)), emit, tuple(reads), tuple(writes), True, 3.0))
            return None
        anchor = writes[0] if writes else reads[0]
        sk = self._dma_sem(anchor)
        self._wait(q, [d for d in self._deps("dma", reads, writes) if d[0] != sk])
        ins = self.engs[q].dma_start(out=out, in_=in_, **kw)
        ins.then_inc(self.semobj[sk], 16)
        self.dma_total[sk] += 16
        val = self.dma_total[sk]
        self.n_ins += 1
        for r in reads:
            if r.reads.get(sk, 0) < val:
                r.reads[sk] = val
        for w in writes:
            w.last_write = (sk, val)
            w.reads = {}
        return ins

    def barrier(self):
        for k, p in self.pending.items():
            assert not p, "unsignaled instruction pending on " + k
        evs = [(k, v) for k, v in self.cnt.items() if v > 0]
        evs += [(k, v) for k, v in self.dma_total.items() if v > 0]
        for eng in ("pe", "act", "dve", "pool", "sp"):
            self._wait(eng, evs)


def build_nc(S, NB):
    NT = S // 128
    NG = S // 512
    assert S % 512 == 0
    nc = bass.Bass("TRN2", target_bir_lowering=False)

    def din(name, shape):
        return nc.dram_tensor(name, shape, F32, kind="ExternalInput").ap()

    x = din("x", [NB, S, D])
    norm_mix_g = din("norm_mix_g", [D])
    w_in = din("w_in", [D, IN_TOTAL])
    b_gate = din("b_gate", [2, D])
    w_alpha_up = din("w_alpha_up", [16, 256])
    b_alpha = din("b_alpha", [256])
    gla_norm_g = din("gla_norm_g", [512])
    w_branch_sb = din("w_branch_sb", [512, D])
    w_branch_gla = din("w_branch_gla", [512, D])
    w_out = din("w_out", [D, D])
    norm_ffn_g = din("norm_ffn_g", [D])
    w_ffn_in = din("w_ffn_in", [D, 2 * DFF])
    conv_w = din("conv_w", [3, DFF])
    conv_b = din("conv_b", [DFF])
    w_ffn_out = din("w_ffn_out", [DFF, D])
    norm_final_g = din("norm_final_g", [D])
    out = nc.dram_tensor("out", [NB, S, D], F32, kind="ExternalOutput").ap()
    hscr = nc.dram_tensor("hscr", [S, D], F32, kind="Internal").ap()

    w_in_v = w_in.rearrange("(kc p) n -> p kc n", p=128)
    wbsb_v = w_branch_sb.rearrange("(kc p) n -> p kc n", p=128)
    wbgl_v = w_branch_gla.rearrange("(kc p) n -> p kc n", p=128)
    w_out_v = w_out.rearrange("(kc p) n -> p kc n", p=128)
    wfi_v = w_ffn_in.rearrange("(kc p) n -> p kc n", p=128)
    wfo_v = w_ffn_out.rearrange("(f p) n -> p f n", p=128)

    with ExitStack() as ctx:
        sch = Sched(nc, ctx)
        op = sch.op

        uid = [0]

        def sbt(c, name, shape, dt):
            uid[0] += 1
            return c.enter_context(nc.sbuf_tensor("%s_%d" % (name, uid[0]), shape, dt))

        PA = ctx.enter_context(nc.psum_tensor("PA", [128, 1024], F32))
        PB = ctx.enter_context(nc.psum_tensor("PB", [128, 1024], F32))
        PCD = ctx.enter_context(nc.psum_tensor("PCD", [128, 1024], F32))
        PC = PCD[:, 0:512]
        PD = PCD[:, 512:1024]
        PE_ = ctx.enter_context(nc.psum_tensor("PE", [128, 512], F32))
        PT = ctx.enter_context(nc.psum_tensor("PT", [128, 8, 128], BF16))
        rPA0, rPA1, rPB0, rPB1 = Res("PA0"), Res("PA1"), Res("PB0"), Res("PB1")
        rPC, rPD, rPE, rPT = Res("PC"), Res("PD"), Res("PE"), Res("PT")
        banks5 = [(PA[:, 0:512], rPA0), (PA[:, 512:1024], rPA1), (PB[:, 0:512], rPB0),
                  (PB[:, 512:1024], rPB1), (PC[:, :], rPC), (PD[:, :], rPD), (PE_[:, :], rPE)]

        cst = ctx
        idf = sbt(cst, "idf", [128, 128], F32); r_idf = Res()
        ident = sbt(cst, "ident", [128, 128], BF16); r_ident = Res()
        trineg = sbt(cst, "trineg", [128, 128], BF16); r_trineg = Res()
        onesneg = sbt(cst, "onesneg", [128, 128], BF16); r_onesneg = Res()
        mstrict = sbt(cst, "mstrict", [128, 128], BF16); r_mstrict = Res()
        mnegbig = sbt(cst, "mnegbig", [128, 128], BF16); r_mnegbig = Res()
        zeros = sbt(cst, "zeros", [128, 512], BF16); r_zeros = Res()
        ust = sbt(cst, "ust", [128, 128], F32); r_ust = Res()
        tincl = sbt(cst, "tincl", [128, 128], F32); r_tincl = Res()
        ones32 = sbt(cst, "ones32", [128, 128], F32); r_ones32 = Res()
        tmpc = sbt(cst, "tmpc", [128, 128], F32); r_tmpc = Res()
        vst = sbt(cst, "vst", [128, 128], F32); r_vst = Res()
        vecs = sbt(cst, "vecs", [128, 128], F32); r_vecs = Res()
        gfin_bc = sbt(cst, "gfin_bc", [128, D], F32); r_gfin = Res()
        ggla_bc = sbt(cst, "ggla_bc", [128, 512], F32); r_ggla = Res()
        balpha = sbt(cst, "balpha", [1, 256], F32); r_balpha = Res()
        walpha = sbt(cst, "walpha", [16, 256], F32); r_walpha = Res()

        def gp(fn, reads=(), writes=()):
            return op("pool", fn, reads=reads, writes=writes)

        def aff(t, cmp, fill, cm=1, pat=-1, base=0):
            return lambda: nc.gpsimd.affine_select(out=t[:], in_=t[:], pattern=[[pat, 128]], compare_op=cmp,
                                                   fill=fill, base=base, channel_multiplier=cm)
        gp(lambda: nc.gpsimd.memset(idf[:], 1.0), writes=[r_idf])
        gp(aff(idf, ALU.is_equal, 0.0), reads=[r_idf], writes=[r_idf])
        op("dve", lambda: nc.vector.tensor_copy(out=ident[:], in_=idf[:]), reads=[r_idf], writes=[r_ident])
        gp(lambda: nc.gpsimd.memset(tmpc[:], -1.0), writes=[r_tmpc])
        gp(aff(tmpc, ALU.is_ge, 0.0), reads=[r_tmpc], writes=[r_tmpc])
        op("dve", lambda: nc.vector.tensor_copy(out=trineg[:], in_=tmpc[:]), reads=[r_tmpc], writes=[r_trineg])
        gp(lambda: nc.gpsimd.memset(tmpc[:], 1.0), reads=[r_tmpc], writes=[r_tmpc])
        gp(aff(tmpc, ALU.is_gt, 0.0, cm=-1, pat=1), reads=[r_tmpc], writes=[r_tmpc])
        op("dve", lambda: nc.vector.tensor_copy(out=mstrict[:], in_=tmpc[:]), reads=[r_tmpc], writes=[r_mstrict])
        gp(lambda: nc.gpsimd.memset(tmpc[:], 0.0), reads=[r_tmpc], writes=[r_tmpc])
        gp(aff(tmpc, ALU.is_gt, -30000.0, cm=-1, pat=1), reads=[r_tmpc], writes=[r_tmpc])
        op("dve", lambda: nc.vector.tensor_copy(out=mnegbig[:], in_=tmpc[:]), reads=[r_tmpc], writes=[r_mnegbig])
        gp(lambda: nc.gpsimd.memset(tmpc[:], -1.0), reads=[r_tmpc], writes=[r_tmpc])
        op("dve", lambda: nc.vector.tensor_copy(out=onesneg[:], in_=tmpc[:]), reads=[r_tmpc], writes=[r_onesneg])
        gp(lambda: nc.gpsimd.memset(zeros[:], 0.0), writes=[r_zeros])
        gp(lambda: nc.gpsimd.memset(ust[:], 1.0), writes=[r_ust])
        gp(aff(ust, ALU.is_gt, 0.0), reads=[r_ust], writes=[r_ust])
        gp(lambda: nc.gpsimd.memset(tincl[:], 1.0), writes=[r_tincl])
        gp(aff(tincl, ALU.is_ge, 0.0, cm=-1, pat=1), reads=[r_tincl], writes=[r_tincl])
        gp(lambda: nc.gpsimd.memset(ones32[:], 1.0), writes=[r_ones32])
        gp(lambda: nc.gpsimd.memset(vst[:], 0.0), writes=[r_vst])
        sch.dma("sp", vst[0:8, :], norm_mix_g.rearrange("(k p) -> k p", p=128), reads=[r_vst], writes=[r_vst])
        sch.dma("sp", vst[8:16, :], norm_ffn_g.rearrange("(k p) -> k p", p=128), writes=[r_vst])
        sch.dma("sp", vst[16:32, :], b_gate.rearrange("j (k p) -> (j k) p", p=128), writes=[r_vst])
        sch.dma("sp", vst[32:98, :], conv_w.rearrange("i (f p) -> (i f) p", p=128), writes=[r_vst])
        sch.dma("sp", vst[98:120, :], conv_b.rearrange("(f p) -> f p", p=128), writes=[r_vst])
        op("pe", lambda: nc.tensor.matmul(PC[:, 0:128], lhsT=vst[:, :], rhs=idf[:, :], start=True, stop=True),
           reads=[r_vst, r_idf], writes=[rPC])
        op("dve", lambda: nc.vector.tensor_copy(out=vecs[:], in_=PC[:, 0:128]), reads=[rPC], writes=[r_vecs])
        gmix = vecs[:, 0:8]
        gffn = vecs[:, 8:16]

        def bgate(j, n):
            return vecs[:, 16 + j * 8 + n:16 + j * 8 + n + 1]

        def convw(i, f):
            return vecs[:, 32 + i * NF + f:32 + i * NF + f + 1]

        def convb(f):
            return vecs[:, 98 + f:99 + f]
        sch.dma("sp", gfin_bc[:], norm_final_g.partition_broadcast(128), writes=[r_gfin])
        sch.dma("sp", ggla_bc[:], gla_norm_g.partition_broadcast(128), writes=[r_ggla])
        sch.dma("sp", balpha[:], b_alpha.rearrange("(o n) -> o n", o=1), writes=[r_balpha])
        sch.dma("sp", walpha[:], w_alpha_up, writes=[r_walpha])
        ones_bf = sbt(cst, "ones_bf", [128, 128], BF16); r_ones_bf = Res()
        ust_bf = sbt(cst, "ust_bf", [128, 128], BF16); r_ust_bf = Res()
        tincl_bf = sbt(cst, "tincl_bf", [128, 128], BF16); r_tincl_bf = Res()
        wa_hi = sbt(cst, "wa_hi", [16, 256], BF16); wa_lo = sbt(cst, "wa_lo", [16, 256], BF16); r_wahl = Res()
        ba_hi = sbt(cst, "ba_hi", [1, 256], BF16); ba_lo = sbt(cst, "ba_lo", [1, 256], BF16); r_bahl = Res()
        op("dve", lambda: nc.vector.tensor_copy(out=ones_bf[:], in_=ones32[:]), reads=[r_ones32], writes=[r_ones_bf])
        op("dve", lambda: nc.vector.tensor_copy(out=ust_bf[:], in_=ust[:]), reads=[r_ust], writes=[r_ust_bf])
        op("dve", lambda: nc.vector.tensor_copy(out=tincl_bf[:], in_=tincl[:]), reads=[r_tincl], writes=[r_tincl_bf])
        op("dve", lambda: nc.vector.tensor_copy(out=wa_hi[:], in_=walpha[:]), reads=[r_walpha], writes=[r_wahl])
        op("dve", lambda: nc.vector.tensor_tensor(out=wa_lo[:], in0=walpha[:], in1=wa_hi[:], op=ALU.subtract),
           reads=[r_walpha, r_wahl], writes=[r_wahl])
        op("dve", lambda: nc.vector.tensor_copy(out=ba_hi[:], in_=balpha[:]), reads=[r_balpha], writes=[r_bahl])
        op("dve", lambda: nc.vector.tensor_tensor(out=ba_lo[:], in0=balpha[:], in1=ba_hi[:], op=ALU.subtract),
           reads=[r_balpha, r_bahl], writes=[r_bahl])

        xnT = sbt(ctx, "xnT", [128, KC, S], BF16); r_xnT = Res("xnT")
        xin = [sbt(ctx, "xin%d" % i, [128, D], F32) for i in range(3)]
        r_xin = [Res("xin%d" % i) for i in range(3)]
        xnb = [sbt(ctx, "xnb%d" % i, [128, D], BF16) for i in range(2)]
        r_xnb = [Res() for _ in range(2)]
        junk = sbt(ctx, "junk", [128, D], BF16); r_junk = Res()
        stt = [sbt(ctx, "stt%d" % i, [128, 8], F32) for i in range(2)]
        r_stt = [Res() for _ in range(2)]
        wgs0 = sbt(ctx, "wgs_p0", [128, KC, 128], BF16); wgg0 = sbt(ctx, "wgg_p0", [128, KC, 128], BF16)
        wsb0 = sbt(ctx, "wsb_p0", [128, 4, 128], BF16); wgl0 = sbt(ctx, "wgl_p0", [128, 4, 128], BF16)
        r_wm0 = Res("wm0")
        wa0 = sbt(ctx, "wa_p0", [128, KC, 128], BF16); wgt0 = sbt(ctx, "wgt_p0", [128, KC, 128], BF16)
        r_wf0 = Res("wf0")

        wgA = sbt(ctx, "wgA_p", [128, KC, 512], BF16); r_wgA = Res("wgA")

        def prefetch_gla_w0():
            sch.dma("pool", wgA[:], w_in_v[:, :, OFF_G:OFF_G + 512], writes=[r_wgA])

        def prefetch_merge_w0():
            sch.dma("pool", wgs0[:], w_in_v[:, :, OFF_GSB:OFF_GSB + 128], writes=[r_wm0])
            sch.dma("pool", wgg0[:], w_in_v[:, :, OFF_GGLA:OFF_GGLA + 128], writes=[r_wm0])
            sch.dma("pool", wsb0[:], wbsb_v[:, :, 0:128], writes=[r_wm0])
            sch.dma("pool", wgl0[:], wbgl_v[:, :, 0:128], writes=[r_wm0])

        def prefetch_ffn_w0():
            sch.dma("pool", wa0[:], wfi_v[:, :, 0:128], writes=[r_wf0])
            sch.dma("pool", wgt0[:], wfi_v[:, :, DFF:DFF + 128], writes=[r_wf0])
        r_hscr = [Res("hscr%d" % t) for t in range(NT)]
        r_out = Res("out")

        def rms_stats(src_ap, r_src, st, r_st, n, width):
            op("act", lambda: nc.scalar.activation(out=junk[:, 0:width], in_=src_ap, func=AF.Square,
                                                   accum_out=st[:, 0:1]),
               reads=[r_src], writes=[r_junk, r_st])
            op("act", lambda: nc.scalar.activation(out=st[:, 1:2], in_=st[:, 0:1], func=AF.Ln,
                                                   scale=1.0 / n, bias=EPS), reads=[r_st], writes=[r_st])
            op("act", lambda: nc.scalar.activation(out=st[:, 2:3], in_=st[:, 1:2], func=AF.Exp, scale=-0.5),
               reads=[r_st], writes=[r_st])

        def norm_pre(src, r_src, slot):
            st, r_st = stt[slot], r_stt[slot]
            rms_stats(src[:], r_src, st, r_st, D, D)
            nb, r_nb = xnb[slot], r_xnb[slot]
            op("dve", lambda: nc.vector.tensor_scalar(out=nb[:], in0=src[:], scalar1=st[:, 2:3], scalar2=None,
                                                      op0=ALU.mult), reads=[r_src, r_st], writes=[r_nb])

        def norm_post(slot, gcols, dstT, r_dst, tt):
            nb, r_nb = xnb[slot], r_xnb[slot]
            for kc in range(KC):
                op("pe", lambda kc=kc: nc.tensor.transpose(PT[:, kc, :], nb[:, kc * 128:(kc + 1) * 128], ident[:]),
                   reads=[r_nb, r_ident], writes=[rPT], signal=(kc == KC - 1))
            op("dve", lambda: nc.vector.tensor_tensor(out=dstT[:, :, tt * 128:(tt + 1) * 128], in0=PT[:],
                                                      in1=gcols.unsqueeze(2).broadcast_to([128, KC, 128]),
                                                      op=ALU.mult),
               reads=[rPT, r_vecs], writes=[r_dst])

        def proj_fm(bank, rbank, w_tile, r_w, wcols, srcT, r_src, nk, tcols, mrows=128):
            for kc in range(nk):
                op("pe", lambda kc=kc: nc.tensor.matmul(bank[0:mrows, :], lhsT=w_tile[:, kc, wcols],
                                                        rhs=srcT[:, kc, tcols], start=(kc == 0), stop=(kc == nk - 1)),
                   reads=[r_w, r_src], writes=[rbank], signal=(kc == nk - 1))

        def proj_tm(bank_ap, rbank, srcT, r_src, tcols, w_tile, r_w, wcols, nk):
            for kc in range(nk):
                op("pe", lambda kc=kc: nc.tensor.matmul(bank_ap, lhsT=srcT[:, kc, tcols], rhs=w_tile[:, kc, wcols],
                                                        start=(kc == 0), stop=(kc == nk - 1)),
                   reads=[r_w, r_src], writes=[rbank], signal=(kc == nk - 1))

        for b in range(NB):
            sch.begin_region()
            for tt in range(NT):
                sl = tt % 2
                xs = tt % 3
                if not (b > 0 and tt < 2):
                    sch.dma("sp", xin[xs][:], x[b, tt * 128:(tt + 1) * 128, :], writes=[r_xin[xs]])
                norm_pre(xin[xs], r_xin[xs], sl)
                if tt > 0:
                    norm_post(1 - sl, gmix, xnT, r_xnT, tt - 1)
            norm_post((NT - 1) % 2, gmix, xnT, r_xnT, NT - 1)
            sch.end_region()
            with ExitStack() as mix:
                osbT = sbt(mix, "osbT", [128, 4, S], BF16); r_osbT = Res("osbT")
                oglT = sbt(mix, "oglT", [128, 4, S], BF16); r_oglT = Res("oglT")
                with ExitStack() as ph:
                    wv = sbt(ph, "wv", [128, KC, 512], BF16); r_wv = Res()
                    wq = [sbt(ph, "wq%d" % i, [128, KC, 128], BF16) for i in range(2)]
                    wk = [sbt(ph, "wk%d" % i, [128, KC, 128], BF16) for i in range(2)]
                    r_wq = [Res() for _ in range(2)]; r_wk = [Res() for _ in range(2)]
                    qT = [sbt(ph, "qT%d" % i, [128, S], BF16) for i in range(2)]
                    kT = [sbt(ph, "kT%d" % i, [128, S], BF16) for i in range(2)]
                    r_qT = [Res() for _ in range(2)]; r_kT = [Res() for _ in range(2)]
                    vtok = sbt(ph, "vtok", [128, NT, 512], BF16); r_vtok = Res()
                    KB = 16
                    SPK = [sbt(ph, "SPK%d" % i, [128, 2, 512], BF16) for i in range(KB)]
                    r_SPK = [Res() for _ in range(KB)]
                    W2 = [sbt(ph, "W2%d" % i, [128, 2, 512], BF16) for i in range(2)]
                    r_W = [Res() for _ in range(2)]
                    lacc = sbt(ph, "lacc", [128, 2, 512], BF16); r_lacc = Res()

                    sch.dma("pool", wq[0][:], w_in_v[:, :, OFF_SBQ:OFF_SBQ + 128], writes=[r_wq[0]])
                    sch.dma("pool", wk[0][:], w_in_v[:, :, OFF_SBK:OFF_SBK + 128], writes=[r_wk[0]])
                    sch.dma("pool", wv[:], w_in_v[:, :, OFF_SBV:OFF_SBV + 512], writes=[r_wv])
                    PS = [PA[:, :].rearrange("p (h n) -> p h n", h=2), PCD[:, :].rearrange("p (h n) -> p h n", h=2)]
                    rPS = [[rPA0, rPA1], [rPC, rPD]]
                    accb, racc = PE_, rPE
                    mstrict2 = mstrict[:].unsqueeze(1).broadcast_to([128, 2, 128])

                    def qk_proj(p, sl):
                        for g in range(NG):
                            tc_ = slice(g * 512, (g + 1) * 512)
                            bk, rb = banks5[2]
                            for kc in range(KC):
                                op("pe", lambda kc=kc, bk=bk, tc_=tc_: nc.tensor.matmul(bk, lhsT=wq[sl][:, kc, :], rhs=xnT[:, kc, tc_],
                                                                  start=(kc == 0), stop=(kc == KC - 1)),
                                   reads=[r_wq[sl], r_xnT], writes=[rb], signal=(kc == KC - 1))
                                if kc % 4 == 3:
                                    yield
                            op("act", lambda bk=bk, tc_=tc_: nc.scalar.mul(out=qT[sl][:, tc_], in_=bk, mul=0.125),
                               reads=[rb], writes=[r_qT[sl]])
                            yield
                            bk, rb = banks5[3]
                            for kc in range(KC):
                                op("pe", lambda kc=kc, bk=bk, tc_=tc_: nc.tensor.matmul(bk, lhsT=wk[sl][:, kc, :], rhs=xnT[:, kc, tc_],
                                                                  start=(kc == 0), stop=(kc == KC - 1)),
                                   reads=[r_wk[sl], r_xnT], writes=[rb], signal=(kc == KC - 1))
                                if kc % 4 == 3:
                                    yield
                            op("dve", lambda bk=bk, tc_=tc_: nc.vector.tensor_copy(out=kT[sl][:, tc_], in_=bk),
                               reads=[rb], writes=[r_kT[sl]])
                            yield

                    for _ in qk_proj(0, 0):
                        pass
                    r_vt = [Res("vtok%d" % t) for t in range(NT)]

                    def vproj_gen():
                        for tt in range(NT):
                            bk, rb = banks5[2 + tt % 2]
                            for kc in range(KC):
                                op("pe", lambda kc=kc, bk=bk, tt=tt: nc.tensor.matmul(
                                    bk, lhsT=xnT[:, kc, tt * 128:(tt + 1) * 128], rhs=wv[:, kc, :],
                                    start=(kc == 0), stop=(kc == KC - 1)),
                                   reads=[r_wv, r_xnT], writes=[rb], signal=(kc == KC - 1))
                            op("dve", lambda bk=bk, tt=tt: nc.vector.tensor_copy(out=vtok[:, tt, :], in_=bk),
                               reads=[rb], writes=[r_vt[tt]], cost=0.7)
                            yield
                    for p in range(4):
                        sl = p % 2
                        nxt = None
                        if p + 1 < 4:
                            sch.dma("pool", wq[1 - sl][:], w_in_v[:, :, OFF_SBQ + (p + 1) * 128:OFF_SBQ + (p + 2) * 128],
                                    writes=[r_wq[1 - sl]])
                            sch.dma("pool", wk[1 - sl][:], w_in_v[:, :, OFF_SBK + (p + 1) * 128:OFF_SBK + (p + 2) * 128],
                                    writes=[r_wk[1 - sl]])
                            nxt = qk_proj(p + 1, 1 - sl)

                        def tick():
                            nonlocal nxt
                            if nxt is not None:
                                try:
                                    next(nxt)
                                except StopIteration:
                                    nxt = None
                        units = []
                        for qg in range(NG):
                            lst = [(4 * qg + i, 128 * i) for i in (3, 2, 1, 0)]
                            lst += [(kb, 0) for kb in range(4 * qg - 1, -1, -1)]
                            for ui, (kb, c0) in enumerate(lst):
                                units.append(dict(qg=qg, kb=kb, c0=c0, first=(ui == 0),
                                                  last=(ui == len(lst) - 1), diag=(kb >= 4 * qg)))
                        nU = len(units)

                        def qk_mm(u, dst, rdst, stop):
                            c0 = u["c0"]
                            qc = slice(u["qg"] * 512 + c0, u["qg"] * 512 + 512)
                            kc_ = slice(u["kb"] * 128, u["kb"] * 128 + 128)
                            for hh in range(2):
                                R = slice(hh * 64, hh * 64 + 64)
                                op("pe", lambda hh=hh, R=R: nc.tensor.matmul(dst[:, hh, c0:512], lhsT=kT[sl][R, kc_],
                                                                  rhs=qT[sl][R, qc], start=True, stop=stop),
                                   reads=[r_kT[sl], r_qT[sl]], writes=[rdst[hh]], signal=(stop and hh == 1))

                        def stB(i, j):
                            u = units[i]
                            c0 = u["c0"]
                            SP = SPK[j]
                            op("act", lambda: nc.scalar.activation(out=SP[:, :, c0:512], in_=PS[j % 2][:, :, c0:512],
                                                                   func=AF.Softplus),
                               reads=rPS[j % 2], writes=[r_SPK[j]], cost=1.05)
                            if u["diag"]:
                                op("dve", lambda: nc.vector.tensor_tensor(out=SP[:, :, c0:c0 + 128],
                                                                          in0=SP[:, :, c0:c0 + 128], in1=mstrict2,
                                                                          op=ALU.mult),
                                   reads=[r_SPK[j], r_mstrict], writes=[r_SPK[j]])

                        def stC(i, j):
                            u = units[i]
                            c0 = u["c0"]
                            SP = SPK[j]
                            lw, rlw = PS[j % 2], rPS[j % 2]
                            if u["first"]:
                                op("pool", lambda: nc.gpsimd.memset(lacc[:], 0.0), writes=[r_lacc])
                            qk_mm(u, lw, rlw, False)
                            for hh in range(2):
                                op("pe", lambda hh=hh: nc.tensor.matmul(lw[:, hh, c0:512], lhsT=trineg[:], rhs=SP[:, hh, c0:512],
                                                                  start=False, stop=False),
                                   reads=[r_trineg, r_SPK[j]], writes=[rlw[hh]], signal=False)
                                if u["diag"]:
                                    op("pe", lambda hh=hh: nc.tensor.matmul(lw[:, hh, c0:c0 + 128], lhsT=ident[:], rhs=mnegbig[:],
                                                                      start=False, stop=False),
                                       reads=[r_ident, r_mnegbig], writes=[rlw[hh]], signal=False)
                                cz = c0 + 128 if (u["diag"] and c0 < 384) else c0
                                op("pe", lambda hh=hh, cz=cz: nc.tensor.matmul(lw[:, hh, cz:512], lhsT=onesneg[:],
                                                                         rhs=lacc[:, hh, cz:512], start=False, stop=True),
                                   reads=[r_onesneg, r_lacc], writes=[rlw[hh]], signal=(hh == 1))
                            if not u["last"]:
                                op("dve", lambda: nc.vector.tensor_tensor(out=lacc[:, :, c0:512], in0=lacc[:, :, c0:512],
                                                                          in1=SP[:, :, c0:512], op=ALU.add),
                                   reads=[r_lacc, r_SPK[j]], writes=[r_lacc], cost=0.7)

                        def stD(i, j):
                            u = units[i]
                            c0 = u["c0"]
                            W = W2[j % 2]
                            op("act", lambda: nc.scalar.activation(out=W[:, :, c0:512], in_=PS[j % 2][:, :, c0:512],
                                                                   func=AF.Exp),
                               reads=rPS[j % 2], writes=[r_W[j % 2]], cost=1.05)

                        def stE(i, j):
                            u = units[i]
                            c0 = u["c0"]
                            W = W2[j % 2]
                            kb = u["kb"]
                            if u["first"]:
                                op("pe", lambda: nc.tensor.matmul(accb[:, :], lhsT=zeros[:, 0:128], rhs=zeros[:, :],
                                                                  start=True, stop=False),
                                   reads=[r_zeros], writes=[racc], signal=False)
                            for hh in range(2):
                                R = slice(hh * 64, hh * 64 + 64)
                                hcol = slice((2 * p + hh) * 64, (2 * p + hh + 1) * 64)
                                op("pe", lambda hh=hh, R=R, hcol=hcol: nc.tensor.matmul(accb[R, c0:512], lhsT=vtok[:, kb, hcol],
                                                                  rhs=W[:, hh, c0:512], start=False, stop=u["last"]),
                                   reads=[r_vt[kb], r_W[j % 2]], writes=[racc], signal=(hh == 1))
                            if u["last"]:
                                qg = u["qg"]
                                op("dve", lambda: nc.vector.tensor_copy(out=osbT[:, p, qg * 512:(qg + 1) * 512],
                                                                        in_=accb[:, :]),
                                   reads=[racc], writes=[r_osbT])

                        for u0 in range(0, nU, KB):
                            kk = min(KB, nU - u0)
                            sch.begin_region()
                            vgen = vproj_gen() if (p == 0 and u0 == 0) else None
                            for j in range(kk + 1):
                                if j < kk:
                                    qk_mm(units[u0 + j], PS[j % 2], rPS[j % 2], True)
                                if j >= 1:
                                    stB(u0 + j - 1, j - 1)
                                if vgen is not None:
                                    next(vgen, None)
                                else:
                                    tick()
                            if vgen is not None:
                                for _ in vgen:
                                    pass
                            sch.end_region()
                            sch.begin_region()
                            for j in range(kk + 1):
                                if j < kk:
                                    stC(u0 + j, j)
                                if j >= 1:
                                    stD(u0 + j - 1, j - 1)
                                    stE(u0 + j - 1, j - 1)
                                tick()
                            sch.end_region()
                        while nxt is not None:
                            tick()
                    prefetch_gla_w0()
                    sch.barrier()
                with ExitStack() as ph:
                    wg = sbt(ph, "wg", [128, KC, 1040], BF16); r_wg = Res(); r_wgv = Res(); r_wgr = Res()
                    qTg = sbt(ph, "qTg", [128, 2, S], BF16); r_qTg = Res()
                    kTg = sbt(ph, "kTg", [128, 2, S], BF16); r_kTg = Res()
                    ga_hi = sbt(ph, "ga_hi", [16, S], BF16); ga_lo = sbt(ph, "ga_lo", [16, S], BF16); r_gaT = Res()
                    sp_hi, r_sp_hi = None, None
                    def dbl(name, shape, dt):
                        return [sbt(ph, name + str(i), shape, dt) for i in range(2)], [Res(name + str(i)) for i in range(2)]
                    E1, r_E1 = dbl("E1", [128, 256], F32)
                    SPa, r_SPa = dbl("SPa", [128, 256], F32)
                    SPh, r_SPh = dbl("SPh", [128, 256], BF16)
                    SPl, r_SPl = dbl("SPl", [128, 256], BF16)
                    Dend, r_Dend = dbl("Dend", [128, 256], F32)
                    kend, r_kend = dbl("kend", [128, 256], BF16)
                    Eq, r_Eq = dbl("Eq", [128, 2, 128], F32)
                    Ek, r_Ek = dbl("Ek", [128, 2, 128], F32)
                    qdec, r_qdec = dbl("qdec", [128, 2, 128], BF16)
                    kinv, r_kinv = dbl("kinv", [128, 2, 128], BF16)
                    vbf, r_vbf = dbl("vbf", [128, 512], BF16)
                    er, r_er = dbl("er", [128, 512], F32)
                    gr, r_gr = dbl("gr", [128, 512], F32)
                    attm, _ = dbl("attm", [128, 4, 128], BF16)
                    r_attm = [[Res() for _ in range(4)] for _ in range(2)]
                    og, r_og = dbl("og", [128, 512], BF16)
                    gst, r_gst = dbl("gst", [128, 12], F32)
                    S32 = sbt(ph, "S32", [128, 2, 128], F32); r_S32 = Res()
                    Sbf = sbt(ph, "Sbf", [128, 2, 128], BF16); r_Sbf = Res()
                    sch.dma("pool", wg[:, :, 1024:1040], w_in_v[:, :, OFF_G + 1536:OFF_G + 1552], writes=[r_wg])
                    sch.dma("pool", wg[:, :, 0:512], w_in_v[:, :, OFF_G + 512:OFF_G + 1024], writes=[r_wgv])
                    sch.dma("pool", wg[:, :, 512:1024], w_in_v[:, :, OFF_G + 1024:OFF_G + 1536], writes=[r_wgr])
                    for g in range(NG):
                        tc_ = slice(g * 512, (g + 1) * 512)
                        for c2 in range(2):
                            bk, rb = banks5[(2 * c2) % 7]
                            proj_fm(bk, rb, wgA, r_wgA, slice(c2 * 128, (c2 + 1) * 128), xnT, r_xnT, KC, tc_)
                            op("act", lambda bk=bk, c2=c2: nc.scalar.mul(out=qTg[:, c2, tc_], in_=bk, mul=0.125),
                               reads=[rb], writes=[r_qTg])
                            bk, rb = banks5[(2 * c2 + 1) % 7]
                            proj_fm(bk, rb, wgA, r_wgA, slice(256 + c2 * 128, 256 + (c2 + 1) * 128), xnT, r_xnT, KC, tc_)
                            op("dve", lambda bk=bk, c2=c2: nc.vector.tensor_copy(out=kTg[:, c2, tc_], in_=bk),
                               reads=[rb], writes=[r_kTg])
                        bk, rb = banks5[4]
                        proj_fm(bk, rb, wg, r_wg, slice(1024, 1040), xnT, r_xnT, KC, tc_, mrows=16)
                        op("act", lambda bk=bk: nc.scalar.copy(out=ga_hi[0:16, tc_], in_=bk[0:16, :]),
                           reads=[rb], writes=[r_gaT])
                        op("dve", lambda bk=bk: nc.vector.tensor_tensor(out=ga_lo[0:16, tc_], in0=bk[0:16, :],
                                                                        in1=ga_hi[0:16, tc_], op=ALU.subtract),
                           reads=[rb, r_gaT], writes=[r_gaT])
                    op("pool", lambda: nc.gpsimd.memset(S32[:], 0.0), writes=[r_S32])
                    op("pool", lambda: nc.gpsimd.memset(Sbf[:], 0.0), writes=[r_Sbf])
                    Pk, rPk = PA[:, 0:256], rPA0
                    Pa, rPa = PA[:, 256:512], rPA0
                    Pv, rPv = PA[:, 512:1024], rPA1
                    Pr, rPr = PB[:, 0:512], rPB0
                    Po, rPo = PB[:, 512:1024], rPB1
                    PR, rPR = PC[:, 0:256], rPC
                    Pcum, rPcum = PC[:, 256:512], rPC
                    PDS, rPDS = PD[:, 0:256], rPD
                    Patt = [PE_[:, 0:128], PD[:, 256:384], PE_[:, 128:256], PD[:, 384:512]]
                    rPatt = [rPE, rPD, rPE, rPD]

                    def gla_front(tt):
                        z = tt % 2
                        tc_ = slice(tt * 128, (tt + 1) * 128)
                        proj_tm(Pk, rPk, xnT, r_xnT, tc_, wgA, r_wgA, slice(256, 512), KC)
                        yield
                        for mi, (lh, rh) in enumerate(((ga_hi, wa_hi), (ga_lo, wa_hi), (ga_hi, wa_lo))):
                            op("pe", lambda lh=lh, rh=rh, mi=mi: nc.tensor.matmul(Pa, lhsT=lh[0:16, tc_], rhs=rh[0:16, :],
                                                                              start=(mi == 0), stop=False),
                               reads=[r_gaT, r_wahl], writes=[rPa], signal=False, cost=0.12)
                        for mi, bh in enumerate((ba_hi, ba_lo)):
                            op("pe", lambda bh=bh, mi=mi: nc.tensor.matmul(Pa, lhsT=ones_bf[0:1, :], rhs=bh[0:1, :],
                                                                       start=False, stop=(mi == 1)),
                               reads=[r_ones_bf, r_bahl], writes=[rPa], signal=(mi == 1), cost=0.12)
                        yield
                        op("act", lambda: nc.scalar.activation(out=E1[z][:], in_=Pa, func=AF.Exp, scale=-1.0),
                           reads=[rPa], writes=[r_E1[z]])
                        yield
                        op("act", lambda: nc.scalar.activation(out=SPa[z][:], in_=E1[z][:], func=AF.Ln, bias=1.0),
                           reads=[r_E1[z]], writes=[r_SPa[z]])
                        yield
                        proj_tm(Pv, rPv, xnT, r_xnT, tc_, wg, r_wgv, slice(0, 512), KC)
                        yield
                        proj_tm(Pr, rPr, xnT, r_xnT, tc_, wg, r_wgr, slice(512, 1024), KC)
                        yield
                        op("dve", lambda: nc.vector.tensor_copy(out=SPh[z][:], in_=SPa[z][:]),
                           reads=[r_SPa[z]], writes=[r_SPh[z]], cost=0.3)
                        op("dve", lambda: nc.vector.tensor_tensor(out=SPl[z][:], in0=SPa[z][:], in1=SPh[z][:],
                                                                  op=ALU.subtract),
                           reads=[r_SPa[z], r_SPh[z]], writes=[r_SPl[z]], cost=0.4)
                        for mi, (sp_, rsp_) in enumerate(((SPh, r_SPh), (SPl, r_SPl))):
                            op("pe", lambda sp_=sp_, mi=mi: nc.tensor.matmul(PR, lhsT=ust_bf[:], rhs=sp_[z][:],
                                                                          start=(mi == 0), stop=(mi == 1)),
                               reads=[r_ust_bf, rsp_[z]], writes=[rPR], signal=(mi == 1), cost=0.12)
                        yield
                        for c2 in range(2):
                            for mi, (sp_, rsp_) in enumerate(((SPh, r_SPh), (SPl, r_SPl))):
                                op("pe", lambda c2=c2, sp_=sp_, mi=mi: nc.tensor.matmul(
                                    Pcum[:, c2 * 128:(c2 + 1) * 128], lhsT=sp_[z][:, c2 * 128:(c2 + 1) * 128],
                                    rhs=tincl_bf[:], start=(mi == 0), stop=(mi == 1)),
                                   reads=[r_tincl_bf, rsp_[z]], writes=[rPcum], signal=(c2 == 1 and mi == 1), cost=0.08)
                            yield
                        op("act", lambda: nc.scalar.copy(out=vbf[z][:], in_=Pv), reads=[rPv], writes=[r_vbf[z]])
                        yield
                        op("act", lambda: nc.scalar.activation(out=er[z][:], in_=Pr, func=AF.Exp, scale=-1.0),
                           reads=[rPr], writes=[r_er[z]])
                        yield
                        op("act", lambda: nc.scalar.activation(out=Dend[z][:], in_=PR, func=AF.Exp, scale=-1.0 / 16),
                           reads=[rPR], writes=[r_Dend[z]])
                        yield
                        op("dve", lambda: nc.vector.tensor_tensor(out=kend[z][:], in0=Pk, in1=Dend[z][:], op=ALU.mult),
                           reads=[rPk, r_Dend[z]], writes=[r_kend[z]])
                        yield
                        op("act", lambda: nc.scalar.activation(out=Eq[z][:].rearrange("p a b -> p (a b)"), in_=Pcum,
                                                               func=AF.Exp, scale=-1.0 / 16),
                           reads=[rPcum], writes=[r_Eq[z]])
                        yield
                        op("act", lambda: nc.scalar.activation(out=Ek[z][:].rearrange("p a b -> p (a b)"), in_=Pcum,
                                                               func=AF.Exp, scale=1.0 / 16),
                           reads=[rPcum], writes=[r_Ek[z]])
                        yield
                        op("dve", lambda: nc.vector.tensor_tensor(out=qdec[z][:], in0=qTg[:, :, tc_], in1=Eq[z][:],
                                                                  op=ALU.mult),
                           reads=[r_qTg, r_Eq[z]], writes=[r_qdec[z]])
                        yield
                        op("dve", lambda: nc.vector.tensor_tensor(out=kinv[z][:], in0=kTg[:, :, tc_], in1=Ek[z][:],
                                                                  op=ALU.mult),
                           reads=[r_kTg, r_Ek[z]], writes=[r_kinv[z]])
                        yield
                        op("act", lambda: nc.scalar.activation(out=er[z][:], in_=er[z][:], func=AF.Ln, bias=1.0),
                           reads=[r_er[z]], writes=[r_er[z]])
                        yield
                        op("act", lambda: nc.scalar.activation(out=er[z][:], in_=er[z][:], func=AF.Exp, scale=-1.0),
                           reads=[r_er[z]], writes=[r_er[z]])
                        yield
                        op("dve", lambda: nc.vector.tensor_tensor(out=gr[z][:], in0=Pr, in1=er[z][:], op=ALU.mult),
                           reads=[rPr, r_er[z]], writes=[r_gr[z]])
                        yield
                        op("pool", lambda: nc.gpsimd.tensor_tensor(out=gr[z][:], in0=gr[z][:], in1=ggla_bc[:],
                                                                   op=ALU.mult),
                           reads=[r_gr[z], r_ggla], writes=[r_gr[z]])
                        yield

                    def gla_back(tt):
                        z = tt % 2
                        tc_ = slice(tt * 128, (tt + 1) * 128)
                        for h in range(4):
                            c2, hh = divmod(h, 2)
                            R = slice(hh * 64, hh * 64 + 64)
                            hc = slice(h * 128, (h + 1) * 128)
                            op("pe", lambda h=h, c2=c2, R=R, hc=hc: nc.tensor.matmul(Patt[h], lhsT=kinv[z][R, c2, :], rhs=qdec[z][R, c2, :],
                                                              start=True, stop=True),
                               reads=[r_kinv[z], r_qdec[z]], writes=[rPatt[h]])
                            yield
                        for h in range(4):
                            c2, hh = divmod(h, 2)
                            R = slice(hh * 64, hh * 64 + 64)
                            hc = slice(h * 128, (h + 1) * 128)
                            op("dve", lambda h=h, c2=c2, R=R, hc=hc: nc.vector.tensor_tensor(out=attm[z][:, h, :], in0=Patt[h], in1=tincl[:],
                                                                      op=ALU.mult),
                               reads=[rPatt[h], r_tincl], writes=[r_attm[z][h]])
                            yield
                            op("pe", lambda h=h, c2=c2, R=R, hc=hc: nc.tensor.matmul(Po[:, hc], lhsT=attm[z][:, h, :], rhs=vbf[z][:, hc],
                                                              start=True, stop=False),
                               reads=[r_attm[z][h], r_vbf[z]], writes=[rPo], signal=False)
                            yield
                            op("pe", lambda h=h, c2=c2, R=R, hc=hc: nc.tensor.matmul(Po[:, hc], lhsT=qdec[z][R, c2, :], rhs=Sbf[R, c2, :],
                                                              start=False, stop=True),
                               reads=[r_qdec[z], r_Sbf], writes=[rPo], signal=False)
                            yield
                            op("pe", lambda h=h, c2=c2, R=R, hc=hc: nc.tensor.matmul(PDS[R, c2 * 128:(c2 + 1) * 128],
                                                              lhsT=kend[z][:, h * 64:(h + 1) * 64], rhs=vbf[z][:, hc],
                                                              start=True, stop=True),
                               reads=[r_kend[z], r_vbf[z]], writes=[rPDS])
                            yield
                        for c2 in range(2):
                            op("dve", lambda c2=c2: nc.vector.scalar_tensor_tensor(
                                out=S32[:, c2, :], in0=S32[:, c2, :], scalar=Eq[z][:, c2, 127:128],
                                in1=PDS[:, c2 * 128:(c2 + 1) * 128], op0=ALU.mult, op1=ALU.add),
                               reads=[r_S32, r_Eq[z], rPDS], writes=[r_S32])
                            yield
                        op("dve", lambda: nc.vector.tensor_copy(out=Sbf[:], in_=S32[:]), reads=[r_S32], writes=[r_Sbf])
                        yield
                        for h in range(4):
                            hc = slice(h * 128, (h + 1) * 128)
                            op("act", lambda h=h, hc=hc: nc.scalar.activation(out=junk[:, hc], in_=Po[:, hc],
                                                                              func=AF.Square,
                                                                              accum_out=gst[z][:, h:h + 1]),
                               reads=[rPo], writes=[r_junk, r_gst[z]])
                            yield
                        op("act", lambda: nc.scalar.activation(out=gst[z][:, 4:8], in_=gst[z][:, 0:4], func=AF.Ln,
                                                               scale=1.0 / 128, bias=EPS),
                           reads=[r_gst[z]], writes=[r_gst[z]])
                        yield
                        op("act", lambda: nc.scalar.activation(out=gst[z][:, 8:12], in_=gst[z][:, 4:8], func=AF.Exp,
                                                               scale=-0.5), reads=[r_gst[z]], writes=[r_gst[z]])
                        yield
                        for h in range(4):
                            hc = slice(h * 128, (h + 1) * 128)
                            op("dve", lambda h=h, hc=hc: nc.vector.scalar_tensor_tensor(
                                out=og[z][:, hc], in0=Po[:, hc], scalar=gst[z][:, 8 + h:9 + h], in1=gr[z][:, hc],
                                op0=ALU.mult, op1=ALU.mult),
                               reads=[rPo, r_gst[z], r_gr[z]], writes=[r_og[z]])
                            yield
                        for h in range(4):
                            op("pe", lambda h=h: nc.tensor.transpose(PT[:, h, :], og[z][:, h * 128:(h + 1) * 128],
                                                                     ident[:]),
                               reads=[r_og[z], r_ident], writes=[rPT], signal=(h == 3))
                            yield
                        op("dve", lambda: nc.vector.tensor_copy(out=oglT[:, :, tc_], in_=PT[:, 0:4, :]),
                           reads=[rPT], writes=[r_oglT])
                        yield

                    def drive(*gens):
                        gens = [g for g in gens if g is not None]
                        while gens:
                            for g in list(gens):
                                try:
                                    next(g)
                                except StopIteration:
                                    gens.remove(g)

                    sch.begin_region()
                    for tt in range(NT):
                        for _ in gla_front(tt):
                            pass
                        for _ in gla_back(tt):
                            pass
                    sch.end_region()
                    prefetch_merge_w0()
                    sch.barrier()
                with ExitStack() as ph:
                    yT = sbt(ph, "yT", [128, KC, S], BF16); r_yT = Res()
                    wgs = [wgs0, sbt(ph, "wgs1", [128, KC, 128], BF16)]
                    wgg = [wgg0, sbt(ph, "wgg1", [128, KC, 128], BF16)]
                    wsb = [wsb0, sbt(ph, "wsb1", [128, 4, 128], BF16)]
                    wgl = [wgl0, sbt(ph, "wgl1", [128, 4, 128], BF16)]
                    r_wm = [r_wm0, Res()]
                    wo = sbt(ph, "wo", [128, KC, D], BF16); r_wo = Res()
                    sg = [sbt(ph, "sg%d" % i, [128, 512], F32) for i in range(4)]
                    r_sg = [Res() for _ in range(4)]
                    t1 = [sbt(ph, "t1%d" % i, [128, 512], F32) for i in range(2)]
                    t2 = [sbt(ph, "t2%d" % i, [128, 512], F32) for i in range(2)]
                    r_t1 = [Res() for _ in range(2)]; r_t2 = [Res() for _ in range(2)]
                    ht = [sbt(ph, "ht%d" % i, [128, D], F32) for i in range(2)]
                    r_ht = [Res() for _ in range(2)]

                    def load_merge_w(n, sl):
                        r = r_wm[sl]
                        sch.dma("pool", wgs[sl][:], w_in_v[:, :, OFF_GSB + n * 128:OFF_GSB + (n + 1) * 128], writes=[r])
                        sch.dma("pool", wgg[sl][:], w_in_v[:, :, OFF_GGLA + n * 128:OFF_GGLA + (n + 1) * 128], writes=[r])
                        sch.dma("pool", wsb[sl][:], wbsb_v[:, :, n * 128:(n + 1) * 128], writes=[r])
                        sch.dma("pool", wgl[sl][:], wbgl_v[:, :, n * 128:(n + 1) * 128], writes=[r])
                    sch.dma("pool", wo[:], w_out_v, writes=[r_wo])
                    sch.begin_region()
                    it = 0
                    for n in range(KC):
                        sl = n % 2
                        if n + 1 < KC:
                            load_merge_w(n + 1, 1 - sl)
                        for g in range(NG):
                            tc_ = slice(g * 512, (g + 1) * 512)
                            j = it % 2
                            it += 1
                            (b0, rb0), (b1, rb1), (b2, rb2), (b3, rb3) = [banks5[(4 * (it - 1) + q_) % 7] for q_ in range(4)]
                            proj_fm(b0, rb0, wgs[sl], r_wm[sl], slice(0, 128), xnT, r_xnT, KC, tc_)
                            op("act", lambda b0=b0, j=j, n=n: nc.scalar.activation(out=sg[2 * j][:], in_=b0, func=AF.Sigmoid,
                                                                                   bias=bgate(0, n)),
                               reads=[rb0, r_vecs], writes=[r_sg[2 * j]])
                            proj_fm(b1, rb1, wgg[sl], r_wm[sl], slice(0, 128), xnT, r_xnT, KC, tc_)
                            op("act", lambda b1=b1, j=j, n=n: nc.scalar.activation(out=sg[2 * j + 1][:], in_=b1,
                                                                                   func=AF.Sigmoid, bias=bgate(1, n)),
                               reads=[rb1, r_vecs], writes=[r_sg[2 * j + 1]])
                            proj_fm(b2, rb2, wsb[sl], r_wm[sl], slice(0, 128), osbT, r_osbT, 4, tc_)
                            op("dve", lambda b2=b2, j=j: nc.vector.tensor_tensor(out=t1[j][:], in0=b2, in1=sg[2 * j][:],
                                                                                 op=ALU.mult),
                               reads=[rb2, r_sg[2 * j]], writes=[r_t1[j]])
                            proj_fm(b3, rb3, wgl[sl], r_wm[sl], slice(0, 128), oglT, r_oglT, 4, tc_)
                            op("dve", lambda b3=b3, j=j: nc.vector.tensor_tensor(out=t2[j][:], in0=b3, in1=sg[2 * j + 1][:],
                                                                                 op=ALU.mult),
                               reads=[rb3, r_sg[2 * j + 1]], writes=[r_t2[j]])
                            op("pool", lambda j=j, n=n, tc_=tc_: nc.gpsimd.tensor_tensor(out=yT[:, n, tc_], in0=t1[j][:],
                                                                                         in1=t2[j][:], op=ALU.add),
                               reads=[r_t1[j], r_t2[j]], writes=[r_yT])
                    sch.end_region()
                    sch.begin_region()
                    for tt in range(NT):
                        sl = tt % 2
                        tc_ = slice(tt * 128, (tt + 1) * 128)
                        PW, rW0, rW1 = (PA, rPA0, rPA1) if sl == 0 else (PB, rPB0, rPB1)
                        sch.dma("sp", xin[sl][:], x[b, tc_, :], writes=[r_xin[sl]])
                        for half in range(2):
                            proj_tm(PW[:, half * 512:(half + 1) * 512], (rW0, rW1)[half], yT, r_yT, tc_, wo, r_wo,
                                    slice(half * 512, (half + 1) * 512), KC)
                        op("dve", lambda PW=PW, sl=sl: nc.vector.tensor_tensor(out=ht[sl][:], in0=PW[:, :], in1=xin[sl][:],
                                                                               op=ALU.add),
                           reads=[rW0, rW1, r_xin[sl]], writes=[r_ht[sl]])
                        sch.dma("sp", hscr[tc_, :], ht[sl][:], reads=[r_ht[sl]], writes=[r_hscr[tt]])
                        norm_pre(ht[sl], r_ht[sl], sl)
                        if tt > 0:
                            norm_post(1 - sl, gffn, xnT, r_xnT, tt - 1)
                    norm_post((NT - 1) % 2, gffn, xnT, r_xnT, NT - 1)
                    sch.end_region()
                    prefetch_ffn_w0()
                    sch.barrier()
            hnT, r_hnT = xnT, r_xnT
            HS = min(S, 1024)
            NGH = HS // 512
            NTH = HS // 128
            with ExitStack() as ph:
                wfo = sbt(ph, "wfo", [128, NF, D], BF16); r_wfo = Res()
                actT = sbt(ph, "actT", [128, NF, HS], BF16); r_actT = Res()
                wa = [wa0, sbt(ph, "wa1", [128, KC, 128], BF16)]
                wgt = [wgt0, sbt(ph, "wgt1", [128, KC, 128], BF16)]
                r_wf = [r_wf0, Res()]
                abuf = [sbt(ph, "abuf%d" % i, [128, 514], F32) for i in range(2)]
                r_abuf = [Res() for _ in range(2)]
                tcv = [sbt(ph, "tcv%d" % i, [128, 512], F32) for i in range(2)]
                r_tcv = [Res() for _ in range(2)]
                gel = [sbt(ph, "gel%d" % i, [128, 512], F32) for i in range(2)]
                r_gel = [Res() for _ in range(2)]
                halo = sbt(ph, "halo", [128, NF, 2], F32); r_halo = Res()
                hin = [sbt(ph, "hin%d" % i, [128, D], F32) for i in range(2)]
                r_hin = [Res() for _ in range(2)]
                h2 = [sbt(ph, "h2%d" % i, [128, D], F32) for i in range(2)]
                r_h2 = [Res() for _ in range(2)]
                op("pool", lambda: nc.gpsimd.memset(halo[:], 0.0), writes=[r_halo])

                def load_ffn_w(f, sl):
                    sch.dma("pool", wa[sl][:], wfi_v[:, :, f * 128:(f + 1) * 128], writes=[r_wf[sl]])
                    sch.dma("pool", wgt[sl][:], wfi_v[:, :, DFF + f * 128:DFF + (f + 1) * 128], writes=[r_wf[sl]])
                it = 0
                for hs in range(S // HS):
                    if hs == 0:
                        load_ffn_w(1, 1)
                    else:
                        load_ffn_w(0, 0)
                    for f in range(NF):
                        sl = f % 2
                        if f + 1 < NF and not (hs == 0 and f == 0):
                            load_ffn_w(f + 1, 1 - sl)
                        if hs == 0 and f < 11:
                            sch.dma("pool", wfo[:, 2 * f:2 * f + 2, :], wfo_v[:, 2 * f:2 * f + 2, :], writes=[r_wfo])
                        for g in range(NGH):
                            tc_ = slice(hs * HS + g * 512, hs * HS + (g + 1) * 512)
                            lc_ = slice(g * 512, (g + 1) * 512)
                            j = it % 2
                            it += 1
                            (bA, rbA), (bG, rbG) = (banks5[4], banks5[5]) if j == 0 else (banks5[6], banks5[3])
                            ab, rab = abuf[j], r_abuf[j]
                            proj_fm(bA, rbA, wa[sl], r_wf[sl], slice(0, 128), hnT, r_hnT, KC, tc_)
                            proj_fm(bG, rbG, wgt[sl], r_wf[sl], slice(0, 128), hnT, r_hnT, KC, tc_)
                            op("pool", lambda ab=ab, f=f: nc.gpsimd.tensor_copy(out=ab[:, 0:2], in_=halo[:, f, :]),
                               reads=[r_halo], writes=[rab])
                            op("act", lambda ab=ab, bA=bA: nc.scalar.copy(out=ab[:, 2:514], in_=bA),
                               reads=[rbA], writes=[rab])
                            op("pool", lambda ab=ab, f=f: nc.gpsimd.tensor_copy(out=halo[:, f, :], in_=ab[:, 512:514]),
                               reads=[rab], writes=[r_halo])
                            tv, rtv = tcv[j], r_tcv[j]
                            op("dve", lambda ab=ab, tv=tv, f=f: nc.vector.tensor_scalar(
                                out=tv[:], in0=ab[:, 2:514], scalar1=convw(2, f), scalar2=convb(f), op0=ALU.mult,
                                op1=ALU.add), reads=[rab, r_vecs], writes=[rtv])
                            op("dve", lambda ab=ab, tv=tv, f=f: nc.vector.scalar_tensor_tensor(
                                out=tv[:], in0=ab[:, 1:513], scalar=convw(1, f), in1=tv[:], op0=ALU.mult, op1=ALU.add),
                               reads=[rab, r_vecs, rtv], writes=[rtv])
                            op("dve", lambda ab=ab, tv=tv, f=f: nc.vector.scalar_tensor_tensor(
                                out=tv[:], in0=ab[:, 0:512], scalar=convw(0, f), in1=tv[:], op0=ALU.mult, op1=ALU.add),
                               reads=[rab, r_vecs, rtv], writes=[rtv])
                            ge, rge = gel[j], r_gel[j]
                            op("act", lambda tv=tv, ge=ge: nc.scalar.activation(out=ge[:], in_=tv[:],
                                                                                func=AF.Gelu_apprx_tanh),
                               reads=[rtv], writes=[rge])
                            op("dve", lambda ge=ge, bG=bG, f=f, lc_=lc_: nc.vector.tensor_tensor(
                                out=actT[:, f, lc_], in0=bG, in1=ge[:], op=ALU.mult),
                               reads=[rbG, rge], writes=[r_actT])
                    for t in range(NTH):
                        tt = hs * NTH + t
                        sl = tt % 2
                        lt_ = slice(t * 128, (t + 1) * 128)
                        tc_ = slice(tt * 128, (tt + 1) * 128)
                        PW, rW0, rW1 = (PA, rPA0, rPA1) if sl == 0 else (PB, rPB0, rPB1)
                        sch.dma("sp", hin[sl][:], hscr[tc_, :], reads=[r_hscr[tt]], writes=[r_hin[sl]])
                        for half in range(2):
                            proj_tm(PW[:, half * 512:(half + 1) * 512], (rW0, rW1)[half], actT, r_actT, lt_, wfo, r_wfo,
                                    slice(half * 512, (half + 1) * 512), NF)
                        op("dve", lambda PW=PW, sl=sl: nc.vector.tensor_tensor(out=h2[sl][:], in0=PW[:, :], in1=hin[sl][:],
                                                                               op=ALU.add),
                           reads=[rW0, rW1, r_hin[sl]], writes=[r_h2[sl]])
                        st, r_st = stt[sl], r_stt[sl]
                        rms_stats(h2[sl][:], r_h2[sl], st, r_st, D, D)
                        op("dve", lambda sl=sl, st=st: nc.vector.scalar_tensor_tensor(
                            out=h2[sl][:], in0=h2[sl][:], scalar=st[:, 2:3], in1=gfin_bc[:], op0=ALU.mult, op1=ALU.mult),
                           reads=[r_h2[sl], r_st, r_gfin], writes=[r_h2[sl]])
                        sch.dma("sp", out[b, tc_, :], h2[sl][:], reads=[r_h2[sl]])
                if b + 1 < NB:
                    for tt in range(2):
                        sch.dma("sp", xin[tt][:], x[b + 1, tt * 128:(tt + 1) * 128, :], writes=[r_xin[tt]])
                sch.barrier()
    return nc


_NC_CACHE = {}


def _get_nc(S, NB):
    key = (S, NB)
    if key not in _NC_CACHE:
        _NC_CACHE[key] = build_nc(S, NB)
    return _NC_CACHE[key]


def kernel(x, norm_mix_g, w_in, b_gate, w_alpha_up, b_alpha, gla_norm_g, w_branch_sb, w_branch_gla, w_out,
           norm_ffn_g, w_ffn_in, conv_w, conv_b, w_ffn_out, norm_final_g, n_cores=N_CORES):
    f = lambda a: np.ascontiguousarray(np.asarray(a, dtype=np.float32))
    x = f(x)
    B, S, _ = x.shape
    NB = B // n_cores
    shared = {
        "norm_mix_g": f(norm_mix_g)[0], "w_in": f(w_in)[0], "b_gate": f(b_gate)[0],
        "w_alpha_up": f(w_alpha_up)[0], "b_alpha": f(b_alpha)[0], "gla_norm_g": f(gla_norm_g)[0],
        "w_branch_sb": f(w_branch_sb)[0], "w_branch_gla": f(w_branch_gla)[0], "w_out": f(w_out)[0],
        "norm_ffn_g": f(norm_ffn_g)[0], "w_ffn_in": f(w_ffn_in)[0], "conv_w": f(conv_w)[0],
        "conv_b": f(conv_b)[0], "w_ffn_out": f(w_ffn_out)[0], "norm_final_g": f(norm_final_g),
    }
    nc = _get_nc(S, NB)
    in_maps = []
    for c in range(n_cores):
        m = dict(shared)
        m["x"] = np.ascontiguousarray(x[c * NB:(c + 1) * NB])
        in_maps.append(m)
    res = run_bass_kernel_spmd(nc, in_maps, core_ids=list(range(n_cores)))
    return np.concatenate([np.asarray(r["out"]) for r in res.results], axis=0).astype(np.float32)
```

```python
import numpy as np
from contextlib import ExitStack
import concourse.bass as bass
import concourse.mybir as mybir
from concourse.bass_utils import run_bass_kernel_spmd

F32 = mybir.dt.float32
BF16 = mybir.dt.bfloat16
AF = mybir.ActivationFunctionType
ALU = mybir.AluOpType

D = 1024
KC = 8
IN_TOTAL = 5136
OFF_SBQ, OFF_SBK, OFF_SBV = 0, 512, 1024
OFF_G = 1536
OFF_GSB, OFF_GGLA = 3088, 4112
DFF = 2816
NF = 22
EPS = 1e-6
N_CORES = 8


class Res:
    __slots__ = ("name", "last_write", "reads", "dma_sem", "dma_cnt", "parent", "children")

    def __init__(self, name="", parent=None):
        self.name = name
        self.last_write = None
        self.reads = {}
        self.dma_sem = None
        self.dma_cnt = 0
        self.parent = parent
        self.children = []
        if parent is not None:
            parent.children.append(self)

    def related(self):
        out = [self]
        if self.parent is not None:
            out.append(self.parent)
        out.extend(self.children)
        return out


class Sched:
    def __init__(self, nc, ctx):
        self.nc = nc
        self.ctx = ctx
        self.engs = {"pe": nc.tensor, "act": nc.scalar, "dve": nc.vector,
                     "pool": nc.gpsimd, "sp": nc.sync}
        self.sems = {}
        self.cnt = {}
        self.semobj = {}
        for k in ("pe", "act", "dve", "pool"):
            self.sems[k] = ctx.enter_context(nc.semaphore("s_" + k))
            self.cnt[k] = 0
            self.semobj[k] = self.sems[k]
        self.waited = {}
        self.pending = {k: False for k in self.cnt}
        self.dma_total = {}
        self.n_dma_sems = 0
        self.n_wait = 0
        self.n_ins = 0

    def _dma_sem(self, r):
        if r.dma_sem is None:
            key = "d%d" % self.n_dma_sems
            self.n_dma_sems += 1
            self.semobj[key] = self.ctx.enter_context(self.nc.semaphore(key))
            self.dma_total[key] = 0
            r.dma_sem = key
        return r.dma_sem

    def share_dma_sem(self, r_from, r_to):
        r_to.dma_sem = self._dma_sem(r_from)

    def _wait(self, eng, deps):
        e = self.engs[eng]
        for (sk, val) in deps:
            if self.waited.get((eng, sk), 0) >= val:
                continue
            e.wait_ge(self.semobj[sk], val)
            self.waited[(eng, sk)] = val
            self.n_wait += 1

    def _deps(self, eng, reads, writes):
        deps = {}

        def add(ev, kind):
            if ev is None:
                return
            sk, val = ev
            if sk == eng and (eng == "pe" or kind == "war"):
                return
            if deps.get(sk, 0) < val:
                deps[sk] = val
        for r0 in reads:
            for r in r0.related():
                add(r.last_write, "raw")
        for w0 in writes:
            for w in w0.related():
                add(w.last_write, "waw")
                for sk, val in w.reads.items():
                    add((sk, val), "war")
        return list(deps.items())

    def begin_region(self):
        self.region = []

    def end_region(self):
        ops, self.region = self.region, None
        n = len(ops)
        last_w, readers = {}, {}
        preds = [set() for _ in range(n)]
        for i, (eng, fn, reads, writes, signal, cost) in enumerate(ops):
            for r0 in reads:
                for r in r0.related():
                    if id(r) in last_w:
                        preds[i].add(last_w[id(r)])
            for w0 in writes:
                for w in w0.related():
                    if id(w) in last_w:
                        preds[i].add(last_w[id(w)])
                    preds[i].update(readers.get(id(w), ()))
            for r0 in reads:
                readers.setdefault(id(r0), []).append(i)
            for w0 in writes:
                last_w[id(w0)] = i
                readers[id(w0)] = []
            preds[i].discard(i)
        succs = [[] for _ in range(n)]
        npred = [len(p) for p in preds]
        for i, p in enumerate(preds):
            for j in p:
                succs[j].append(i)
        eng_free = {}
        fin = [0.0] * n
        ready = [i for i in range(n) if npred[i] == 0]
        LAT = 0.15
        while ready:
            best, best_t = None, None
            for i in ready:
                eng = ops[i][0]
                t = eng_free.get(eng, 0.0)
                for j in preds[i]:
                    tj = fin[j] + (LAT if ops[j][0] != eng else 0.0)
                    if tj > t:
                        t = tj
                if best is None or t < best_t - 1e-9 or (abs(t - best_t) <= 1e-9 and i < best):
                    best, best_t = i, t
            ready.remove(best)
            eng, fn, reads, writes, signal, cost = ops[best]
            fin[best] = best_t + cost
            if isinstance(eng, tuple):
                fn(reads, writes)
            else:
                eng_free[eng] = fin[best]
                self.op(eng, fn, reads=reads, writes=writes, signal=signal)
            for k in succs[best]:
                npred[k] -= 1
                if npred[k] == 0:
                    ready.append(k)

    def op(self, eng, fn, reads=(), writes=(), signal=True, cost=None):
        if getattr(self, "region", None) is not None:
            if cost is None:
                cost = {"pe": 0.25, "act": 0.6, "dve": 0.5, "pool": 1.0}[eng]
            self.region.append((eng, fn, tuple(reads), tuple(writes), signal, cost))
            return None
        self._wait(eng, self._deps(eng, reads, writes))
        ins = fn()
        self.n_ins += 1
        if signal:
            self.cnt[eng] += 1
            ins.then_inc(self.sems[eng], 1)
            self.pending[eng] = False
            val = self.cnt[eng]
        else:
            self.pending[eng] = True
            val = self.cnt[eng] + 1
        for r in reads:
            if r.reads.get(eng, 0) < val:
                r.reads[eng] = val
        for w in writes:
            w.last_write = (eng, val)
            w.reads = {}
        return ins

    def dma(self, q, out, in_, reads=(), writes=(), **kw):
        if getattr(self, "region", None) is not None:
            def emit(reads_, writes_, q=q, out=out, in_=in_, kw=kw):
                reg, self.region = self.region, None
                self.dma(q, out, in_, reads=reads_, writes=writes_, **kw)
                self.region = reg
            self.region.append((("dma", q, len(self.region)), emit, tuple(reads), tuple(writes), True, 3.0))
            return None
        anchor = writes[0] if writes else reads[0]
        sk = self._dma_sem(anchor)
        self._wait(q, [d for d in self._deps("dma", reads, writes) if d[0] != sk])
        ins = self.engs[q].dma_start(out=out, in_=in_, **kw)
        ins.then_inc(self.semobj[sk], 16)
        self.dma_total[sk] += 16
        val = self.dma_total[sk]
        self.n_ins += 1
        for r in reads:
            if r.reads.get(sk, 0) < val:
                r.reads[sk] = val
        for w in writes:
            w.last_write = (sk, val)
            w.reads = {}
        return ins

    def barrier(self):
        for k, p in self.pending.items():
            assert not p, "unsignaled instruction pending on " + k
        evs = [(k, v) for k, v in self.cnt.items() if v > 0]
        evs += [(k, v) for k, v in self.dma_total.items() if v > 0]
        for eng in ("pe", "act", "dve", "pool", "sp"):
            self._wait(eng, evs)


def build_nc(S, NB):
    NT = S // 128
    NG = S // 512
    assert S % 512 == 0
    nc = bass.Bass("TRN2", target_bir_lowering=False)

    def din(name, shape):
        return nc.dram_tensor(name, shape, F32, kind="ExternalInput").ap()

    x = din("x", [NB, S, D])
    norm_mix_g = din("norm_mix_g", [D])
    w_in = din("w_in", [D, IN_TOTAL])
    b_gate = din("b_gate", [2, D])
    w_alpha_up = din("w_alpha_up", [16, 256])
    b_alpha = din("b_alpha", [256])
    gla_norm_g = din("gla_norm_g", [512])
    w_branch_sb = din("w_branch_sb", [512, D])
    w_branch_gla = din("w_branch_gla", [512, D])
    w_out = din("w_out", [D, D])
    norm_ffn_g = din("norm_ffn_g", [D])
    w_ffn_in = din("w_ffn_in", [D, 2 * DFF])
    conv_w = din("conv_w", [3, DFF])
    conv_b = din("conv_b", [DFF])
    w_ffn_out = din("w_ffn_out", [DFF, D])
    norm_final_g = din("norm_final_g", [D])
    out = nc.dram_tensor("out", [NB, S, D], F32, kind="ExternalOutput").ap()
    hscr = nc.dram_tensor("hscr", [S, D], F32, kind="Internal").ap()

    w_in_v = w_in.rearrange("(kc p) n -> p kc n", p=128)
    wbsb_v = w_branch_sb.rearrange("(kc p) n -> p kc n", p=128)
    wbgl_v = w_branch_gla.rearrange("(kc p) n -> p kc n", p=128)
    w_out_v = w_out.rearrange("(kc p) n -> p kc n", p=128)
    wfi_v = w_ffn_in.rearrange("(kc p) n -> p kc n", p=128)
    wfo_v = w_ffn_out.rearrange("(f p) n -> p f n", p=128)

    with ExitStack() as ctx:
        sch = Sched(nc, ctx)
        op = sch.op

        uid = [0]

        def sbt(c, name, shape, dt):
            uid[0] += 1
            return c.enter_context(nc.sbuf_tensor("%s_%d" % (name, uid[0]), shape, dt))

        PA = ctx.enter_context(nc.psum_tensor("PA", [128, 1024], F32))
        PB = ctx.enter_context(nc.psum_tensor("PB", [128, 1024], F32))
        PCD = ctx.enter_context(nc.psum_tensor("PCD", [128, 1024], F32))
        PC = PCD[:, 0:512]
        PD = PCD[:, 512:1024]
        PE_ = ctx.enter_context(nc.psum_tensor("PE", [128, 512], F32))
        PT = ctx.enter_context(nc.psum_tensor("PT", [128, 8, 128], BF16))
        rPA0, rPA1, rPB0, rPB1 = Res("PA0"), Res("PA1"), Res("PB0"), Res("PB1")
        rPC, rPD, rPE, rPT = Res("PC"), Res("PD"), Res("PE"), Res("PT")
        banks5 = [(PA[:, 0:512], rPA0), (PA[:, 512:1024], rPA1), (PB[:, 0:512], rPB0),
                  (PB[:, 512:1024], rPB1), (PC[:, :], rPC), (PD[:, :], rPD), (PE_[:, :], rPE)]

        cst = ctx
        idf = sbt(cst, "idf", [128, 128], F32); r_idf = Res()
        ident = sbt(cst, "ident", [128, 128], BF16); r_ident = Res()
        trineg = sbt(cst, "trineg", [128, 128], BF16); r_trineg = Res()
        onesneg = sbt(cst, "onesneg", [128, 128], BF16); r_onesneg = Res()
        mstrict = sbt(cst, "mstrict", [128, 128], BF16); r_mstrict = Res()
        mnegbig = sbt(cst, "mnegbig", [128, 128], BF16); r_mnegbig = Res()
        zeros = sbt(cst, "zeros", [128, 512], BF16); r_zeros = Res()
        ust = sbt(cst, "ust", [128, 128], F32); r_ust = Res()
        tincl = sbt(cst, "tincl", [128, 128], F32); r_tincl = Res()
        ones32 = sbt(cst, "ones32", [128, 128], F32); r_ones32 = Res()
        tmpc = sbt(cst, "tmpc", [128, 128], F32); r_tmpc = Res()
        vst = sbt(cst, "vst", [128, 128], F32); r_vst = Res()
        vecs = sbt(cst, "vecs", [128, 128], F32); r_vecs = Res()
        gfin_bc = sbt(cst, "gfin_bc", [128, D], F32); r_gfin = Res()
        ggla_bc = sbt(cst, "ggla_bc", [128, 512], F32); r_ggla = Res()
        balpha = sbt(cst, "balpha", [1, 256], F32); r_balpha = Res()
        walpha = sbt(cst, "walpha", [16, 256], F32); r_walpha = Res()

        def gp(fn, reads=(), writes=()):
            return op("pool", fn, reads=reads, writes=writes)

        def aff(t, cmp, fill, cm=1, pat=-1, base=0):
            return lambda: nc.gpsimd.affine_select(out=t[:], in_=t[:], pattern=[[pat, 128]], compare_op=cmp,
                                                   fill=fill, base=base, channel_multiplier=cm)
        gp(lambda: nc.gpsimd.memset(idf[:], 1.0), writes=[r_idf])
        gp(aff(idf, ALU.is_equal, 0.0), reads=[r_idf], writes=[r_idf])
        op("dve", lambda: nc.vector.tensor_copy(out=ident[:], in_=idf[:]), reads=[r_idf], writes=[r_ident])
        gp(lambda: nc.gpsimd.memset(tmpc[:], -1.0), writes=[r_tmpc])
        gp(aff(tmpc, ALU.is_ge, 0.0), reads=[r_tmpc], writes=[r_tmpc])
        op("dve", lambda: nc.vector.tensor_copy(out=trineg[:], in_=tmpc[:]), reads=[r_tmpc], writes=[r_trineg])
        gp(lambda: nc.gpsimd.memset(tmpc[:], 1.0), reads=[r_tmpc], writes=[r_tmpc])
        gp(aff(tmpc, ALU.is_gt, 0.0, cm=-1, pat=1), reads=[r_tmpc], writes=[r_tmpc])
        op("dve", lambda: nc.vector.tensor_copy(out=mstrict[:], in_=tmpc[:]), reads=[r_tmpc], writes=[r_mstrict])
        gp(lambda: nc.gpsimd.memset(tmpc[:], 0.0), reads=[r_tmpc], writes=[r_tmpc])
        gp(aff(tmpc, ALU.is_gt, -30000.0, cm=-1, pat=1), reads=[r_tmpc], writes=[r_tmpc])
        op("dve", lambda: nc.vector.tensor_copy(out=mnegbig[:], in_=tmpc[:]), reads=[r_tmpc], writes=[r_mnegbig])
        gp(lambda: nc.gpsimd.memset(tmpc[:], -1.0), reads=[r_tmpc], writes=[r_tmpc])
        op("dve", lambda: nc.vector.tensor_copy(out=onesneg[:], in_=tmpc[:]), reads=[r_tmpc], writes=[r_onesneg])
        gp(lambda: nc.gpsimd.memset(zeros[:], 0.0), writes=[r_zeros])
        gp(lambda: nc.gpsimd.memset(ust[:], 1.0), writes=[r_ust])
        gp(aff(ust, ALU.is_gt, 0.0), reads=[r_ust], writes=[r_ust])
        gp(lambda: nc.gpsimd.memset(tincl[:], 1.0), writes=[r_tincl])
        gp(aff(tincl, ALU.is_ge, 0.0, cm=-1, pat=1), reads=[r_tincl], writes=[r_tincl])
        gp(lambda: nc.gpsimd.memset(ones32[:], 1.0), writes=[r_ones32])
        gp(lambda: nc.gpsimd.memset(vst[:], 0.0), writes=[r_vst])
        sch.dma("sp", vst[0:8, :], norm_mix_g.rearrange("(k p) -> k p", p=128), reads=[r_vst], writes=[r_vst])
        sch.dma("sp", vst[8:16, :], norm_ffn_g.rearrange("(k p) -> k p", p=128), writes=[r_vst])
        sch.dma("sp", vst[16:32, :], b_gate.rearrange("j (k p) -> (j k) p", p=128), writes=[r_vst])
        sch.dma("sp", vst[32:98, :], conv_w.rearrange("i (f p) -> (i f) p", p=128), writes=[r_vst])
        sch.dma("sp", vst[98:120, :], conv_b.rearrange("(f p) -> f p", p=128), writes=[r_vst])
        op("pe", lambda: nc.tensor.matmul(PC[:, 0:128], lhsT=vst[:, :], rhs=idf[:, :], start=True, stop=True),
           reads=[r_vst, r_idf], writes=[rPC])
        op("dve", lambda: nc.vector.tensor_copy(out=vecs[:], in_=PC[:, 0:128]), reads=[rPC], writes=[r_vecs])
        gmix = vecs[:, 0:8]
        gffn = vecs[:, 8:16]

        def bgate(j, n):
            return vecs[:, 16 + j * 8 + n:16 + j * 8 + n + 1]

        def convw(i, f):
            return vecs[:, 32 + i * NF + f:32 + i * NF + f + 1]

        def convb(f):
            return vecs[:, 98 + f:99 + f]
        sch.dma("sp", gfin_bc[:], norm_final_g.partition_broadcast(128), writes=[r_gfin])
        sch.dma("sp", ggla_bc[:], gla_norm_g.partition_broadcast(128), writes=[r_ggla])
        sch.dma("sp", balpha[:], b_alpha.rearrange("(o n) -> o n", o=1), writes=[r_balpha])
        sch.dma("sp", walpha[:], w_alpha_up, writes=[r_walpha])
        ones_bf = sbt(cst, "ones_bf", [128, 128], BF16); r_ones_bf = Res()
        ust_bf = sbt(cst, "ust_bf", [128, 128], BF16); r_ust_bf = Res()
        tincl_bf = sbt(cst, "tincl_bf", [128, 128], BF16); r_tincl_bf = Res()
        wa_hi = sbt(cst, "wa_hi", [16, 256], BF16); wa_lo = sbt(cst, "wa_lo", [16, 256], BF16); r_wahl = Res()
        ba_hi = sbt(cst, "ba_hi", [1, 256], BF16); ba_lo = sbt(cst, "ba_lo", [1, 256], BF16); r_bahl = Res()
        op("dve", lambda: nc.vector.tensor_copy(out=ones_bf[:], in_=ones32[:]), reads=[r_ones32], writes=[r_ones_bf])
        op("dve", lambda: nc.vector.tensor_copy(out=ust_bf[:], in_=ust[:]), reads=[r_ust], writes=[r_ust_bf])
        op("dve", lambda: nc.vector.tensor_copy(out=tincl_bf[:], in_=tincl[:]), reads=[r_tincl], writes=[r_tincl_bf])
        op("dve", lambda: nc.vector.tensor_copy(out=wa_hi[:], in_=walpha[:]), reads=[r_walpha], writes=[r_wahl])
        op("dve", lambda: nc.vector.tensor_tensor(out=wa_lo[:], in0=walpha[:], in1=wa_hi[:], op=ALU.subtract),
           reads=[r_walpha, r_wahl], writes=[r_wahl])
        op("dve", lambda: nc.vector.tensor_copy(out=ba_hi[:], in_=balpha[:]), reads=[r_balpha], writes=[r_bahl])
        op("dve", lambda: nc.vector.tensor_tensor(out=ba_lo[:], in0=balpha[:], in1=ba_hi[:], op=ALU.subtract),
           reads=[r_balpha, r_bahl], writes=[r_bahl])

        xnT = sbt(ctx, "xnT", [128, KC, S], BF16); r_xnT = Res("xnT")
        xin = [sbt(ctx, "xin%d" % i, [128, D], F32) for i in range(3)]
        r_xin = [Res("xin%d" % i) for i in range(3)]
        xnb = [sbt(ctx, "xnb%d" % i, [128, D], BF16) for i in range(2)]
        r_xnb = [Res() for _ in range(2)]
        junk = sbt(ctx, "junk", [128, D], BF16); r_junk = Res()
        stt = [sbt(ctx, "stt%d" % i, [128, 8], F32) for i in range(2)]
        r_stt = [Res() for _ in range(2)]
        wgs0 = sbt(ctx, "wgs_p0", [128, KC, 128], BF16); wgg0 = sbt(ctx, "wgg_p0", [128, KC, 128], BF16)
        wsb0 = sbt(ctx, "wsb_p0", [128, 4, 128], BF16); wgl0 = sbt(ctx, "wgl_p0", [128, 4, 128], BF16)
        r_wm0 = Res("wm0")
        wa0 = sbt(ctx, "wa_p0", [128, KC, 128], BF16); wgt0 = sbt(ctx, "wgt_p0", [128, KC, 128], BF16)
        r_wf0 = Res("wf0")

        wgA = sbt(ctx, "wgA_p", [128, KC, 512], BF16); r_wgA = Res("wgA")

        def prefetch_gla_w0():
            sch.dma("pool", wgA[:], w_in_v[:, :, OFF_G:OFF_G + 512], writes=[r_wgA])

        def prefetch_merge_w0():
            sch.dma("pool", wgs0[:], w_in_v[:, :, OFF_GSB:OFF_GSB + 128], writes=[r_wm0])
            sch.dma("pool", wgg0[:], w_in_v[:, :, OFF_GGLA:OFF_GGLA + 128], writes=[r_wm0])
            sch.dma("pool", wsb0[:], wbsb_v[:, :, 0:128], writes=[r_wm0])
            sch.dma("pool", wgl0[:], wbgl_v[:, :, 0:128], writes=[r_wm0])

        def prefetch_ffn_w0():
            sch.dma("pool", wa0[:], wfi_v[:, :, 0:128], writes=[r_wf0])
            sch.dma("pool", wgt0[:], wfi_v[:, :, DFF:DFF + 128], writes=[r_wf0])
        r_hscr = [Res("hscr%d" % t) for t in range(NT)]
        r_out = Res("out")

        def rms_stats(src_ap, r_src, st, r_st, n, width):
            op("act", lambda: nc.scalar.activation(out=junk[:, 0:width], in_=src_ap, func=AF.Square,
                                                   accum_out=st[:, 0:1]),
               reads=[r_src], writes=[r_junk, r_st])
            op("act", lambda: nc.scalar.activation(out=st[:, 1:2], in_=st[:, 0:1], func=AF.Ln,
                                                   scale=1.0 / n, bias=EPS), reads=[r_st], writes=[r_st])
            op("act", lambda: nc.scalar.activation(out=st[:, 2:3], in_=st[:, 1:2], func=AF.Exp, scale=-0.5),
               reads=[r_st], writes=[r_st])

        def norm_pre(src, r_src, slot):
            st, r_st = stt[slot], r_stt[slot]
            rms_stats(src[:], r_src, st, r_st, D, D)
            nb, r_nb = xnb[slot], r_xnb[slot]
            op("dve", lambda: nc.vector.tensor_scalar(out=nb[:], in0=src[:], scalar1=st[:, 2:3], scalar2=None,
                                                      op0=ALU.mult), reads=[r_src, r_st], writes=[r_nb])

        def norm_post(slot, gcols, dstT, r_dst, tt):
            nb, r_nb = xnb[slot], r_xnb[slot]
            for kc in range(KC):
                op("pe", lambda kc=kc: nc.tensor.transpose(PT[:, kc, :], nb[:, kc * 128:(kc + 1) * 128], ident[:]),
                   reads=[r_nb, r_ident], writes=[rPT], signal=(kc == KC - 1))
            op("dve", lambda: nc.vector.tensor_tensor(out=dstT[:, :, tt * 128:(tt + 1) * 128], in0=PT[:],
                                                      in1=gcols.unsqueeze(2).broadcast_to([128, KC, 128]),
                                                      op=ALU.mult),
               reads=[rPT, r_vecs], writes=[r_dst])

        def proj_fm(bank, rbank, w_tile, r_w, wcols, srcT, r_src, nk, tcols, mrows=128):
            for kc in range(nk):
                op("pe", lambda kc=kc: nc.tensor.matmul(bank[0:mrows, :], lhsT=w_tile[:, kc, wcols],
                                                        rhs=srcT[:, kc, tcols], start=(kc == 0), stop=(kc == nk - 1)),
                   reads=[r_w, r_src], writes=[rbank], signal=(kc == nk - 1))

        def proj_tm(bank_ap, rbank, srcT, r_src, tcols, w_tile, r_w, wcols, nk):
            for kc in range(nk):
                op("pe", lambda kc=kc: nc.tensor.matmul(bank_ap, lhsT=srcT[:, kc, tcols], rhs=w_tile[:, kc, wcols],
                                                        start=(kc == 0), stop=(kc == nk - 1)),
                   reads=[r_w, r_src], writes=[rbank], signal=(kc == nk - 1))

        for b in range(NB):
            sch.begin_region()
            for tt in range(NT):
                sl = tt % 2
                xs = tt % 3
                if not (b > 0 and tt < 2):
                    sch.dma("sp", xin[xs][:], x[b, tt * 128:(tt + 1) * 128, :], writes=[r_xin[xs]])
                norm_pre(xin[xs], r_xin[xs], sl)
                if tt > 0:
                    norm_post(1 - sl, gmix, xnT, r_xnT, tt - 1)
            norm_post((NT - 1) % 2, gmix, xnT, r_xnT, NT - 1)
            sch.end_region()
            with ExitStack() as mix:
                osbT = sbt(mix, "osbT", [128, 4, S], BF16); r_osbT = Res("osbT")
                oglT = sbt(mix, "oglT", [128, 4, S], BF16); r_oglT = Res("oglT")
                with ExitStack() as ph:
                    wv = sbt(ph, "wv", [128, KC, 512], BF16); r_wv = Res()
                    wq = [sbt(ph, "wq%d" % i, [128, KC, 128], BF16) for i in range(2)]
                    wk = [sbt(ph, "wk%d" % i, [128, KC, 128], BF16) for i in range(2)]
                    r_wq = [Res() for _ in range(2)]; r_wk = [Res() for _ in range(2)]
                    qT = [sbt(ph, "qT%d" % i, [128, S], BF16) for i in range(2)]
                    kT = [sbt(ph, "kT%d" % i, [128, S], BF16) for i in range(2)]
                    r_qT = [Res() for _ in range(2)]; r_kT = [Res() for _ in range(2)]
                    vtok = sbt(ph, "vtok", [128, NT, 512], BF16); r_vtok = Res()
                    KB = 16
                    SPK = [sbt(ph, "SPK%d" % i, [128, 2, 512], BF16) for i in range(KB)]
                    r_SPK = [Res() for _ in range(KB)]
                    W2 = [sbt(ph, "W2%d" % i, [128, 2, 512], BF16) for i in range(2)]
                    r_W = [Res() for _ in range(2)]
                    lacc = sbt(ph, "lacc", [128, 2, 512], BF16); r_lacc = Res()

                    sch.dma("pool", wq[0][:], w_in_v[:, :, OFF_SBQ:OFF_SBQ + 128], writes=[r_wq[0]])
                    sch.dma("pool", wk[0][:], w_in_v[:, :, OFF_SBK:OFF_SBK + 128], writes=[r_wk[0]])
                    sch.dma("pool", wv[:], w_in_v[:, :, OFF_SBV:OFF_SBV + 512], writes=[r_wv])
                    PS = [PA[:, :].rearrange("p (h n) -> p h n", h=2), PCD[:, :].rearrange("p (h n) -> p h n", h=2)]
                    rPS = [[rPA0, rPA1], [rPC, rPD]]
                    accb, racc = PE_, rPE
                    mstrict2 = mstrict[:].unsqueeze(1).broadcast_to([128, 2, 128])

                    def qk_proj(p, sl):
                        for g in range(NG):
                            tc_ = slice(g * 512, (g + 1) * 512)
                            bk, rb = banks5[2]
                            for kc in range(KC):
                                op("pe", lambda kc=kc, bk=bk, tc_=tc_: nc.tensor.matmul(bk, lhsT=wq[sl][:, kc, :], rhs=xnT[:, kc, tc_],
                                                                  start=(kc == 0), stop=(kc == KC - 1)),
                                   reads=[r_wq[sl], r_xnT], writes=[rb], signal=(kc == KC - 1))
                                if kc % 4 == 3:
                                    yield
                            op("dve", lambda bk=bk, tc_=tc_: nc.vector.tensor_scalar(out=qT[sl][:, tc_], in0=bk, scalar1=0.125,
                                                                                     scalar2=None, op0=ALU.mult),
                               reads=[rb], writes=[r_qT[sl]], cost=0.7)
                            yield
                            bk, rb = banks5[3]
                            for kc in range(KC):
                                op("pe", lambda kc=kc, bk=bk, tc_=tc_: nc.tensor.matmul(bk, lhsT=wk[sl][:, kc, :], rhs=xnT[:, kc, tc_],
                                                                  start=(kc == 0), stop=(kc == KC - 1)),
                                   reads=[r_wk[sl], r_xnT], writes=[rb], signal=(kc == KC - 1))
                                if kc % 4 == 3:
                                    yield
                            op("dve", lambda bk=bk, tc_=tc_: nc.vector.tensor_copy(out=kT[sl][:, tc_], in_=bk),
                               reads=[rb], writes=[r_kT[sl]])
                            yield

                    for _ in qk_proj(0, 0):
                        pass
                    for tt in range(NT):
                        bk, rb = banks5[4 + tt % 2]
                        proj_tm(bk, rb, xnT, r_xnT, slice(tt * 128, (tt + 1) * 128), wv, r_wv, slice(0, 512), KC)
                        if tt % 2 == 0:
                            op("act", lambda bk=bk, tt=tt: nc.scalar.copy(out=vtok[:, tt, :], in_=bk),
                               reads=[rb], writes=[r_vtok])
                        else:
                            op("dve", lambda bk=bk, tt=tt: nc.vector.tensor_copy(out=vtok[:, tt, :], in_=bk),
                               reads=[rb], writes=[r_vtok])
                    for p in range(4):
                        sl = p % 2
                        nxt = None
                        if p + 1 < 4:
                            sch.dma("pool", wq[1 - sl][:], w_in_v[:, :, OFF_SBQ + (p + 1) * 128:OFF_SBQ + (p + 2) * 128],
                                    writes=[r_wq[1 - sl]])
                            sch.dma("pool", wk[1 - sl][:], w_in_v[:, :, OFF_SBK + (p + 1) * 128:OFF_SBK + (p + 2) * 128],
                                    writes=[r_wk[1 - sl]])
                            nxt = qk_proj(p + 1, 1 - sl)

                        def tick():
                            nonlocal nxt
                            if nxt is not None:
                                try:
                                    next(nxt)
                                except StopIteration:
                                    nxt = None
                        units = []
                        for qg in range(NG):
                            lst = [(4 * qg + i, 128 * i) for i in (3, 2, 1, 0)]
                            lst += [(kb, 0) for kb in range(4 * qg - 1, -1, -1)]
                            for ui, (kb, c0) in enumerate(lst):
                                units.append(dict(qg=qg, kb=kb, c0=c0, first=(ui == 0),
                                                  last=(ui == len(lst) - 1), diag=(kb >= 4 * qg)))
                        nU = len(units)

                        def qk_mm(u, dst, rdst, stop):
                            c0 = u["c0"]
                            qc = slice(u["qg"] * 512 + c0, u["qg"] * 512 + 512)
                            kc_ = slice(u["kb"] * 128, u["kb"] * 128 + 128)
                            for hh in range(2):
                                R = slice(hh * 64, hh * 64 + 64)
                                op("pe", lambda hh=hh, R=R: nc.tensor.matmul(dst[:, hh, c0:512], lhsT=kT[sl][R, kc_],
                                                                  rhs=qT[sl][R, qc], start=True, stop=stop),
                                   reads=[r_kT[sl], r_qT[sl]], writes=[rdst[hh]], signal=(stop and hh == 1))

                        def stB(i, j):
                            u = units[i]
                            c0 = u["c0"]
                            SP = SPK[j]
                            op("act", lambda: nc.scalar.activation(out=SP[:, :, c0:512], in_=PS[j % 2][:, :, c0:512],
                                                                   func=AF.Softplus),
                               reads=rPS[j % 2], writes=[r_SPK[j]], cost=1.05)
                            if u["diag"]:
                                op("dve", lambda: nc.vector.tensor_tensor(out=SP[:, :, c0:c0 + 128],
                                                                          in0=SP[:, :, c0:c0 + 128], in1=mstrict2,
                                                                          op=ALU.mult),
                                   reads=[r_SPK[j], r_mstrict], writes=[r_SPK[j]])

                        def stC(i, j):
                            u = units[i]
                            c0 = u["c0"]
                            SP = SPK[j]
                            lw, rlw = PS[j % 2], rPS[j % 2]
                            if u["first"]:
                                op("pool", lambda: nc.gpsimd.memset(lacc[:], 0.0), writes=[r_lacc])
                            qk_mm(u, lw, rlw, False)
                            for hh in range(2):
                                op("pe", lambda hh=hh: nc.tensor.matmul(lw[:, hh, c0:512], lhsT=trineg[:], rhs=SP[:, hh, c0:512],
                                                                  start=False, stop=False),
                                   reads=[r_trineg, r_SPK[j]], writes=[rlw[hh]], signal=False)
                                if u["diag"]:
                                    op("pe", lambda hh=hh: nc.tensor.matmul(lw[:, hh, c0:c0 + 128], lhsT=ident[:], rhs=mnegbig[:],
                                                                      start=False, stop=False),
                                       reads=[r_ident, r_mnegbig], writes=[rlw[hh]], signal=False)
                                cz = c0 + 128 if (u["diag"] and c0 < 384) else c0
                                op("pe", lambda hh=hh, cz=cz: nc.tensor.matmul(lw[:, hh, cz:512], lhsT=onesneg[:],
                                                                         rhs=lacc[:, hh, cz:512], start=False, stop=True),
                                   reads=[r_onesneg, r_lacc], writes=[rlw[hh]], signal=(hh == 1))
                            if not u["last"]:
                                op("dve", lambda: nc.vector.tensor_tensor(out=lacc[:, :, c0:512], in0=lacc[:, :, c0:512],
                                                                          in1=SP[:, :, c0:512], op=ALU.add),
                                   reads=[r_lacc, r_SPK[j]], writes=[r_lacc], cost=0.7)

                        def stD(i, j):
                            u = units[i]
                            c0 = u["c0"]
                            W = W2[j % 2]
                            op("act", lambda: nc.scalar.activation(out=W[:, :, c0:512], in_=PS[j % 2][:, :, c0:512],
                                                                   func=AF.Exp),
                               reads=rPS[j % 2], writes=[r_W[j % 2]], cost=1.05)

                        def stE(i, j):
                            u = units[i]
                            c0 = u["c0"]
                            W = W2[j % 2]
                            kb = u["kb"]
                            if u["first"]:
                                op("pe", lambda: nc.tensor.matmul(accb[:, :], lhsT=zeros[:, 0:128], rhs=zeros[:, :],
                                                                  start=True, stop=False),
                                   reads=[r_zeros], writes=[racc], signal=False)
                            for hh in range(2):
                                R = slice(hh * 64, hh * 64 + 64)
                                hcol = slice((2 * p + hh) * 64, (2 * p + hh + 1) * 64)
                                op("pe", lambda hh=hh, R=R, hcol=hcol: nc.tensor.matmul(accb[R, c0:512], lhsT=vtok[:, kb, hcol],
                                                                  rhs=W[:, hh, c0:512], start=False, stop=u["last"]),
                                   reads=[r_vtok, r_W[j % 2]], writes=[racc], signal=(hh == 1))
                            if u["last"]:
                                qg = u["qg"]
                                op("dve", lambda: nc.vector.tensor_copy(out=osbT[:, p, qg * 512:(qg + 1) * 512],
                                                                        in_=accb[:, :]),
                                   reads=[racc], writes=[r_osbT])

                        for u0 in range(0, nU, KB):
                            kk = min(KB, nU - u0)
                            sch.begin_region()
                            for j in range(kk + 1):
                                if j < kk:
                                    qk_mm(units[u0 + j], PS[j % 2], rPS[j % 2], True)
                                if j >= 1:
                                    stB(u0 + j - 1, j - 1)
                                tick()
                                tick()
                            sch.end_region()
                            sch.begin_region()
                            for j in range(kk + 1):
                                if j < kk:
                                    stC(u0 + j, j)
                                if j >= 1:
                                    stD(u0 + j - 1, j - 1)
                                    stE(u0 + j - 1, j - 1)
                            sch.end_region()
                        while nxt is not None:
                            tick()
                    prefetch_gla_w0()
                    sch.barrier()
                with ExitStack() as ph:
                    wg = sbt(ph, "wg", [128, KC, 1040], BF16); r_wg = Res(); r_wgv = Res(); r_wgr = Res()
                    qTg = sbt(ph, "qTg", [128, 2, S], BF16); r_qTg = Res()
                    kTg = sbt(ph, "kTg", [128, 2, S], BF16); r_kTg = Res()
                    ga_hi = sbt(ph, "ga_hi", [16, S], BF16); ga_lo = sbt(ph, "ga_lo", [16, S], BF16); r_gaT = Res()
                    sp_hi, r_sp_hi = None, None
                    def dbl(name, shape, dt):
                        return [sbt(ph, name + str(i), shape, dt) for i in range(2)], [Res(name + str(i)) for i in range(2)]
                    E1, r_E1 = dbl("E1", [128, 256], F32)
                    SPa, r_SPa = dbl("SPa", [128, 256], F32)
                    SPh, r_SPh = dbl("SPh", [128, 256], BF16)
                    SPl, r_SPl = dbl("SPl", [128, 256], BF16)
                    Dend, r_Dend = dbl("Dend", [128, 256], F32)
                    kend, r_kend = dbl("kend", [128, 256], BF16)
                    Eq, r_Eq = dbl("Eq", [128, 2, 128], F32)
                    Ek, r_Ek = dbl("Ek", [128, 2, 128], F32)
                    qdec, r_qdec = dbl("qdec", [128, 2, 128], BF16)
                    kinv, r_kinv = dbl("kinv", [128, 2, 128], BF16)
                    vbf, r_vbf = dbl("vbf", [128, 512], BF16)
                    er, r_er = dbl("er", [128, 512], F32)
                    gr, r_gr = dbl("gr", [128, 512], F32)
                    attm, _ = dbl("attm", [128, 4, 128], BF16)
                    r_attm = [[Res() for _ in range(4)] for _ in range(2)]
                    og, r_og = dbl("og", [128, 512], BF16)
                    gst, r_gst = dbl("gst", [128, 12], F32)
                    S32 = sbt(ph, "S32", [128, 2, 128], F32); r_S32 = Res()
                    Sbf = sbt(ph, "Sbf", [128, 2, 128], BF16); r_Sbf = Res()
                    sch.dma("pool", wg[:, :, 1024:1040], w_in_v[:, :, OFF_G + 1536:OFF_G + 1552], writes=[r_wg])
                    sch.dma("pool", wg[:, :, 0:512], w_in_v[:, :, OFF_G + 512:OFF_G + 1024], writes=[r_wgv])
                    sch.dma("pool", wg[:, :, 512:1024], w_in_v[:, :, OFF_G + 1024:OFF_G + 1536], writes=[r_wgr])
                    for g in range(NG):
                        tc_ = slice(g * 512, (g + 1) * 512)
                        for c2 in range(2):
                            bk, rb = banks5[(2 * c2) % 7]
                            proj_fm(bk, rb, wgA, r_wgA, slice(c2 * 128, (c2 + 1) * 128), xnT, r_xnT, KC, tc_)
                            op("act", lambda bk=bk, c2=c2: nc.scalar.mul(out=qTg[:, c2, tc_], in_=bk, mul=0.125),
                               reads=[rb], writes=[r_qTg])
                            bk, rb = banks5[(2 * c2 + 1) % 7]
                            proj_fm(bk, rb, wgA, r_wgA, slice(256 + c2 * 128, 256 + (c2 + 1) * 128), xnT, r_xnT, KC, tc_)
                            op("dve", lambda bk=bk, c2=c2: nc.vector.tensor_copy(out=kTg[:, c2, tc_], in_=bk),
                               reads=[rb], writes=[r_kTg])
                        bk, rb = banks5[4]
                        proj_fm(bk, rb, wg, r_wg, slice(1024, 1040), xnT, r_xnT, KC, tc_, mrows=16)
                        op("act", lambda bk=bk: nc.scalar.copy(out=ga_hi[0:16, tc_], in_=bk[0:16, :]),
                           reads=[rb], writes=[r_gaT])
                        op("dve", lambda bk=bk: nc.vector.tensor_tensor(out=ga_lo[0:16, tc_], in0=bk[0:16, :],
                                                                        in1=ga_hi[0:16, tc_], op=ALU.subtract),
                           reads=[rb, r_gaT], writes=[r_gaT])
                    op("pool", lambda: nc.gpsimd.memset(S32[:], 0.0), writes=[r_S32])
                    op("pool", lambda: nc.gpsimd.memset(Sbf[:], 0.0), writes=[r_Sbf])
                    Pk, rPk = PA[:, 0:256], rPA0
                    Pa, rPa = PA[:, 256:512], rPA0
                    Pv, rPv = PA[:, 512:1024], rPA1
                    Pr, rPr = PB[:, 0:512], rPB0
                    Po, rPo = PB[:, 512:1024], rPB1
                    PR, rPR = PC[:, 0:256], rPC
                    Pcum, rPcum = PC[:, 256:512], rPC
                    PDS, rPDS = PD[:, 0:256], rPD
                    Patt = [PE_[:, 0:128], PD[:, 256:384], PE_[:, 128:256], PD[:, 384:512]]
                    rPatt = [rPE, rPD, rPE, rPD]

                    def gla_front(tt):
                        z = tt % 2
                        tc_ = slice(tt * 128, (tt + 1) * 128)
                        proj_tm(Pk, rPk, xnT, r_xnT, tc_, wgA, r_wgA, slice(256, 512), KC)
                        yield
                        for mi, (lh, rh) in enumerate(((ga_hi, wa_hi), (ga_lo, wa_hi), (ga_hi, wa_lo))):
                            op("pe", lambda lh=lh, rh=rh, mi=mi: nc.tensor.matmul(Pa, lhsT=lh[0:16, tc_], rhs=rh[0:16, :],
                                                                              start=(mi == 0), stop=False),
                               reads=[r_gaT, r_wahl], writes=[rPa], signal=False, cost=0.12)
                        for mi, bh in enumerate((ba_hi, ba_lo)):
                            op("pe", lambda bh=bh, mi=mi: nc.tensor.matmul(Pa, lhsT=ones_bf[0:1, :], rhs=bh[0:1, :],
                                                                       start=False, stop=(mi == 1)),
                               reads=[r_ones_bf, r_bahl], writes=[rPa], signal=(mi == 1), cost=0.12)
                        yield
                        op("act", lambda: nc.scalar.activation(out=E1[z][:], in_=Pa, func=AF.Exp, scale=-1.0),
                           reads=[rPa], writes=[r_E1[z]])
                        yield
                        op("act", lambda: nc.scalar.activation(out=SPa[z][:], in_=E1[z][:], func=AF.Ln, bias=1.0),
                           reads=[r_E1[z]], writes=[r_SPa[z]])
                        yield
                        proj_tm(Pv, rPv, xnT, r_xnT, tc_, wg, r_wgv, slice(0, 512), KC)
                        yield
                        proj_tm(Pr, rPr, xnT, r_xnT, tc_, wg, r_wgr, slice(512, 1024), KC)
                        yield
                        op("dve", lambda: nc.vector.tensor_copy(out=SPh[z][:], in_=SPa[z][:]),
                           reads=[r_SPa[z]], writes=[r_SPh[z]], cost=0.3)
                        op("dve", lambda: nc.vector.tensor_tensor(out=SPl[z][:], in0=SPa[z][:], in1=SPh[z][:],
                                                                  op=ALU.subtract),
                           reads=[r_SPa[z], r_SPh[z]], writes=[r_SPl[z]], cost=0.4)
                        for mi, (sp_, rsp_) in enumerate(((SPh, r_SPh), (SPl, r_SPl))):
                            op("pe", lambda sp_=sp_, mi=mi: nc.tensor.matmul(PR, lhsT=ust_bf[:], rhs=sp_[z][:],
                                                                          start=(mi == 0), stop=(mi == 1)),
                               reads=[r_ust_bf, rsp_[z]], writes=[rPR], signal=(mi == 1), cost=0.12)
                        yield
                        for c2 in range(2):
                            for mi, (sp_, rsp_) in enumerate(((SPh, r_SPh), (SPl, r_SPl))):
                                op("pe", lambda c2=c2, sp_=sp_, mi=mi: nc.tensor.matmul(
                                    Pcum[:, c2 * 128:(c2 + 1) * 128], lhsT=sp_[z][:, c2 * 128:(c2 + 1) * 128],
                                    rhs=tincl_bf[:], start=(mi == 0), stop=(mi == 1)),
                                   reads=[r_tincl_bf, rsp_[z]], writes=[rPcum], signal=(c2 == 1 and mi == 1), cost=0.08)
                            yield
                        op("act", lambda: nc.scalar.copy(out=vbf[z][:], in_=Pv), reads=[rPv], writes=[r_vbf[z]])
                        yield
                        op("act", lambda: nc.scalar.activation(out=er[z][:], in_=Pr, func=AF.Exp, scale=-1.0),
                           reads=[rPr], writes=[r_er[z]])
                        yield
                        op("act", lambda: nc.scalar.activation(out=Dend[z][:], in_=PR, func=AF.Exp, scale=-1.0 / 16),
                           reads=[rPR], writes=[r_Dend[z]])
                        yield
                        op("dve", lambda: nc.vector.tensor_tensor(out=kend[z][:], in0=Pk, in1=Dend[z][:], op=ALU.mult),
                           reads=[rPk, r_Dend[z]], writes=[r_kend[z]])
                        yield
                        op("act", lambda: nc.scalar.activation(out=Eq[z][:].rearrange("p a b -> p (a b)"), in_=Pcum,
                                                               func=AF.Exp, scale=-1.0 / 16),
                           reads=[rPcum], writes=[r_Eq[z]])
                        yield
                        op("act", lambda: nc.scalar.activation(out=Ek[z][:].rearrange("p a b -> p (a b)"), in_=Pcum,
                                                               func=AF.Exp, scale=1.0 / 16),
                           reads=[rPcum], writes=[r_Ek[z]])
                        yield
                        op("dve", lambda: nc.vector.tensor_tensor(out=qdec[z][:], in0=qTg[:, :, tc_], in1=Eq[z][:],
                                                                  op=ALU.mult),
                           reads=[r_qTg, r_Eq[z]], writes=[r_qdec[z]])
                        yield
                        op("dve", lambda: nc.vector.tensor_tensor(out=kinv[z][:], in0=kTg[:, :, tc_], in1=Ek[z][:],
                                                                  op=ALU.mult),
                           reads=[r_kTg, r_Ek[z]], writes=[r_kinv[z]])
                        yield
                        op("act", lambda: nc.scalar.activation(out=er[z][:], in_=er[z][:], func=AF.Ln, bias=1.0),
                           reads=[r_er[z]], writes=[r_er[z]])
                        yield
                        op("act", lambda: nc.scalar.activation(out=er[z][:], in_=er[z][:], func=AF.Exp, scale=-1.0),
                           reads=[r_er[z]], writes=[r_er[z]])
                        yield
                        op("dve", lambda: nc.vector.tensor_tensor(out=gr[z][:], in0=Pr, in1=er[z][:], op=ALU.mult),
                           reads=[rPr, r_er[z]], writes=[r_gr[z]])
                        yield
                        op("pool", lambda: nc.gpsimd.tensor_tensor(out=gr[z][:], in0=gr[z][:], in1=ggla_bc[:],
                                                                   op=ALU.mult),
                           reads=[r_gr[z], r_ggla], writes=[r_gr[z]])
                        yield

                    def gla_back(tt):
                        z = tt % 2
                        tc_ = slice(tt * 128, (tt + 1) * 128)
                        for h in range(4):
                            c2, hh = divmod(h, 2)
                            R = slice(hh * 64, hh * 64 + 64)
                            hc = slice(h * 128, (h + 1) * 128)
                            op("pe", lambda h=h, c2=c2, R=R, hc=hc: nc.tensor.matmul(Patt[h], lhsT=kinv[z][R, c2, :], rhs=qdec[z][R, c2, :],
                                                              start=True, stop=True),
                               reads=[r_kinv[z], r_qdec[z]], writes=[rPatt[h]])
                            yield
                        for h in range(4):
                            c2, hh = divmod(h, 2)
                            R = slice(hh * 64, hh * 64 + 64)
                            hc = slice(h * 128, (h + 1) * 128)
                            op("dve", lambda h=h, c2=c2, R=R, hc=hc: nc.vector.tensor_tensor(out=attm[z][:, h, :], in0=Patt[h], in1=tincl[:],
                                                                      op=ALU.mult),
                               reads=[rPatt[h], r_tincl], writes=[r_attm[z][h]])
                            yield
                            op("pe", lambda h=h, c2=c2, R=R, hc=hc: nc.tensor.matmul(Po[:, hc], lhsT=attm[z][:, h, :], rhs=vbf[z][:, hc],
                                                              start=True, stop=False),
                               reads=[r_attm[z][h], r_vbf[z]], writes=[rPo], signal=False)
                            yield
                            op("pe", lambda h=h, c2=c2, R=R, hc=hc: nc.tensor.matmul(Po[:, hc], lhsT=qdec[z][R, c2, :], rhs=Sbf[R, c2, :],
                                                              start=False, stop=True),
                               reads=[r_qdec[z], r_Sbf], writes=[rPo], signal=False)
                            yield
                            op("pe", lambda h=h, c2=c2, R=R, hc=hc: nc.tensor.matmul(PDS[R, c2 * 128:(c2 + 1) * 128],
                                                              lhsT=kend[z][:, h * 64:(h + 1) * 64], rhs=vbf[z][:, hc],
                                                              start=True, stop=True),
                               reads=[r_kend[z], r_vbf[z]], writes=[rPDS])
                            yield
                        for c2 in range(2):
                            op("dve", lambda c2=c2: nc.vector.scalar_tensor_tensor(
                                out=S32[:, c2, :], in0=S32[:, c2, :], scalar=Eq[z][:, c2, 127:128],
                                in1=PDS[:, c2 * 128:(c2 + 1) * 128], op0=ALU.mult, op1=ALU.add),
                               reads=[r_S32, r_Eq[z], rPDS], writes=[r_S32])
                            yield
                        op("dve", lambda: nc.vector.tensor_copy(out=Sbf[:], in_=S32[:]), reads=[r_S32], writes=[r_Sbf])
                        yield
                        for h in range(4):
                            hc = slice(h * 128, (h + 1) * 128)
                            op("act", lambda h=h, hc=hc: nc.scalar.activation(out=junk[:, hc], in_=Po[:, hc],
                                                                              func=AF.Square,
                                                                              accum_out=gst[z][:, h:h + 1]),
                               reads=[rPo], writes=[r_junk, r_gst[z]])
                            yield
                        op("act", lambda: nc.scalar.activation(out=gst[z][:, 4:8], in_=gst[z][:, 0:4], func=AF.Ln,
                                                               scale=1.0 / 128, bias=EPS),
                           reads=[r_gst[z]], writes=[r_gst[z]])
                        yield
                        op("act", lambda: nc.scalar.activation(out=gst[z][:, 8:12], in_=gst[z][:, 4:8], func=AF.Exp,
                                                               scale=-0.5), reads=[r_gst[z]], writes=[r_gst[z]])
                        yield
                        for h in range(4):
                            hc = slice(h * 128, (h + 1) * 128)
                            op("dve", lambda h=h, hc=hc: nc.vector.scalar_tensor_tensor(
                                out=og[z][:, hc], in0=Po[:, hc], scalar=gst[z][:, 8 + h:9 + h], in1=gr[z][:, hc],
                                op0=ALU.mult, op1=ALU.mult),
                               reads=[rPo, r_gst[z], r_gr[z]], writes=[r_og[z]])
                            yield
                        for h in range(4):
                            op("pe", lambda h=h: nc.tensor.transpose(PT[:, h, :], og[z][:, h * 128:(h + 1) * 128],
                                                                     ident[:]),
                               reads=[r_og[z], r_ident], writes=[rPT], signal=(h == 3))
                            yield
                        op("dve", lambda: nc.vector.tensor_copy(out=oglT[:, :, tc_], in_=PT[:, 0:4, :]),
                           reads=[rPT], writes=[r_oglT])
                        yield

                    def drive(*gens):
                        gens = [g for g in gens if g is not None]
                        while gens:
                            for g in list(gens):
                                try:
                                    next(g)
                                except StopIteration:
                                    gens.remove(g)

                    sch.begin_region()
                    for tt in range(NT):
                        for _ in gla_front(tt):
                            pass
                        for _ in gla_back(tt):
                            pass
                    sch.end_region()
                    prefetch_merge_w0()
                    sch.barrier()
                with ExitStack() as ph:
                    yT = sbt(ph, "yT", [128, KC, S], BF16); r_yT = Res()
                    wgs = [wgs0, sbt(ph, "wgs1", [128, KC, 128], BF16)]
                    wgg = [wgg0, sbt(ph, "wgg1", [128, KC, 128], BF16)]
                    wsb = [wsb0, sbt(ph, "wsb1", [128, 4, 128], BF16)]
                    wgl = [wgl0, sbt(ph, "wgl1", [128, 4, 128], BF16)]
                    r_wm = [r_wm0, Res()]
                    wo = sbt(ph, "wo", [128, KC, D], BF16); r_wo = Res()
                    sg = [sbt(ph, "sg%d" % i, [128, 512], F32) for i in range(4)]
                    r_sg = [Res() for _ in range(4)]
                    t1 = [sbt(ph, "t1%d" % i, [128, 512], F32) for i in range(2)]
                    t2 = [sbt(ph, "t2%d" % i, [128, 512], F32) for i in range(2)]
                    r_t1 = [Res() for _ in range(2)]; r_t2 = [Res() for _ in range(2)]
                    ht = [sbt(ph, "ht%d" % i, [128, D], F32) for i in range(2)]
                    r_ht = [Res() for _ in range(2)]

                    def load_merge_w(n, sl):
                        r = r_wm[sl]
                        sch.dma("pool", wgs[sl][:], w_in_v[:, :, OFF_GSB + n * 128:OFF_GSB + (n + 1) * 128], writes=[r])
                        sch.dma("pool", wgg[sl][:], w_in_v[:, :, OFF_GGLA + n * 128:OFF_GGLA + (n + 1) * 128], writes=[r])
                        sch.dma("pool", wsb[sl][:], wbsb_v[:, :, n * 128:(n + 1) * 128], writes=[r])
                        sch.dma("pool", wgl[sl][:], wbgl_v[:, :, n * 128:(n + 1) * 128], writes=[r])
                    sch.dma("pool", wo[:], w_out_v, writes=[r_wo])
                    sch.begin_region()
                    it = 0
                    for n in range(KC):
                        sl = n % 2
                        if n + 1 < KC:
                            load_merge_w(n + 1, 1 - sl)
                        for g in range(NG):
                            tc_ = slice(g * 512, (g + 1) * 512)
                            j = it % 2
                            it += 1
                            (b0, rb0), (b1, rb1), (b2, rb2), (b3, rb3) = [banks5[(4 * (it - 1) + q_) % 7] for q_ in range(4)]
                            proj_fm(b0, rb0, wgs[sl], r_wm[sl], slice(0, 128), xnT, r_xnT, KC, tc_)
                            op("act", lambda b0=b0, j=j, n=n: nc.scalar.activation(out=sg[2 * j][:], in_=b0, func=AF.Sigmoid,
                                                                                   bias=bgate(0, n)),
                               reads=[rb0, r_vecs], writes=[r_sg[2 * j]])
                            proj_fm(b1, rb1, wgg[sl], r_wm[sl], slice(0, 128), xnT, r_xnT, KC, tc_)
                            op("act", lambda b1=b1, j=j, n=n: nc.scalar.activation(out=sg[2 * j + 1][:], in_=b1,
                                                                                   func=AF.Sigmoid, bias=bgate(1, n)),
                               reads=[rb1, r_vecs], writes=[r_sg[2 * j + 1]])
                            proj_fm(b2, rb2, wsb[sl], r_wm[sl], slice(0, 128), osbT, r_osbT, 4, tc_)
                            op("dve", lambda b2=b2, j=j: nc.vector.tensor_tensor(out=t1[j][:], in0=b2, in1=sg[2 * j][:],
                                                                                 op=ALU.mult),
                               reads=[rb2, r_sg[2 * j]], writes=[r_t1[j]])
                            proj_fm(b3, rb3, wgl[sl], r_wm[sl], slice(0, 128), oglT, r_oglT, 4, tc_)
                            op("dve", lambda b3=b3, j=j: nc.vector.tensor_tensor(out=t2[j][:], in0=b3, in1=sg[2 * j + 1][:],
                                                                                 op=ALU.mult),
                               reads=[rb3, r_sg[2 * j + 1]], writes=[r_t2[j]])
                            op("pool", lambda j=j, n=n, tc_=tc_: nc.gpsimd.tensor_tensor(out=yT[:, n, tc_], in0=t1[j][:],
                                                                                         in1=t2[j][:], op=ALU.add),
                               reads=[r_t1[j], r_t2[j]], writes=[r_yT])
                    sch.end_region()
                    sch.begin_region()
                    for tt in range(NT):
                        sl = tt % 2
                        tc_ = slice(tt * 128, (tt + 1) * 128)
                        PW, rW0, rW1 = (PA, rPA0, rPA1) if sl == 0 else (PB, rPB0, rPB1)
                        sch.dma("sp", xin[sl][:], x[b, tc_, :], writes=[r_xin[sl]])
                        for half in range(2):
                            proj_tm(PW[:, half * 512:(half + 1) * 512], (rW0, rW1)[half], yT, r_yT, tc_, wo, r_wo,
                                    slice(half * 512, (half + 1) * 512), KC)
                        op("dve", lambda PW=PW, sl=sl: nc.vector.tensor_tensor(out=ht[sl][:], in0=PW[:, :], in1=xin[sl][:],
                                                                               op=ALU.add),
                           reads=[rW0, rW1, r_xin[sl]], writes=[r_ht[sl]])
                        sch.dma("sp", hscr[tc_, :], ht[sl][:], reads=[r_ht[sl]], writes=[r_hscr[tt]])
                        norm_pre(ht[sl], r_ht[sl], sl)
                        if tt > 0:
                            norm_post(1 - sl, gffn, xnT, r_xnT, tt - 1)
                    norm_post((NT - 1) % 2, gffn, xnT, r_xnT, NT - 1)
                    sch.end_region()
                    prefetch_ffn_w0()
                    sch.barrier()
            hnT, r_hnT = xnT, r_xnT
            HS = min(S, 1024)
            NGH = HS // 512
            NTH = HS // 128
            with ExitStack() as ph:
                wfo = sbt(ph, "wfo", [128, NF, D], BF16); r_wfo = Res()
                actT = sbt(ph, "actT", [128, NF, HS], BF16); r_actT = Res()
                wa = [wa0, sbt(ph, "wa1", [128, KC, 128], BF16)]
                wgt = [wgt0, sbt(ph, "wgt1", [128, KC, 128], BF16)]
                r_wf = [r_wf0, Res()]
                abuf = [sbt(ph, "abuf%d" % i, [128, 514], F32) for i in range(2)]
                r_abuf = [Res() for _ in range(2)]
                tcv = [sbt(ph, "tcv%d" % i, [128, 512], F32) for i in range(2)]
                r_tcv = [Res() for _ in range(2)]
                gel = [sbt(ph, "gel%d" % i, [128, 512], F32) for i in range(2)]
                r_gel = [Res() for _ in range(2)]
                halo = sbt(ph, "halo", [128, NF, 2], F32); r_halo = Res()
                hin = [sbt(ph, "hin%d" % i, [128, D], F32) for i in range(2)]
                r_hin = [Res() for _ in range(2)]
                h2 = [sbt(ph, "h2%d" % i, [128, D], F32) for i in range(2)]
                r_h2 = [Res() for _ in range(2)]
                op("pool", lambda: nc.gpsimd.memset(halo[:], 0.0), writes=[r_halo])

                def load_ffn_w(f, sl):
                    sch.dma("pool", wa[sl][:], wfi_v[:, :, f * 128:(f + 1) * 128], writes=[r_wf[sl]])
                    sch.dma("pool", wgt[sl][:], wfi_v[:, :, DFF + f * 128:DFF + (f + 1) * 128], writes=[r_wf[sl]])
                it = 0
                for hs in range(S // HS):
                    if hs == 0:
                        load_ffn_w(1, 1)
                    else:
                        load_ffn_w(0, 0)
                    for f in range(NF):
                        sl = f % 2
                        if f + 1 < NF and not (hs == 0 and f == 0):
                            load_ffn_w(f + 1, 1 - sl)
                        if hs == 0 and f < 11:
                            sch.dma("pool", wfo[:, 2 * f:2 * f + 2, :], wfo_v[:, 2 * f:2 * f + 2, :], writes=[r_wfo])
                        for g in range(NGH):
                            tc_ = slice(hs * HS + g * 512, hs * HS + (g + 1) * 512)
                            lc_ = slice(g * 512, (g + 1) * 512)
                            j = it % 2
                            it += 1
                            (bA, rbA), (bG, rbG) = (banks5[4], banks5[5]) if j == 0 else (banks5[6], banks5[3])
                            ab, rab = abuf[j], r_abuf[j]
                            proj_fm(bA, rbA, wa[sl], r_wf[sl], slice(0, 128), hnT, r_hnT, KC, tc_)
                            proj_fm(bG, rbG, wgt[sl], r_wf[sl], slice(0, 128), hnT, r_hnT, KC, tc_)
                            op("pool", lambda ab=ab, f=f: nc.gpsimd.tensor_copy(out=ab[:, 0:2], in_=halo[:, f, :]),
                               reads=[r_halo], writes=[rab])
                            op("act", lambda ab=ab, bA=bA: nc.scalar.copy(out=ab[:, 2:514], in_=bA),
                               reads=[rbA], writes=[rab])
                            op("pool", lambda ab=ab, f=f: nc.gpsimd.tensor_copy(out=halo[:, f, :], in_=ab[:, 512:514]),
                               reads=[rab], writes=[r_halo])
                            tv, rtv = tcv[j], r_tcv[j]
                            op("dve", lambda ab=ab, tv=tv, f=f: nc.vector.tensor_scalar(
                                out=tv[:], in0=ab[:, 2:514], scalar1=convw(2, f), scalar2=convb(f), op0=ALU.mult,
                                op1=ALU.add), reads=[rab, r_vecs], writes=[rtv])
                            op("dve", lambda ab=ab, tv=tv, f=f: nc.vector.scalar_tensor_tensor(
                                out=tv[:], in0=ab[:, 1:513], scalar=convw(1, f), in1=tv[:], op0=ALU.mult, op1=ALU.add),
                               reads=[rab, r_vecs, rtv], writes=[rtv])
                            op("dve", lambda ab=ab, tv=tv, f=f: nc.vector.scalar_tensor_tensor(
                                out=tv[:], in0=ab[:, 0:512], scalar=convw(0, f), in1=tv[:], op0=ALU.mult, op1=ALU.add),
                               reads=[rab, r_vecs, rtv], writes=[rtv])
                            ge, rge = gel[j], r_gel[j]
                            op("act", lambda tv=tv, ge=ge: nc.scalar.activation(out=ge[:], in_=tv[:],
                                                                                func=AF.Gelu_apprx_tanh),
                               reads=[rtv], writes=[rge])
                            op("dve", lambda ge=ge, bG=bG, f=f, lc_=lc_: nc.vector.tensor_tensor(
                                out=actT[:, f, lc_], in0=bG, in1=ge[:], op=ALU.mult),
                               reads=[rbG, rge], writes=[r_actT])
                    for t in range(NTH):
                        tt = hs * NTH + t
                        sl = tt % 2
                        lt_ = slice(t * 128, (t + 1) * 128)
                        tc_ = slice(tt * 128, (tt + 1) * 128)
                        PW, rW0, rW1 = (PA, rPA0, rPA1) if sl == 0 else (PB, rPB0, rPB1)
                        sch.dma("sp", hin[sl][:], hscr[tc_, :], reads=[r_hscr[tt]], writes=[r_hin[sl]])
                        for half in range(2):
                            proj_tm(PW[:, half * 512:(half + 1) * 512], (rW0, rW1)[half], actT, r_actT, lt_, wfo, r_wfo,
                                    slice(half * 512, (half + 1) * 512), NF)
                        op("dve", lambda PW=PW, sl=sl: nc.vector.tensor_tensor(out=h2[sl][:], in0=PW[:, :], in1=hin[sl][:],
                                                                               op=ALU.add),
                           reads=[rW0, rW1, r_hin[sl]], writes=[r_h2[sl]])
                        st, r_st = stt[sl], r_stt[sl]
                        rms_stats(h2[sl][:], r_h2[sl], st, r_st, D, D)
                        op("dve", lambda sl=sl, st=st: nc.vector.scalar_tensor_tensor(
                            out=h2[sl][:], in0=h2[sl][:], scalar=st[:, 2:3], in1=gfin_bc[:], op0=ALU.mult, op1=ALU.mult),
                           reads=[r_h2[sl], r_st, r_gfin], writes=[r_h2[sl]])
                        sch.dma("sp", out[b, tc_, :], h2[sl][:], reads=[r_h2[sl]])
                if b + 1 < NB:
                    for tt in range(2):
                        sch.dma("sp", xin[tt][:], x[b + 1, tt * 128:(tt + 1) * 128, :], writes=[r_xin[tt]])
                sch.barrier()
    return nc


_NC_CACHE = {}


def _get_nc(S, NB):
    key = (S, NB)
    if key not in _NC_CACHE:
        _NC_CACHE[key] = build_nc(S, NB)
    return _NC_CACHE[key]


def kernel(x, norm_mix_g, w_in, b_gate, w_alpha_up, b_alpha, gla_norm_g, w_branch_sb, w_branch_gla, w_out,
           norm_ffn_g, w_ffn_in, conv_w, conv_b, w_ffn_out, norm_final_g, n_cores=N_CORES):
    f = lambda a: np.ascontiguousarray(np.asarray(a, dtype=np.float32))
    x = f(x)
    B, S, _ = x.shape
    NB = B // n_cores
    shared = {
        "norm_mix_g": f(norm_mix_g)[0], "w_in": f(w_in)[0], "b_gate": f(b_gate)[0],
        "w_alpha_up": f(w_alpha_up)[0], "b_alpha": f(b_alpha)[0], "gla_norm_g": f(gla_norm_g)[0],
        "w_branch_sb": f(w_branch_sb)[0], "w_branch_gla": f(w_branch_gla)[0], "w_out": f(w_out)[0],
        "norm_ffn_g": f(norm_ffn_g)[0], "w_ffn_in": f(w_ffn_in)[0], "conv_w": f(conv_w)[0],
        "conv_b": f(conv_b)[0], "w_ffn_out": f(w_ffn_out)[0], "norm_final_g": f(norm_final_g),
    }
    nc = _get_nc(S, NB)
    in_maps = []
    for c in range(n_cores):
        m = dict(shared)
        m["x"] = np.ascontiguousarray(x[c * NB:(c + 1) * NB])
        in_maps.append(m)
    res = run_bass_kernel_spmd(nc, in_maps, core_ids=list(range(n_cores)))
    return np.concatenate([np.asarray(r["out"]) for r in res.results], axis=0).astype(np.float32)
```

```python
import numpy as np
from contextlib import ExitStack
import concourse.bass as bass
import concourse.mybir as mybir
from concourse.bass_utils import run_bass_kernel_spmd

F32 = mybir.dt.float32
BF16 = mybir.dt.bfloat16
AF = mybir.ActivationFunctionType
ALU = mybir.AluOpType

D = 1024
KC = 8
IN_TOTAL = 5136
OFF_SBQ, OFF_SBK, OFF_SBV = 0, 512, 1024
OFF_G = 1536
OFF_GSB, OFF_GGLA = 3088, 4112
DFF = 2816
NF = 22
EPS = 1e-6
N_CORES = 8


class Res:
    __slots__ = ("name", "last_write", "reads", "dma_sem", "dma_cnt", "parent", "children")

    def __init__(self, name="", parent=None):
        self.name = name
        self.last_write = None
        self.reads = {}
        self.dma_sem = None
        self.dma_cnt = 0
        self.parent = parent
        self.children = []
        if parent is not None:
            parent.children.append(self)

    def related(self):
        out = [self]
        if self.parent is not None:
            out.append(self.parent)
        out.extend(self.children)
        return out


class Sched:
    def __init__(self, nc, ctx):
        self.nc = nc
        self.ctx = ctx
        self.engs = {"pe": nc.tensor, "act": nc.scalar, "dve": nc.vector,
                     "pool": nc.gpsimd, "sp": nc.sync}
        self.sems = {}
        self.cnt = {}
        self.semobj = {}
        for k in ("pe", "act", "dve", "pool"):
            self.sems[k] = ctx.enter_context(nc.semaphore("s_" + k))
            self.cnt[k] = 0
            self.semobj[k] = self.sems[k]
        self.waited = {}
        self.pending = {k: False for k in self.cnt}
        self.dma_total = {}
        self.n_dma_sems = 0
        self.n_wait = 0
        self.n_ins = 0

    def _dma_sem(self, r):
        if r.dma_sem is None:
            key = "d%d" % self.n_dma_sems
            self.n_dma_sems += 1
            self.semobj[key] = self.ctx.enter_context(self.nc.semaphore(key))
            self.dma_total[key] = 0
            r.dma_sem = key
        return r.dma_sem

    def share_dma_sem(self, r_from, r_to):
        r_to.dma_sem = self._dma_sem(r_from)

    def _wait(self, eng, deps):
        e = self.engs[eng]
        for (sk, val) in deps:
            if self.waited.get((eng, sk), 0) >= val:
                continue
            e.wait_ge(self.semobj[sk], val)
            self.waited[(eng, sk)] = val
            self.n_wait += 1

    def _deps(self, eng, reads, writes):
        deps = {}

        def add(ev, kind):
            if ev is None:
                return
            sk, val = ev
            if sk == eng and (eng == "pe" or kind == "war"):
                return
            if deps.get(sk, 0) < val:
                deps[sk] = val
        for r0 in reads:
            for r in r0.related():
                add(r.last_write, "raw")
        for w0 in writes:
            for w in w0.related():
                add(w.last_write, "waw")
                for sk, val in w.reads.items():
                    add((sk, val), "war")
        return list(deps.items())

    def begin_region(self):
        self.region = []

    def end_region(self):
        ops, self.region = self.region, None
        n = len(ops)
        last_w, readers = {}, {}
        preds = [set() for _ in range(n)]
        for i, (eng, fn, reads, writes, signal, cost) in enumerate(ops):
            for r0 in reads:
                for r in r0.related():
                    if id(r) in last_w:
                        preds[i].add(last_w[id(r)])
            for w0 in writes:
                for w in w0.related():
                    if id(w) in last_w:
                        preds[i].add(last_w[id(w)])
                    preds[i].update(readers.get(id(w), ()))
            for r0 in reads:
                readers.setdefault(id(r0), []).append(i)
            for w0 in writes:
                last_w[id(w0)] = i
                readers[id(w0)] = []
            preds[i].discard(i)
        succs = [[] for _ in range(n)]
        npred = [len(p) for p in preds]
        for i, p in enumerate(preds):
            for j in p:
                succs[j].append(i)
        eng_free = {}
        fin = [0.0] * n
        ready = [i for i in range(n) if npred[i] == 0]
        LAT = 0.15
        while ready:
            best, best_t = None, None
            for i in ready:
                eng = ops[i][0]
                t = eng_free.get(eng, 0.0)
                for j in preds[i]:
                    tj = fin[j] + (LAT if ops[j][0] != eng else 0.0)
                    if tj > t:
                        t = tj
                if best is None or t < best_t - 1e-9 or (abs(t - best_t) <= 1e-9 and i < best):
                    best, best_t = i, t
            ready.remove(best)
            eng, fn, reads, writes, signal, cost = ops[best]
            fin[best] = best_t + cost
            if isinstance(eng, tuple):
                fn(reads, writes)
            else:
                eng_free[eng] = fin[best]
                self.op(eng, fn, reads=reads, writes=writes, signal=signal)
            for k in succs[best]:
                npred[k] -= 1
                if npred[k] == 0:
                    ready.append(k)

    def op(self, eng, fn, reads=(), writes=(), signal=True, cost=None):
        if getattr(self, "region", None) is not None:
            if cost is None:
                cost = {"pe": 0.25, "act": 0.6, "dve": 0.5, "pool": 1.0}[eng]
            self.region.append((eng, fn, tuple(reads), tuple(writes), signal, cost))
            return None
        self._wait(eng, self._deps(eng, reads, writes))
        ins = fn()
        self.n_ins += 1
        if signal:
            self.cnt[eng] += 1
            ins.then_inc(self.sems[eng], 1)
            self.pending[eng] = False
            val = self.cnt[eng]
        else:
            self.pending[eng] = True
            val = self.cnt[eng] + 1
        for r in reads:
            if r.reads.get(eng, 0) < val:
                r.reads[eng] = val
        for w in writes:
            w.last_write = (eng, val)
            w.reads = {}
        return ins

    def dma(self, q, out, in_, reads=(), writes=(), **kw):
        if getattr(self, "region", None) is not None:
            def emit(reads_, writes_, q=q, out=out, in_=in_, kw=kw):
                reg, self.region = self.region, None
                self.dma(q, out, in_, reads=reads_, writes=writes_, **kw)
                self.region = reg
            self.region.append((("dma", q, len(self.region)), emit, tuple(reads), tuple(writes), True, 3.0))
            return None
        anchor = writes[0] if writes else reads[0]
        sk = self._dma_sem(anchor)
        self._wait(q, [d for d in self._deps("dma", reads, writes) if d[0] != sk])
        ins = self.engs[q].dma_start(out=out, in_=in_, **kw)
        ins.then_inc(self.semobj[sk], 16)
        self.dma_total[sk] += 16
        val = self.dma_total[sk]
        self.n_ins += 1
        for r in reads:
            if r.reads.get(sk, 0) < val:
                r.reads[sk] = val
        for w in writes:
            w.last_write = (sk, val)
            w.reads = {}
        return ins

    def barrier(self):
        for k, p in self.pending.items():
            assert not p, "unsignaled instruction pending on " + k
        evs = [(k, v) for k, v in self.cnt.items() if v > 0]
        evs += [(k, v) for k, v in self.dma_total.items() if v > 0]
        for eng in ("pe", "act", "dve", "pool", "sp"):
            self._wait(eng, evs)


def build_nc(S, NB):
    NT = S // 128
    NG = S // 512
    assert S % 512 == 0
    nc = bass.Bass("TRN2", target_bir_lowering=False)

    def din(name, shape):
        return nc.dram_tensor(name, shape, F32, kind="ExternalInput").ap()

    x = din("x", [NB, S, D])
    norm_mix_g = din("norm_mix_g", [D])
    w_in = din("w_in", [D, IN_TOTAL])
    b_gate = din("b_gate", [2, D])
    w_alpha_up = din("w_alpha_up", [16, 256])
    b_alpha = din("b_alpha", [256])
    gla_norm_g = din("gla_norm_g", [512])
    w_branch_sb = din("w_branch_sb", [512, D])
    w_branch_gla = din("w_branch_gla", [512, D])
    w_out = din("w_out", [D, D])
    norm_ffn_g = din("norm_ffn_g", [D])
    w_ffn_in = din("w_ffn_in", [D, 2 * DFF])
    conv_w = din("conv_w", [3, DFF])
    conv_b = din("conv_b", [DFF])
    w_ffn_out = din("w_ffn_out", [DFF, D])
    norm_final_g = din("norm_final_g", [D])
    out = nc.dram_tensor("out", [NB, S, D], F32, kind="ExternalOutput").ap()
    hscr = nc.dram_tensor("hscr", [S, D], F32, kind="Internal").ap()

    w_in_v = w_in.rearrange("(kc p) n -> p kc n", p=128)
    wbsb_v = w_branch_sb.rearrange("(kc p) n -> p kc n", p=128)
    wbgl_v = w_branch_gla.rearrange("(kc p) n -> p kc n", p=128)
    w_out_v = w_out.rearrange("(kc p) n -> p kc n", p=128)
    wfi_v = w_ffn_in.rearrange("(kc p) n -> p kc n", p=128)
    wfo_v = w_ffn_out.rearrange("(f p) n -> p f n", p=128)

    with ExitStack() as ctx:
        sch = Sched(nc, ctx)
        op = sch.op

        uid = [0]

        def sbt(c, name, shape, dt):
            uid[0] += 1
            return c.enter_context(nc.sbuf_tensor("%s_%d" % (name, uid[0]), shape, dt))

        PA = ctx.enter_context(nc.psum_tensor("PA", [128, 1024], F32))
        PB = ctx.enter_context(nc.psum_tensor("PB", [128, 1024], F32))
        PCD = ctx.enter_context(nc.psum_tensor("PCD", [128, 1024], F32))
        PC = PCD[:, 0:512]
        PD = PCD[:, 512:1024]
        PE_ = ctx.enter_context(nc.psum_tensor("PE", [128, 512], F32))
        PT = ctx.enter_context(nc.psum_tensor("PT", [128, 8, 128], BF16))
        rPA0, rPA1, rPB0, rPB1 = Res("PA0"), Res("PA1"), Res("PB0"), Res("PB1")
        rPC, rPD, rPE, rPT = Res("PC"), Res("PD"), Res("PE"), Res("PT")
        banks5 = [(PA[:, 0:512], rPA0), (PA[:, 512:1024], rPA1), (PB[:, 0:512], rPB0),
                  (PB[:, 512:1024], rPB1), (PC[:, :], rPC), (PD[:, :], rPD), (PE_[:, :], rPE)]

        cst = ctx
        idf = sbt(cst, "idf", [128, 128], F32); r_idf = Res()
        ident = sbt(cst, "ident", [128, 128], BF16); r_ident = Res()
        trineg = sbt(cst, "trineg", [128, 128], BF16); r_trineg = Res()
        onesneg = sbt(cst, "onesneg", [128, 128], BF16); r_onesneg = Res()
        mstrict = sbt(cst, "mstrict", [128, 128], BF16); r_mstrict = Res()
        mnegbig = sbt(cst, "mnegbig", [128, 128], BF16); r_mnegbig = Res()
        zeros = sbt(cst, "zeros", [128, 512], BF16); r_zeros = Res()
        ust = sbt(cst, "ust", [128, 128], F32); r_ust = Res()
        tincl = sbt(cst, "tincl", [128, 128], F32); r_tincl = Res()
        ones32 = sbt(cst, "ones32", [128, 128], F32); r_ones32 = Res()
        tmpc = sbt(cst, "tmpc", [128, 128], F32); r_tmpc = Res()
        vst = sbt(cst, "vst", [128, 128], F32); r_vst = Res()
        vecs = sbt(cst, "vecs", [128, 128], F32); r_vecs = Res()
        gfin_bc = sbt(cst, "gfin_bc", [128, D], F32); r_gfin = Res()
        ggla_bc = sbt(cst, "ggla_bc", [128, 512], F32); r_ggla = Res()
        balpha = sbt(cst, "balpha", [1, 256], F32); r_balpha = Res()
        walpha = sbt(cst, "walpha", [16, 256], F32); r_walpha = Res()

        def gp(fn, reads=(), writes=()):
            return op("pool", fn, reads=reads, writes=writes)

        def aff(t, cmp, fill, cm=1, pat=-1, base=0):
            return lambda: nc.gpsimd.affine_select(out=t[:], in_=t[:], pattern=[[pat, 128]], compare_op=cmp,
                                                   fill=fill, base=base, channel_multiplier=cm)
        gp(lambda: nc.gpsimd.memset(idf[:], 1.0), writes=[r_idf])
        gp(aff(idf, ALU.is_equal, 0.0), reads=[r_idf], writes=[r_idf])
        op("dve", lambda: nc.vector.tensor_copy(out=ident[:], in_=idf[:]), reads=[r_idf], writes=[r_ident])
        gp(lambda: nc.gpsimd.memset(tmpc[:], -1.0), writes=[r_tmpc])
        gp(aff(tmpc, ALU.is_ge, 0.0), reads=[r_tmpc], writes=[r_tmpc])
        op("dve", lambda: nc.vector.tensor_copy(out=trineg[:], in_=tmpc[:]), reads=[r_tmpc], writes=[r_trineg])
        gp(lambda: nc.gpsimd.memset(tmpc[:], 1.0), reads=[r_tmpc], writes=[r_tmpc])
        gp(aff(tmpc, ALU.is_gt, 0.0, cm=-1, pat=1), reads=[r_tmpc], writes=[r_tmpc])
        op("dve", lambda: nc.vector.tensor_copy(out=mstrict[:], in_=tmpc[:]), reads=[r_tmpc], writes=[r_mstrict])
        gp(lambda: nc.gpsimd.memset(tmpc[:], 0.0), reads=[r_tmpc], writes=[r_tmpc])
        gp(aff(tmpc, ALU.is_gt, -30000.0, cm=-1, pat=1), reads=[r_tmpc], writes=[r_tmpc])
        op("dve", lambda: nc.vector.tensor_copy(out=mnegbig[:], in_=tmpc[:]), reads=[r_tmpc], writes=[r_mnegbig])
        gp(lambda: nc.gpsimd.memset(tmpc[:], -1.0), reads=[r_tmpc], writes=[r_tmpc])
        op("dve", lambda: nc.vector.tensor_copy(out=onesneg[:], in_=tmpc[:]), reads=[r_tmpc], writes=[r_onesneg])
        gp(lambda: nc.gpsimd.memset(zeros[:], 0.0), writes=[r_zeros])
        gp(lambda: nc.gpsimd.memset(ust[:], 1.0), writes=[r_ust])
        gp(aff(ust, ALU.is_gt, 0.0), reads=[r_ust], writes=[r_ust])
        gp(lambda: nc.gpsimd.memset(tincl[:], 1.0), writes=[r_tincl])
        gp(aff(tincl, ALU.is_ge, 0.0, cm=-1, pat=1), reads=[r_tincl], writes=[r_tincl])
        gp(lambda: nc.gpsimd.memset(ones32[:], 1.0), writes=[r_ones32])
        gp(lambda: nc.gpsimd.memset(vst[:], 0.0), writes=[r_vst])
        sch.dma("sp", vst[0:8, :], norm_mix_g.rearrange("(k p) -> k p", p=128), reads=[r_vst], writes=[r_vst])
        sch.dma("sp", vst[8:16, :], norm_ffn_g.rearrange("(k p) -> k p", p=128), writes=[r_vst])
        sch.dma("sp", vst[16:32, :], b_gate.rearrange("j (k p) -> (j k) p", p=128), writes=[r_vst])
        sch.dma("sp", vst[32:98, :], conv_w.rearrange("i (f p) -> (i f) p", p=128), writes=[r_vst])
        sch.dma("sp", vst[98:120, :], conv_b.rearrange("(f p) -> f p", p=128), writes=[r_vst])
        op("pe", lambda: nc.tensor.matmul(PC[:, 0:128], lhsT=vst[:, :], rhs=idf[:, :], start=True, stop=True),
           reads=[r_vst, r_idf], writes=[rPC])
        op("dve", lambda: nc.vector.tensor_copy(out=vecs[:], in_=PC[:, 0:128]), reads=[rPC], writes=[r_vecs])
        gmix = vecs[:, 0:8]
        gffn = vecs[:, 8:16]

        def bgate(j, n):
            return vecs[:, 16 + j * 8 + n:16 + j * 8 + n + 1]

        def convw(i, f):
            return vecs[:, 32 + i * NF + f:32 + i * NF + f + 1]

        def convb(f):
            return vecs[:, 98 + f:99 + f]
        sch.dma("sp", gfin_bc[:], norm_final_g.partition_broadcast(128), writes=[r_gfin])
        sch.dma("sp", ggla_bc[:], gla_norm_g.partition_broadcast(128), writes=[r_ggla])
        sch.dma("sp", balpha[:], b_alpha.rearrange("(o n) -> o n", o=1), writes=[r_balpha])
        sch.dma("sp", walpha[:], w_alpha_up, writes=[r_walpha])
        ones_bf = sbt(cst, "ones_bf", [128, 128], BF16); r_ones_bf = Res()
        ust_bf = sbt(cst, "ust_bf", [128, 128], BF16); r_ust_bf = Res()
        tincl_bf = sbt(cst, "tincl_bf", [128, 128], BF16); r_tincl_bf = Res()
        wa_hi = sbt(cst, "wa_hi", [16, 256], BF16); wa_lo = sbt(cst, "wa_lo", [16, 256], BF16); r_wahl = Res()
        ba_hi = sbt(cst, "ba_hi", [1, 256], BF16); ba_lo = sbt(cst, "ba_lo", [1, 256], BF16); r_bahl = Res()
        op("dve", lambda: nc.vector.tensor_copy(out=ones_bf[:], in_=ones32[:]), reads=[r_ones32], writes=[r_ones_bf])
        op("dve", lambda: nc.vector.tensor_copy(out=ust_bf[:], in_=ust[:]), reads=[r_ust], writes=[r_ust_bf])
        op("dve", lambda: nc.vector.tensor_copy(out=tincl_bf[:], in_=tincl[:]), reads=[r_tincl], writes=[r_tincl_bf])
        op("dve", lambda: nc.vector.tensor_copy(out=wa_hi[:], in_=walpha[:]), reads=[r_walpha], writes=[r_wahl])
        op("dve", lambda: nc.vector.tensor_tensor(out=wa_lo[:], in0=walpha[:], in1=wa_hi[:], op=ALU.subtract),
           reads=[r_walpha, r_wahl], writes=[r_wahl])
        op("dve", lambda: nc.vector.tensor_copy(out=ba_hi[:], in_=balpha[:]), reads=[r_balpha], writes=[r_bahl])
        op("dve", lambda: nc.vector.tensor_tensor(out=ba_lo[:], in0=balpha[:], in1=ba_hi[:], op=ALU.subtract),
           reads=[r_balpha, r_bahl], writes=[r_bahl])

        xnT = sbt(ctx, "xnT", [128, KC, S], BF16); r_xnT = Res("xnT")
        r_xnTg = [Res("xnTg%d" % g, r_xnT) for g in range(NG)]
        xin = [sbt(ctx, "xin%d" % i, [128, D], F32) for i in range(3)]
        r_xin = [Res("xin%d" % i) for i in range(3)]
        xnb = [sbt(ctx, "xnb%d" % i, [128, D], BF16) for i in range(2)]
        r_xnb = [Res() for _ in range(2)]
        junk = sbt(ctx, "junk", [128, D], BF16); r_junk = Res()
        stt = [sbt(ctx, "stt%d" % i, [128, 8], F32) for i in range(2)]
        r_stt = [Res() for _ in range(2)]
        wgs0 = sbt(ctx, "wgs_p0", [128, KC, 128], BF16); wgg0 = sbt(ctx, "wgg_p0", [128, KC, 128], BF16)
        wsb0 = sbt(ctx, "wsb_p0", [128, 4, 128], BF16); wgl0 = sbt(ctx, "wgl_p0", [128, 4, 128], BF16)
        r_wm0 = Res("wm0")
        wa0 = sbt(ctx, "wa_p0", [128, KC, 128], BF16); wgt0 = sbt(ctx, "wgt_p0", [128, KC, 128], BF16)
        r_wf0 = Res("wf0")

        wgA = sbt(ctx, "wgA_p", [128, KC, 512], BF16); r_wgA = Res("wgA")

        def prefetch_gla_w0():
            sch.dma("pool", wgA[:], w_in_v[:, :, OFF_G:OFF_G + 512], writes=[r_wgA])

        def prefetch_merge_w0():
            sch.dma("pool", wgs0[:], w_in_v[:, :, OFF_GSB:OFF_GSB + 128], writes=[r_wm0])
            sch.dma("pool", wgg0[:], w_in_v[:, :, OFF_GGLA:OFF_GGLA + 128], writes=[r_wm0])
            sch.dma("pool", wsb0[:], wbsb_v[:, :, 0:128], writes=[r_wm0])
            sch.dma("pool", wgl0[:], wbgl_v[:, :, 0:128], writes=[r_wm0])

        def prefetch_ffn_w0():
            sch.dma("pool", wa0[:], wfi_v[:, :, 0:128], writes=[r_wf0])
            sch.dma("pool", wgt0[:], wfi_v[:, :, DFF:DFF + 128], writes=[r_wf0])
        r_hscr = [Res("hscr%d" % t) for t in range(NT)]
        r_out = Res("out")

        def rms_stats(src_ap, r_src, st, r_st, n, width):
            op("act", lambda: nc.scalar.activation(out=junk[:, 0:width], in_=src_ap, func=AF.Square,
                                                   accum_out=st[:, 0:1]),
               reads=[r_src], writes=[r_junk, r_st])
            op("act", lambda: nc.scalar.activation(out=st[:, 1:2], in_=st[:, 0:1], func=AF.Ln,
                                                   scale=1.0 / n, bias=EPS), reads=[r_st], writes=[r_st])
            op("act", lambda: nc.scalar.activation(out=st[:, 2:3], in_=st[:, 1:2], func=AF.Exp, scale=-0.5),
               reads=[r_st], writes=[r_st])

        def norm_pre(src, r_src, slot):
            st, r_st = stt[slot], r_stt[slot]
            rms_stats(src[:], r_src, st, r_st, D, D)
            nb, r_nb = xnb[slot], r_xnb[slot]
            op("dve", lambda: nc.vector.tensor_scalar(out=nb[:], in0=src[:], scalar1=st[:, 2:3], scalar2=None,
                                                      op0=ALU.mult), reads=[r_src, r_st], writes=[r_nb])

        def norm_post(slot, gcols, dstT, r_dst, tt):
            nb, r_nb = xnb[slot], r_xnb[slot]
            for kc in range(KC):
                op("pe", lambda kc=kc: nc.tensor.transpose(PT[:, kc, :], nb[:, kc * 128:(kc + 1) * 128], ident[:]),
                   reads=[r_nb, r_ident], writes=[rPT], signal=(kc == KC - 1))
            op("dve", lambda: nc.vector.tensor_tensor(out=dstT[:, :, tt * 128:(tt + 1) * 128], in0=PT[:],
                                                      in1=gcols.unsqueeze(2).broadcast_to([128, KC, 128]),
                                                      op=ALU.mult),
               reads=[rPT, r_vecs], writes=[r_dst])

        def proj_fm(bank, rbank, w_tile, r_w, wcols, srcT, r_src, nk, tcols, mrows=128):
            for kc in range(nk):
                op("pe", lambda kc=kc: nc.tensor.matmul(bank[0:mrows, :], lhsT=w_tile[:, kc, wcols],
                                                        rhs=srcT[:, kc, tcols], start=(kc == 0), stop=(kc == nk - 1)),
                   reads=[r_w, r_src], writes=[rbank], signal=(kc == nk - 1))

        def proj_tm(bank_ap, rbank, srcT, r_src, tcols, w_tile, r_w, wcols, nk):
            for kc in range(nk):
                op("pe", lambda kc=kc: nc.tensor.matmul(bank_ap, lhsT=srcT[:, kc, tcols], rhs=w_tile[:, kc, wcols],
                                                        start=(kc == 0), stop=(kc == nk - 1)),
                   reads=[r_w, r_src], writes=[rbank], signal=(kc == nk - 1))

        for b in range(NB):
            sch.begin_region()
            for tt in range(NT):
                sl = tt % 2
                xs = tt % 3
                if not (b > 0 and tt < 2):
                    sch.dma("sp", xin[xs][:], x[b, tt * 128:(tt + 1) * 128, :], writes=[r_xin[xs]])
                norm_pre(xin[xs], r_xin[xs], sl)
                if tt > 0:
                    norm_post(1 - sl, gmix, xnT, r_xnTg[(tt - 1) // 4], tt - 1)
            norm_post((NT - 1) % 2, gmix, xnT, r_xnTg[(NT - 1) // 4], NT - 1)
            with ExitStack() as mix:
                osbT = sbt(mix, "osbT", [128, 4, S], BF16); r_osbT = Res("osbT")
                oglT = sbt(mix, "oglT", [128, 4, S], BF16); r_oglT = Res("oglT")
                with ExitStack() as ph:
                    wv = sbt(ph, "wv", [128, KC, 512], BF16); r_wv = Res()
                    wq = [sbt(ph, "wq%d" % i, [128, KC, 128], BF16) for i in range(2)]
                    wk = [sbt(ph, "wk%d" % i, [128, KC, 128], BF16) for i in range(2)]
                    r_wq = [Res() for _ in range(2)]; r_wk = [Res() for _ in range(2)]
                    qT = [sbt(ph, "qT%d" % i, [128, S], BF16) for i in range(2)]
                    kT = [sbt(ph, "kT%d" % i, [128, S], BF16) for i in range(2)]
                    r_qT = [Res() for _ in range(2)]; r_kT = [Res() for _ in range(2)]
                    vtok = sbt(ph, "vtok", [128, NT, 512], BF16); r_vtok = Res()
                    KB = 16
                    SPK = [sbt(ph, "SPK%d" % i, [128, 2, 512], BF16) for i in range(KB)]
                    r_SPK = [Res() for _ in range(KB)]
                    W2 = [sbt(ph, "W2%d" % i, [128, 2, 512], BF16) for i in range(2)]
                    r_W = [Res() for _ in range(2)]
                    lacc = sbt(ph, "lacc", [128, 2, 512], BF16); r_lacc = Res()

                    sch.dma("pool", wq[0][:], w_in_v[:, :, OFF_SBQ:OFF_SBQ + 128], writes=[r_wq[0]])
                    sch.dma("pool", wk[0][:], w_in_v[:, :, OFF_SBK:OFF_SBK + 128], writes=[r_wk[0]])
                    sch.dma("pool", wv[:], w_in_v[:, :, OFF_SBV:OFF_SBV + 512], writes=[r_wv])
                    PS = [PA[:, :].rearrange("p (h n) -> p h n", h=2), PCD[:, :].rearrange("p (h n) -> p h n", h=2)]
                    rPS = [[rPA0, rPA1], [rPC, rPD]]
                    accb, racc = PE_, rPE
                    mstrict2 = mstrict[:].unsqueeze(1).broadcast_to([128, 2, 128])

                    def qk_proj(p, sl):
                        for g in range(NG):
                            tc_ = slice(g * 512, (g + 1) * 512)
                            bk, rb = banks5[2]
                            for kc in range(KC):
                                op("pe", lambda kc=kc, bk=bk, tc_=tc_: nc.tensor.matmul(bk, lhsT=wq[sl][:, kc, :], rhs=xnT[:, kc, tc_],
                                                                  start=(kc == 0), stop=(kc == KC - 1)),
                                   reads=[r_wq[sl], r_xnTg[g]], writes=[rb], signal=(kc == KC - 1))
                                if kc % 4 == 3:
                                    yield
                            op("act", lambda bk=bk, tc_=tc_: nc.scalar.mul(out=qT[sl][:, tc_], in_=bk, mul=0.125),
                               reads=[rb], writes=[r_qT[sl]])
                            yield
                            bk, rb = banks5[3]
                            for kc in range(KC):
                                op("pe", lambda kc=kc, bk=bk, tc_=tc_: nc.tensor.matmul(bk, lhsT=wk[sl][:, kc, :], rhs=xnT[:, kc, tc_],
                                                                  start=(kc == 0), stop=(kc == KC - 1)),
                                   reads=[r_wk[sl], r_xnTg[g]], writes=[rb], signal=(kc == KC - 1))
                                if kc % 4 == 3:
                                    yield
                            op("dve", lambda bk=bk, tc_=tc_: nc.vector.tensor_copy(out=kT[sl][:, tc_], in_=bk),
                               reads=[rb], writes=[r_kT[sl]])
                            yield

                    for _ in qk_proj(0, 0):
                        pass
                    for tt in range(NT):
                        bk, rb = banks5[4 + tt % 2]
                        proj_tm(bk, rb, xnT, r_xnTg[tt // 4], slice(tt * 128, (tt + 1) * 128), wv, r_wv, slice(0, 512), KC)
                        if tt % 2 == 0:
                            op("act", lambda bk=bk, tt=tt: nc.scalar.copy(out=vtok[:, tt, :], in_=bk),
                               reads=[rb], writes=[r_vtok])
                        else:
                            op("dve", lambda bk=bk, tt=tt: nc.vector.tensor_copy(out=vtok[:, tt, :], in_=bk),
                               reads=[rb], writes=[r_vtok])
                    sch.end_region()
                    for p in range(4):
                        sl = p % 2
                        nxt = None
                        if p + 1 < 4:
                            sch.dma("pool", wq[1 - sl][:], w_in_v[:, :, OFF_SBQ + (p + 1) * 128:OFF_SBQ + (p + 2) * 128],
                                    writes=[r_wq[1 - sl]])
                            sch.dma("pool", wk[1 - sl][:], w_in_v[:, :, OFF_SBK + (p + 1) * 128:OFF_SBK + (p + 2) * 128],
                                    writes=[r_wk[1 - sl]])
                            nxt = qk_proj(p + 1, 1 - sl)

                        def tick():
                            nonlocal nxt
                            if nxt is not None:
                                try:
                                    next(nxt)
                                except StopIteration:
                                    nxt = None
                        units = []
                        for qg in range(NG):
                            lst = [(4 * qg + i, 128 * i) for i in (3, 2, 1, 0)]
                            lst += [(kb, 0) for kb in range(4 * qg - 1, -1, -1)]
                            for ui, (kb, c0) in enumerate(lst):
                                units.append(dict(qg=qg, kb=kb, c0=c0, first=(ui == 0),
                                                  last=(ui == len(lst) - 1), diag=(kb >= 4 * qg)))
                        nU = len(units)

                        def qk_mm(u, dst, rdst, stop):
                            c0 = u["c0"]
                            qc = slice(u["qg"] * 512 + c0, u["qg"] * 512 + 512)
                            kc_ = slice(u["kb"] * 128, u["kb"] * 128 + 128)
                            for hh in range(2):
                                R = slice(hh * 64, hh * 64 + 64)
                                op("pe", lambda hh=hh, R=R: nc.tensor.matmul(dst[:, hh, c0:512], lhsT=kT[sl][R, kc_],
                                                                  rhs=qT[sl][R, qc], start=True, stop=stop),
                                   reads=[r_kT[sl], r_qT[sl]], writes=[rdst[hh]], signal=(stop and hh == 1))

                        def stB(i, j):
                            u = units[i]
                            c0 = u["c0"]
                            SP = SPK[j]
                            op("act", lambda: nc.scalar.activation(out=SP[:, :, c0:512], in_=PS[j % 2][:, :, c0:512],
                                                                   func=AF.Softplus),
                               reads=rPS[j % 2], writes=[r_SPK[j]], cost=1.05)
                            if u["diag"]:
                                op("dve", lambda: nc.vector.tensor_tensor(out=SP[:, :, c0:c0 + 128],
                                                                          in0=SP[:, :, c0:c0 + 128], in1=mstrict2,
                                                                          op=ALU.mult),
                                   reads=[r_SPK[j], r_mstrict], writes=[r_SPK[j]])

                        def stC(i, j):
                            u = units[i]
                            c0 = u["c0"]
                            SP = SPK[j]
                            lw, rlw = PS[j % 2], rPS[j % 2]
                            if u["first"]:
                                op("pool", lambda: nc.gpsimd.memset(lacc[:], 0.0), writes=[r_lacc])
                            qk_mm(u, lw, rlw, False)
                            for hh in range(2):
                                op("pe", lambda hh=hh: nc.tensor.matmul(lw[:, hh, c0:512], lhsT=trineg[:], rhs=SP[:, hh, c0:512],
                                                                  start=False, stop=False),
                                   reads=[r_trineg, r_SPK[j]], writes=[rlw[hh]], signal=False)
                                if u["diag"]:
                                    op("pe", lambda hh=hh: nc.tensor.matmul(lw[:, hh, c0:c0 + 128], lhsT=ident[:], rhs=mnegbig[:],
                                                                      start=False, stop=False),
                                       reads=[r_ident, r_mnegbig], writes=[rlw[hh]], signal=False)
                                cz = c0 + 128 if (u["diag"] and c0 < 384) else c0
                                op("pe", lambda hh=hh, cz=cz: nc.tensor.matmul(lw[:, hh, cz:512], lhsT=onesneg[:],
                                                                         rhs=lacc[:, hh, cz:512], start=False, stop=True),
                                   reads=[r_onesneg, r_lacc], writes=[rlw[hh]], signal=(hh == 1))
                            if not u["last"]:
                                op("dve", lambda: nc.vector.tensor_tensor(out=lacc[:, :, c0:512], in0=lacc[:, :, c0:512],
                                                                          in1=SP[:, :, c0:512], op=ALU.add),
                                   reads=[r_lacc, r_SPK[j]], writes=[r_lacc], cost=0.7)

                        def stD(i, j):
                            u = units[i]
                            c0 = u["c0"]
                            W = W2[j % 2]
                            op("act", lambda: nc.scalar.activation(out=W[:, :, c0:512], in_=PS[j % 2][:, :, c0:512],
                                                                   func=AF.Exp),
                               reads=rPS[j % 2], writes=[r_W[j % 2]], cost=1.05)

                        def stE(i, j):
                            u = units[i]
                            c0 = u["c0"]
                            W = W2[j % 2]
                            kb = u["kb"]
                            if u["first"]:
                                op("pe", lambda: nc.tensor.matmul(accb[:, :], lhsT=zeros[:, 0:128], rhs=zeros[:, :],
                                                                  start=True, stop=False),
                                   reads=[r_zeros], writes=[racc], signal=False)
                            for hh in range(2):
                                R = slice(hh * 64, hh * 64 + 64)
                                hcol = slice((2 * p + hh) * 64, (2 * p + hh + 1) * 64)
                                op("pe", lambda hh=hh, R=R, hcol=hcol: nc.tensor.matmul(accb[R, c0:512], lhsT=vtok[:, kb, hcol],
                                                                  rhs=W[:, hh, c0:512], start=False, stop=u["last"]),
                                   reads=[r_vtok, r_W[j % 2]], writes=[racc], signal=(hh == 1))
                            if u["last"]:
                                qg = u["qg"]
                                op("dve", lambda: nc.vector.tensor_copy(out=osbT[:, p, qg * 512:(qg + 1) * 512],
                                                                        in_=accb[:, :]),
                                   reads=[racc], writes=[r_osbT])

                        for u0 in range(0, nU, KB):
                            kk = min(KB, nU - u0)
                            sch.begin_region()
                            for j in range(kk + 1):
                                if j < kk:
                                    qk_mm(units[u0 + j], PS[j % 2], rPS[j % 2], True)
                                if j >= 1:
                                    stB(u0 + j - 1, j - 1)
                                tick()
                            sch.end_region()
                            sch.begin_region()
                            for j in range(kk + 1):
                                if j < kk:
                                    stC(u0 + j, j)
                                if j >= 1:
                                    stD(u0 + j - 1, j - 1)
                                    stE(u0 + j - 1, j - 1)
                                tick()
                            sch.end_region()
                        while nxt is not None:
                            tick()
                    prefetch_gla_w0()
                    sch.barrier()
                with ExitStack() as ph:
                    wg = sbt(ph, "wg", [128, KC, 1040], BF16); r_wg = Res(); r_wgv = Res(); r_wgr = Res()
                    qTg = sbt(ph, "qTg", [128, 2, S], BF16); r_qTg = Res()
                    kTg = sbt(ph, "kTg", [128, 2, S], BF16); r_kTg = Res()
                    ga_hi = sbt(ph, "ga_hi", [16, S], BF16); ga_lo = sbt(ph, "ga_lo", [16, S], BF16); r_gaT = Res()
                    sp_hi, r_sp_hi = None, None
                    def dbl(name, shape, dt):
                        return [sbt(ph, name + str(i), shape, dt) for i in range(2)], [Res(name + str(i)) for i in range(2)]
                    E1, r_E1 = dbl("E1", [128, 256], F32)
                    SPa, r_SPa = dbl("SPa", [128, 256], F32)
                    SPh, r_SPh = dbl("SPh", [128, 256], BF16)
                    SPl, r_SPl = dbl("SPl", [128, 256], BF16)
                    Dend, r_Dend = dbl("Dend", [128, 256], F32)
                    kend, r_kend = dbl("kend", [128, 256], BF16)
                    Eq, r_Eq = dbl("Eq", [128, 2, 128], F32)
                    Ek, r_Ek = dbl("Ek", [128, 2, 128], F32)
                    qdec, r_qdec = dbl("qdec", [128, 2, 128], BF16)
                    kinv, r_kinv = dbl("kinv", [128, 2, 128], BF16)
                    vbf, r_vbf = dbl("vbf", [128, 512], BF16)
                    er, r_er = dbl("er", [128, 512], F32)
                    gr, r_gr = dbl("gr", [128, 512], F32)
                    attm, _ = dbl("attm", [128, 4, 128], BF16)
                    r_attm = [[Res() for _ in range(4)] for _ in range(2)]
                    og, r_og = dbl("og", [128, 512], BF16)
                    gst, r_gst = dbl("gst", [128, 12], F32)
                    S32 = sbt(ph, "S32", [128, 2, 128], F32); r_S32 = Res()
                    Sbf = sbt(ph, "Sbf", [128, 2, 128], BF16); r_Sbf = Res()
                    sch.dma("pool", wg[:, :, 1024:1040], w_in_v[:, :, OFF_G + 1536:OFF_G + 1552], writes=[r_wg])
                    sch.dma("pool", wg[:, :, 0:512], w_in_v[:, :, OFF_G + 512:OFF_G + 1024], writes=[r_wgv])
                    sch.dma("pool", wg[:, :, 512:1024], w_in_v[:, :, OFF_G + 1024:OFF_G + 1536], writes=[r_wgr])
                    for g in range(NG):
                        tc_ = slice(g * 512, (g + 1) * 512)
                        for c2 in range(2):
                            bk, rb = banks5[(2 * c2) % 7]
                            proj_fm(bk, rb, wgA, r_wgA, slice(c2 * 128, (c2 + 1) * 128), xnT, r_xnT, KC, tc_)
                            op("act", lambda bk=bk, c2=c2: nc.scalar.mul(out=qTg[:, c2, tc_], in_=bk, mul=0.125),
                               reads=[rb], writes=[r_qTg])
                            bk, rb = banks5[(2 * c2 + 1) % 7]
                            proj_fm(bk, rb, wgA, r_wgA, slice(256 + c2 * 128, 256 + (c2 + 1) * 128), xnT, r_xnT, KC, tc_)
                            op("dve", lambda bk=bk, c2=c2: nc.vector.tensor_copy(out=kTg[:, c2, tc_], in_=bk),
                               reads=[rb], writes=[r_kTg])
                        bk, rb = banks5[4]
                        proj_fm(bk, rb, wg, r_wg, slice(1024, 1040), xnT, r_xnT, KC, tc_, mrows=16)
                        op("act", lambda bk=bk: nc.scalar.copy(out=ga_hi[0:16, tc_], in_=bk[0:16, :]),
                           reads=[rb], writes=[r_gaT])
                        op("dve", lambda bk=bk: nc.vector.tensor_tensor(out=ga_lo[0:16, tc_], in0=bk[0:16, :],
                                                                        in1=ga_hi[0:16, tc_], op=ALU.subtract),
                           reads=[rb, r_gaT], writes=[r_gaT])
                    op("pool", lambda: nc.gpsimd.memset(S32[:], 0.0), writes=[r_S32])
                    op("pool", lambda: nc.gpsimd.memset(Sbf[:], 0.0), writes=[r_Sbf])
                    Pk, rPk = PA[:, 0:256], rPA0
                    Pa, rPa = PA[:, 256:512], rPA0
                    Pv, rPv = PA[:, 512:1024], rPA1
                    Pr, rPr = PB[:, 0:512], rPB0
                    Po, rPo = PB[:, 512:1024], rPB1
                    PR, rPR = PC[:, 0:256], rPC
                    Pcum, rPcum = PC[:, 256:512], rPC
                    PDS, rPDS = PD[:, 0:256], rPD
                    Patt = [PE_[:, 0:128], PD[:, 256:384], PE_[:, 128:256], PD[:, 384:512]]
                    rPatt = [rPE, rPD, rPE, rPD]

                    def gla_front(tt):
                        z = tt % 2
                        tc_ = slice(tt * 128, (tt + 1) * 128)
                        proj_tm(Pk, rPk, xnT, r_xnT, tc_, wgA, r_wgA, slice(256, 512), KC)
                        yield
                        for mi, (lh, rh) in enumerate(((ga_hi, wa_hi), (ga_lo, wa_hi), (ga_hi, wa_lo))):
                            op("pe", lambda lh=lh, rh=rh, mi=mi: nc.tensor.matmul(Pa, lhsT=lh[0:16, tc_], rhs=rh[0:16, :],
                                                                              start=(mi == 0), stop=False),
                               reads=[r_gaT, r_wahl], writes=[rPa], signal=False, cost=0.12)
                        for mi, bh in enumerate((ba_hi, ba_lo)):
                            op("pe", lambda bh=bh, mi=mi: nc.tensor.matmul(Pa, lhsT=ones_bf[0:1, :], rhs=bh[0:1, :],
                                                                       start=False, stop=(mi == 1)),
                               reads=[r_ones_bf, r_bahl], writes=[rPa], signal=(mi == 1), cost=0.12)
                        yield
                        op("act", lambda: nc.scalar.activation(out=E1[z][:], in_=Pa, func=AF.Exp, scale=-1.0),
                           reads=[rPa], writes=[r_E1[z]])
                        yield
                        op("act", lambda: nc.scalar.activation(out=SPa[z][:], in_=E1[z][:], func=AF.Ln, bias=1.0),
                           reads=[r_E1[z]], writes=[r_SPa[z]])
                        yield
                        proj_tm(Pv, rPv, xnT, r_xnT, tc_, wg, r_wgv, slice(0, 512), KC)
                        yield
                        proj_tm(Pr, rPr, xnT, r_xnT, tc_, wg, r_wgr, slice(512, 1024), KC)
                        yield
                        op("dve", lambda: nc.vector.tensor_copy(out=SPh[z][:], in_=SPa[z][:]),
                           reads=[r_SPa[z]], writes=[r_SPh[z]], cost=0.3)
                        op("dve", lambda: nc.vector.tensor_tensor(out=SPl[z][:], in0=SPa[z][:], in1=SPh[z][:],
                                                                  op=ALU.subtract),
                           reads=[r_SPa[z], r_SPh[z]], writes=[r_SPl[z]], cost=0.4)
                        for mi, (sp_, rsp_) in enumerate(((SPh, r_SPh), (SPl, r_SPl))):
                            op("pe", lambda sp_=sp_, mi=mi: nc.tensor.matmul(PR, lhsT=ust_bf[:], rhs=sp_[z][:],
                                                                          start=(mi == 0), stop=(mi == 1)),
                               reads=[r_ust_bf, rsp_[z]], writes=[rPR], signal=(mi == 1), cost=0.12)
                        yield
                        for c2 in range(2):
                            for mi, (sp_, rsp_) in enumerate(((SPh, r_SPh), (SPl, r_SPl))):
                                op("pe", lambda c2=c2, sp_=sp_, mi=mi: nc.tensor.matmul(
                                    Pcum[:, c2 * 128:(c2 + 1) * 128], lhsT=sp_[z][:, c2 * 128:(c2 + 1) * 128],
                                    rhs=tincl_bf[:], start=(mi == 0), stop=(mi == 1)),
                                   reads=[r_tincl_bf, rsp_[z]], writes=[rPcum], signal=(c2 == 1 and mi == 1), cost=0.08)
                            yield
                        op("act", lambda: nc.scalar.copy(out=vbf[z][:], in_=Pv), reads=[rPv], writes=[r_vbf[z]])
                        yield
                        op("act", lambda: nc.scalar.activation(out=er[z][:], in_=Pr, func=AF.Exp, scale=-1.0),
                           reads=[rPr], writes=[r_er[z]])
                        yield
                        op("act", lambda: nc.scalar.activation(out=Dend[z][:], in_=PR, func=AF.Exp, scale=-1.0 / 16),
                           reads=[rPR], writes=[r_Dend[z]])
                        yield
                        op("dve", lambda: nc.vector.tensor_tensor(out=kend[z][:], in0=Pk, in1=Dend[z][:], op=ALU.mult),
                           reads=[rPk, r_Dend[z]], writes=[r_kend[z]])
                        yield
                        op("act", lambda: nc.scalar.activation(out=Eq[z][:].rearrange("p a b -> p (a b)"), in_=Pcum,
                                                               func=AF.Exp, scale=-1.0 / 16),
                           reads=[rPcum], writes=[r_Eq[z]])
                        yield
                        op("act", lambda: nc.scalar.activation(out=Ek[z][:].rearrange("p a b -> p (a b)"), in_=Pcum,
                                                               func=AF.Exp, scale=1.0 / 16),
                           reads=[rPcum], writes=[r_Ek[z]])
                        yield
                        op("dve", lambda: nc.vector.tensor_tensor(out=qdec[z][:], in0=qTg[:, :, tc_], in1=Eq[z][:],
                                                                  op=ALU.mult),
                           reads=[r_qTg, r_Eq[z]], writes=[r_qdec[z]])
                        yield
                        op("dve", lambda: nc.vector.tensor_tensor(out=kinv[z][:], in0=kTg[:, :, tc_], in1=Ek[z][:],
                                                                  op=ALU.mult),
                           reads=[r_kTg, r_Ek[z]], writes=[r_kinv[z]])
                        yield
                        op("act", lambda: nc.scalar.activation(out=er[z][:], in_=er[z][:], func=AF.Ln, bias=1.0),
                           reads=[r_er[z]], writes=[r_er[z]])
                        yield
                        op("act", lambda: nc.scalar.activation(out=er[z][:], in_=er[z][:], func=AF.Exp, scale=-1.0),
                           reads=[r_er[z]], writes=[r_er[z]])
                        yield
                        op("dve", lambda: nc.vector.tensor_tensor(out=gr[z][:], in0=Pr, in1=er[z][:], op=ALU.mult),
                           reads=[rPr, r_er[z]], writes=[r_gr[z]])
                        yield
                        op("pool", lambda: nc.gpsimd.tensor_tensor(out=gr[z][:], in0=gr[z][:], in1=ggla_bc[:],
                                                                   op=ALU.mult),
                           reads=[r_gr[z], r_ggla], writes=[r_gr[z]])
                        yield

                    def gla_back(tt):
                        z = tt % 2
                        tc_ = slice(tt * 128, (tt + 1) * 128)
                        for h in range(4):
                            c2, hh = divmod(h, 2)
                            R = slice(hh * 64, hh * 64 + 64)
                            hc = slice(h * 128, (h + 1) * 128)
                            op("pe", lambda h=h, c2=c2, R=R, hc=hc: nc.tensor.matmul(Patt[h], lhsT=kinv[z][R, c2, :], rhs=qdec[z][R, c2, :],
                                                              start=True, stop=True),
                               reads=[r_kinv[z], r_qdec[z]], writes=[rPatt[h]])
                            yield
                        for h in range(4):
                            c2, hh = divmod(h, 2)
                            R = slice(hh * 64, hh * 64 + 64)
                            hc = slice(h * 128, (h + 1) * 128)
                            op("dve", lambda h=h, c2=c2, R=R, hc=hc: nc.vector.tensor_tensor(out=attm[z][:, h, :], in0=Patt[h], in1=tincl[:],
                                                                      op=ALU.mult),
                               reads=[rPatt[h], r_tincl], writes=[r_attm[z][h]])
                            yield
                            op("pe", lambda h=h, c2=c2, R=R, hc=hc: nc.tensor.matmul(Po[:, hc], lhsT=attm[z][:, h, :], rhs=vbf[z][:, hc],
                                                              start=True, stop=False),
                               reads=[r_attm[z][h], r_vbf[z]], writes=[rPo], signal=False)
                            yield
                            op("pe", lambda h=h, c2=c2, R=R, hc=hc: nc.tensor.matmul(Po[:, hc], lhsT=qdec[z][R, c2, :], rhs=Sbf[R, c2, :],
                                                              start=False, stop=True),
                               reads=[r_qdec[z], r_Sbf], writes=[rPo], signal=False)
                            yield
                            op("pe", lambda h=h, c2=c2, R=R, hc=hc: nc.tensor.matmul(PDS[R, c2 * 128:(c2 + 1) * 128],
                                                              lhsT=kend[z][:, h * 64:(h + 1) * 64], rhs=vbf[z][:, hc],
                                                              start=True, stop=True),
                               reads=[r_kend[z], r_vbf[z]], writes=[rPDS])
                            yield
                        for c2 in range(2):
                            op("dve", lambda c2=c2: nc.vector.scalar_tensor_tensor(
                                out=S32[:, c2, :], in0=S32[:, c2, :], scalar=Eq[z][:, c2, 127:128],
                                in1=PDS[:, c2 * 128:(c2 + 1) * 128], op0=ALU.mult, op1=ALU.add),
                               reads=[r_S32, r_Eq[z], rPDS], writes=[r_S32])
                            yield
                        op("dve", lambda: nc.vector.tensor_copy(out=Sbf[:], in_=S32[:]), reads=[r_S32], writes=[r_Sbf])
                        yield
                        for h in range(4):
                            hc = slice(h * 128, (h + 1) * 128)
                            op("act", lambda h=h, hc=hc: nc.scalar.activation(out=junk[:, hc], in_=Po[:, hc],
                                                                              func=AF.Square,
                                                                              accum_out=gst[z][:, h:h + 1]),
                               reads=[rPo], writes=[r_junk, r_gst[z]])
                            yield
                        op("act", lambda: nc.scalar.activation(out=gst[z][:, 4:8], in_=gst[z][:, 0:4], func=AF.Ln,
                                                               scale=1.0 / 128, bias=EPS),
                           reads=[r_gst[z]], writes=[r_gst[z]])
                        yield
                        op("act", lambda: nc.scalar.activation(out=gst[z][:, 8:12], in_=gst[z][:, 4:8], func=AF.Exp,
                                                               scale=-0.5), reads=[r_gst[z]], writes=[r_gst[z]])
                        yield
                        for h in range(4):
                            hc = slice(h * 128, (h + 1) * 128)
                            op("dve", lambda h=h, hc=hc: nc.vector.scalar_tensor_tensor(
                                out=og[z][:, hc], in0=Po[:, hc], scalar=gst[z][:, 8 + h:9 + h], in1=gr[z][:, hc],
                                op0=ALU.mult, op1=ALU.mult),
                               reads=[rPo, r_gst[z], r_gr[z]], writes=[r_og[z]])
                            yield
                        for h in range(4):
                            op("pe", lambda h=h: nc.tensor.transpose(PT[:, h, :], og[z][:, h * 128:(h + 1) * 128],
                                                                     ident[:]),
                               reads=[r_og[z], r_ident], writes=[rPT], signal=(h == 3))
                            yield
                        op("dve", lambda: nc.vector.tensor_copy(out=oglT[:, :, tc_], in_=PT[:, 0:4, :]),
                           reads=[rPT], writes=[r_oglT])
                        yield

                    def drive(*gens):
                        gens = [g for g in gens if g is not None]
                        while gens:
                            for g in list(gens):
                                try:
                                    next(g)
                                except StopIteration:
                                    gens.remove(g)

                    sch.begin_region()
                    for tt in range(NT):
                        for _ in gla_front(tt):
                            pass
                        for _ in gla_back(tt):
                            pass
                    sch.end_region()
                    prefetch_merge_w0()
                    sch.barrier()
                with ExitStack() as ph:
                    yT = sbt(ph, "yT", [128, KC, S], BF16); r_yT = Res()
                    wgs = [wgs0, sbt(ph, "wgs1", [128, KC, 128], BF16)]
                    wgg = [wgg0, sbt(ph, "wgg1", [128, KC, 128], BF16)]
                    wsb = [wsb0, sbt(ph, "wsb1", [128, 4, 128], BF16)]
                    wgl = [wgl0, sbt(ph, "wgl1", [128, 4, 128], BF16)]
                    r_wm = [r_wm0, Res()]
                    wo = sbt(ph, "wo", [128, KC, D], BF16); r_wo = Res()
                    sg = [sbt(ph, "sg%d" % i, [128, 512], F32) for i in range(4)]
                    r_sg = [Res() for _ in range(4)]
                    t1 = [sbt(ph, "t1%d" % i, [128, 512], F32) for i in range(2)]
                    t2 = [sbt(ph, "t2%d" % i, [128, 512], F32) for i in range(2)]
                    r_t1 = [Res() for _ in range(2)]; r_t2 = [Res() for _ in range(2)]
                    ht = [sbt(ph, "ht%d" % i, [128, D], F32) for i in range(2)]
                    r_ht = [Res() for _ in range(2)]

                    def load_merge_w(n, sl):
                        r = r_wm[sl]
                        sch.dma("pool", wgs[sl][:], w_in_v[:, :, OFF_GSB + n * 128:OFF_GSB + (n + 1) * 128], writes=[r])
                        sch.dma("pool", wgg[sl][:], w_in_v[:, :, OFF_GGLA + n * 128:OFF_GGLA + (n + 1) * 128], writes=[r])
                        sch.dma("pool", wsb[sl][:], wbsb_v[:, :, n * 128:(n + 1) * 128], writes=[r])
                        sch.dma("pool", wgl[sl][:], wbgl_v[:, :, n * 128:(n + 1) * 128], writes=[r])
                    sch.dma("pool", wo[:], w_out_v, writes=[r_wo])
                    sch.begin_region()
                    it = 0
                    for n in range(KC):
                        sl = n % 2
                        if n + 1 < KC:
                            load_merge_w(n + 1, 1 - sl)
                        for g in range(NG):
                            tc_ = slice(g * 512, (g + 1) * 512)
                            j = it % 2
                            it += 1
                            (b0, rb0), (b1, rb1), (b2, rb2), (b3, rb3) = [banks5[(4 * (it - 1) + q_) % 7] for q_ in range(4)]
                            proj_fm(b0, rb0, wgs[sl], r_wm[sl], slice(0, 128), xnT, r_xnT, KC, tc_)
                            op("act", lambda b0=b0, j=j, n=n: nc.scalar.activation(out=sg[2 * j][:], in_=b0, func=AF.Sigmoid,
                                                                                   bias=bgate(0, n)),
                               reads=[rb0, r_vecs], writes=[r_sg[2 * j]])
                            proj_fm(b1, rb1, wgg[sl], r_wm[sl], slice(0, 128), xnT, r_xnT, KC, tc_)
                            op("act", lambda b1=b1, j=j, n=n: nc.scalar.activation(out=sg[2 * j + 1][:], in_=b1,
                                                                                   func=AF.Sigmoid, bias=bgate(1, n)),
                               reads=[rb1, r_vecs], writes=[r_sg[2 * j + 1]])
                            proj_fm(b2, rb2, wsb[sl], r_wm[sl], slice(0, 128), osbT, r_osbT, 4, tc_)
                            op("dve", lambda b2=b2, j=j: nc.vector.tensor_tensor(out=t1[j][:], in0=b2, in1=sg[2 * j][:],
                                                                                 op=ALU.mult),
                               reads=[rb2, r_sg[2 * j]], writes=[r_t1[j]])
                            proj_fm(b3, rb3, wgl[sl], r_wm[sl], slice(0, 128), oglT, r_oglT, 4, tc_)
                            op("dve", lambda b3=b3, j=j: nc.vector.tensor_tensor(out=t2[j][:], in0=b3, in1=sg[2 * j + 1][:],
                                                                                 op=ALU.mult),
                               reads=[rb3, r_sg[2 * j + 1]], writes=[r_t2[j]])
                            op("pool", lambda j=j, n=n, tc_=tc_: nc.gpsimd.tensor_tensor(out=yT[:, n, tc_], in0=t1[j][:],
                                                                                         in1=t2[j][:], op=ALU.add),
                               reads=[r_t1[j], r_t2[j]], writes=[r_yT])
                    sch.end_region()
                    sch.begin_region()
                    for tt in range(NT):
                        sl = tt % 2
                        tc_ = slice(tt * 128, (tt + 1) * 128)
                        PW, rW0, rW1 = (PA, rPA0, rPA1) if sl == 0 else (PB, rPB0, rPB1)
                        sch.dma("sp", xin[sl][:], x[b, tc_, :], writes=[r_xin[sl]])
                        for half in range(2):
                            proj_tm(PW[:, half * 512:(half + 1) * 512], (rW0, rW1)[half], yT, r_yT, tc_, wo, r_wo,
                                    slice(half * 512, (half + 1) * 512), KC)
                        op("dve", lambda PW=PW, sl=sl: nc.vector.tensor_tensor(out=ht[sl][:], in0=PW[:, :], in1=xin[sl][:],
                                                                               op=ALU.add),
                           reads=[rW0, rW1, r_xin[sl]], writes=[r_ht[sl]])
                        sch.dma("sp", hscr[tc_, :], ht[sl][:], reads=[r_ht[sl]], writes=[r_hscr[tt]])
                        norm_pre(ht[sl], r_ht[sl], sl)
                        if tt > 0:
                            norm_post(1 - sl, gffn, xnT, r_xnT, tt - 1)
                    norm_post((NT - 1) % 2, gffn, xnT, r_xnT, NT - 1)
                    sch.end_region()
                    prefetch_ffn_w0()
                    sch.barrier()
            hnT, r_hnT = xnT, r_xnT
            HS = min(S, 1024)
            NGH = HS // 512
            NTH = HS // 128
            with ExitStack() as ph:
                wfo = sbt(ph, "wfo", [128, NF, D], BF16); r_wfo = Res()
                actT = sbt(ph, "actT", [128, NF, HS], BF16); r_actT = Res()
                wa = [wa0, sbt(ph, "wa1", [128, KC, 128], BF16)]
                wgt = [wgt0, sbt(ph, "wgt1", [128, KC, 128], BF16)]
                r_wf = [r_wf0, Res()]
                abuf = [sbt(ph, "abuf%d" % i, [128, 514], F32) for i in range(2)]
                r_abuf = [Res() for _ in range(2)]
                tcv = [sbt(ph, "tcv%d" % i, [128, 512], F32) for i in range(2)]
                r_tcv = [Res() for _ in range(2)]
                gel = [sbt(ph, "gel%d" % i, [128, 512], F32) for i in range(2)]
                r_gel = [Res() for _ in range(2)]
                halo = sbt(ph, "halo", [128, NF, 2], F32); r_halo = Res()
                hin = [sbt(ph, "hin%d" % i, [128, D], F32) for i in range(2)]
                r_hin = [Res() for _ in range(2)]
                h2 = [sbt(ph, "h2%d" % i, [128, D], F32) for i in range(2)]
                r_h2 = [Res() for _ in range(2)]
                op("pool", lambda: nc.gpsimd.memset(halo[:], 0.0), writes=[r_halo])

                def load_ffn_w(f, sl):
                    sch.dma("pool", wa[sl][:], wfi_v[:, :, f * 128:(f + 1) * 128], writes=[r_wf[sl]])
                    sch.dma("pool", wgt[sl][:], wfi_v[:, :, DFF + f * 128:DFF + (f + 1) * 128], writes=[r_wf[sl]])
                it = 0
                for hs in range(S // HS):
                    if hs == 0:
                        load_ffn_w(1, 1)
                    else:
                        load_ffn_w(0, 0)
                    for f in range(NF):
                        sl = f % 2
                        if f + 1 < NF and not (hs == 0 and f == 0):
                            load_ffn_w(f + 1, 1 - sl)
                        if hs == 0 and f < 11:
                            sch.dma("pool", wfo[:, 2 * f:2 * f + 2, :], wfo_v[:, 2 * f:2 * f + 2, :], writes=[r_wfo])
                        for g in range(NGH):
                            tc_ = slice(hs * HS + g * 512, hs * HS + (g + 1) * 512)
                            lc_ = slice(g * 512, (g + 1) * 512)
                            j = it % 2
                            it += 1
                            (bA, rbA), (bG, rbG) = (banks5[4], banks5[5]) if j == 0 else (banks5[6], banks5[3])
                            ab, rab = abuf[j], r_abuf[j]
                            proj_fm(bA, rbA, wa[sl], r_wf[sl], slice(0, 128), hnT, r_hnT, KC, tc_)
                            proj_fm(bG, rbG, wgt[sl], r_wf[sl], slice(0, 128), hnT, r_hnT, KC, tc_)
                            op("pool", lambda ab=ab, f=f: nc.gpsimd.tensor_copy(out=ab[:, 0:2], in_=halo[:, f, :]),
                               reads=[r_halo], writes=[rab])
                            op("act", lambda ab=ab, bA=bA: nc.scalar.copy(out=ab[:, 2:514], in_=bA),
                               reads=[rbA], writes=[rab])
                            op("pool", lambda ab=ab, f=f: nc.gpsimd.tensor_copy(out=halo[:, f, :], in_=ab[:, 512:514]),
                               reads=[rab], writes=[r_halo])
                            tv, rtv = tcv[j], r_tcv[j]
                            op("dve", lambda ab=ab, tv=tv, f=f: nc.vector.tensor_scalar(
                                out=tv[:], in0=ab[:, 2:514], scalar1=convw(2, f), scalar2=convb(f), op0=ALU.mult,
                                op1=ALU.add), reads=[rab, r_vecs], writes=[rtv])
                            op("dve", lambda ab=ab, tv=tv, f=f: nc.vector.scalar_tensor_tensor(
                                out=tv[:], in0=ab[:, 1:513], scalar=convw(1, f), in1=tv[:], op0=ALU.mult, op1=ALU.add),
                               reads=[rab, r_vecs, rtv], writes=[rtv])
                            op("dve", lambda ab=ab, tv=tv, f=f: nc.vector.scalar_tensor_tensor(
                                out=tv[:], in0=ab[:, 0:512], scalar=convw(0, f), in1=tv[:], op0=ALU.mult, op1=ALU.add),
                               reads=[rab, r_vecs, rtv], writes=[rtv])
                            ge, rge = gel[j], r_gel[j]
                            op("act", lambda tv=tv, ge=ge: nc.scalar.activation(out=ge[:], in_=tv[:],
                                                                                func=AF.Gelu_apprx_tanh),
                               reads=[rtv], writes=[rge])
                            op("dve", lambda ge=ge, bG=bG, f=f, lc_=lc_: nc.vector.tensor_tensor(
                                out=actT[:, f, lc_], in0=bG, in1=ge[:], op=ALU.mult),
                               reads=[rbG, rge], writes=[r_actT])
                    for t in range(NTH):
                        tt = hs * NTH + t
                        sl = tt % 2
                        lt_ = slice(t * 128, (t + 1) * 128)
                        tc_ = slice(tt * 128, (tt + 1) * 128)
                        PW, rW0, rW1 = (PA, rPA0, rPA1) if sl == 0 else (PB, rPB0, rPB1)
                        sch.dma("sp", hin[sl][:], hscr[tc_, :], reads=[r_hscr[tt]], writes=[r_hin[sl]])
                        for half in range(2):
                            proj_tm(PW[:, half * 512:(half + 1) * 512], (rW0, rW1)[half], actT, r_actT, lt_, wfo, r_wfo,
                                    slice(half * 512, (half + 1) * 512), NF)
                        op("dve", lambda PW=PW, sl=sl: nc.vector.tensor_tensor(out=h2[sl][:], in0=PW[:, :], in1=hin[sl][:],
                                                                               op=ALU.add),
                           reads=[rW0, rW1, r_hin[sl]], writes=[r_h2[sl]])
                        st, r_st = stt[sl], r_stt[sl]
                        rms_stats(h2[sl][:], r_h2[sl], st, r_st, D, D)
                        op("dve", lambda sl=sl, st=st: nc.vector.scalar_tensor_tensor(
                            out=h2[sl][:], in0=h2[sl][:], scalar=st[:, 2:3], in1=gfin_bc[:], op0=ALU.mult, op1=ALU.mult),
                           reads=[r_h2[sl], r_st, r_gfin], writes=[r_h2[sl]])
                        sch.dma("sp", out[b, tc_, :], h2[sl][:], reads=[r_h2[sl]])
                if b + 1 < NB:
                    for tt in range(2):
                        sch.dma("sp", xin[tt][:], x[b + 1, tt * 128:(tt + 1) * 128, :], writes=[r_xin[tt]])
                sch.barrier()
    return nc


_NC_CACHE = {}


def _get_nc(S, NB):
    key = (S, NB)
    if key not in _NC_CACHE:
        _NC_CACHE[key] = build_nc(S, NB)
    return _NC_CACHE[key]


def kernel(x, norm_mix_g, w_in, b_gate, w_alpha_up, b_alpha, gla_norm_g, w_branch_sb, w_branch_gla, w_out,
           norm_ffn_g, w_ffn_in, conv_w, conv_b, w_ffn_out, norm_final_g, n_cores=N_CORES):
    f = lambda a: np.ascontiguousarray(np.asarray(a, dtype=np.float32))
    x = f(x)
    B, S, _ = x.shape
    NB = B // n_cores
    shared = {
        "norm_mix_g": f(norm_mix_g)[0], "w_in": f(w_in)[0], "b_gate": f(b_gate)[0],
        "w_alpha_up": f(w_alpha_up)[0], "b_alpha": f(b_alpha)[0], "gla_norm_g": f(gla_norm_g)[0],
        "w_branch_sb": f(w_branch_sb)[0], "w_branch_gla": f(w_branch_gla)[0], "w_out": f(w_out)[0],
        "norm_ffn_g": f(norm_ffn_g)[0], "w_ffn_in": f(w_ffn_in)[0], "conv_w": f(conv_w)[0],
        "conv_b": f(conv_b)[0], "w_ffn_out": f(w_ffn_out)[0], "norm_final_g": f(norm_final_g),
    }
    nc = _get_nc(S, NB)
    in_maps = []
    for c in range(n_cores):
        m = dict(shared)
        m["x"] = np.ascontiguousarray(x[c * NB:(c + 1) * NB])
        in_maps.append(m)
    res = run_bass_kernel_spmd(nc, in_maps, core_ids=list(range(n_cores)))
    return np.concatenate([np.asarray(r["out"]) for r in res.results], axis=0).astype(np.float32)
```
